# Optimizing a Trainium2 kernel written in Bass

```python
import math, functools
import jax, jax.numpy as jnp
from jax import lax
import numpy as np

D_MODEL = 1024
BATCH = 8
SEQ = 8192
DEPTH = 2
DEC_BATCH = 32
DEC_SEQ = 16
PAST_LEN = 4096

CHUNK = 64
HEAD_DIM = 64
H_A = 6
H_B = 6
G_C = 4
W_A = H_A * HEAD_DIM
W_B = H_B * HEAD_DIM
W_C = G_C * HEAD_DIM
MIX = W_A + W_B + W_C
Q_BLOCK = 128
GDN_CHUNK = CHUNK
CONV_K = 4
CM_LEN = 128
D_FF = -(-8 * D_MODEL // (3 * 256)) * 256
FORGET_BIAS = 3.0
ATTN_SCALE = HEAD_DIM ** -0.5
SPLIT_SIZES = (W_A, W_A, W_A, H_A, 3 * W_B, H_B, H_B, W_B, W_C, W_C)
SPLIT_IDX = tuple(int(i) for i in np.cumsum(SPLIT_SIZES)[:-1])
D_IN = sum(SPLIT_SIZES)

kernel_name = 'hybrid_stream_fox_gdn_sgu_step'


def rmsnorm(x, g, eps=1e-6):
    xf = x.astype(jnp.float32)
    y = xf * lax.rsqrt(jnp.mean(xf * xf, axis=-1, keepdims=True) + eps)
    return (y * g.astype(jnp.float32)).astype(x.dtype)


def layernorm(x, g, b, eps=1e-5):
    xf = x.astype(jnp.float32)
    mu = jnp.mean(xf, axis=-1, keepdims=True)
    var = jnp.mean(jnp.square(xf - mu), axis=-1, keepdims=True)
    return ((xf - mu) * lax.rsqrt(var + eps) * g.astype(jnp.float32) + b.astype(jnp.float32)).astype(x.dtype)


def l2norm(x, eps=1e-6):
    return x * lax.rsqrt(jnp.sum(x * x, axis=-1, keepdims=True) + eps)


def fox_attend_prompt(q, k, v, logf):
    B, S, H, Dh = q.shape
    nb = S // Q_BLOCK
    cT = jnp.cumsum(logf, axis=1).transpose(0, 2, 1)
    qb = q.reshape(B, nb, Q_BLOCK, H, Dh).swapaxes(0, 1)
    cqb = cT.reshape(B, H, nb, Q_BLOCK).transpose(2, 0, 1, 3)
    kpos = jnp.arange(S)

    def block(args):
        qi, cqi, i = args
        s = jnp.einsum('bqhd,bkhd->bhqk', qi, k, preferred_element_type=jnp.float32) * ATTN_SCALE
        s = s + cqi[..., :, None] - cT[..., None, :]
        qpos = i * Q_BLOCK + jnp.arange(Q_BLOCK)
        s = jnp.where(kpos[None, :] <= qpos[:, None], s, -jnp.inf)
        p = jax.nn.softmax(s, axis=-1)
        return jnp.einsum('bhqk,bkhd->bqhd', p.astype(v.dtype), v)

    o = lax.map(block, (qb, cqb, jnp.arange(nb)))
    return o.swapaxes(0, 1).reshape(B, S, H, Dh)


def fox_attend_sample(q, k, v, logf, ck, cv, clogf):
    B, n, H, Dh = q.shape
    P = ck.shape[1]
    kk = jnp.concatenate([ck.astype(k.dtype), k], axis=1)
    vv = jnp.concatenate([cv.astype(v.dtype), v], axis=1)
    c = jnp.cumsum(jnp.concatenate([clogf.astype(jnp.float32), logf], axis=1), axis=1).transpose(0, 2, 1)
    s = jnp.einsum('bqhd,bkhd->bhqk', q, kk, preferred_element_type=jnp.float32) * ATTN_SCALE
    s = s + c[..., P:, None] - c[..., None, :]
    mask = jnp.arange(P + n)[None, :] <= (P + jnp.arange(n))[:, None]
    p = jax.nn.softmax(jnp.where(mask, s, -jnp.inf), axis=-1)
    return jnp.einsum('bhqk,bkhd->bqhd', p.astype(vv.dtype), vv)


def short_conv(xin, prev, w):
    T = xin.shape[1]
    xp = jnp.concatenate([prev.astype(xin.dtype), xin], axis=1)
    y = xp[:, 0:T] * w[0]
    for i in range(1, CONV_K):
        y = y + xp[:, i:i + T] * w[i]
    return y, xp[:, -(CONV_K - 1):]


def gdn_chunked(q, k, v, beta, gl, S0):
    L = q.shape[2]
    Gh = jnp.cumsum(gl, axis=2).transpose(0, 1, 3, 2)
    incl = jnp.tril(jnp.ones((L, L), dtype=bool))
    strict = jnp.tril(jnp.ones((L, L), dtype=bool), -1)
    diff = Gh[..., :, None] - Gh[..., None, :]
    dec = jnp.exp(jnp.where(incl, diff, -jnp.inf))
    bh = beta.transpose(0, 1, 3, 2)
    kk = jnp.einsum('bnihd,bnjhd->bnhij', k, k)
    A = jnp.where(strict, bh[..., :, None] * kk * dec, 0.0)
    IA = A + jnp.eye(L, dtype=A.dtype)
    kt = k.transpose(0, 1, 3, 2, 4)
    rhs_v = v.transpose(0, 1, 3, 2, 4) * bh[..., None]
    rhs_k = kt * (bh * jnp.exp(Gh))[..., None]
    Uv = lax.linalg.triangular_solve(IA, rhs_v, left_side=True, lower=True, unit_diagonal=True)
    W = lax.linalg.triangular_solve(IA, rhs_k, left_side=True, lower=True, unit_diagonal=True)
    qk = jnp.einsum('bnihd,bnjhd->bnhij', q, k) * dec
    qg = q.transpose(0, 1, 3, 2, 4) * jnp.exp(Gh)[..., None]
    kdec = kt * jnp.exp(Gh[..., -1:] - Gh)[..., None]
    gL = jnp.exp(Gh[..., -1])

    def step(S, inp):
        uv_c, w_c, qk_c, qg_c, kd_c, gl_c = inp
        U = uv_c - jnp.einsum('bhld,bhde->bhle', w_c, S)
        o = jnp.einsum('bhld,bhde->bhle', qg_c, S) + jnp.einsum('bhij,bhje->bhie', qk_c, U)
        S = S * gl_c[..., None, None] + jnp.einsum('bhld,bhle->bhde', kd_c, U)
        return S, o

    xs = tuple(a.swapaxes(0, 1) for a in (Uv, W, qk, qg, kdec, gL))
    S, o = lax.scan(step, S0, xs)
    return o.transpose(1, 0, 3, 2, 4), S


def spatial_gate(u, v, w_s, b_s):
    L = u.shape[2]
    pos = jnp.arange(L)
    mask = (pos[None, :] // CHUNK) <= (pos[:, None] // CHUNK)
    w = jnp.where(mask, w_s[:, :L, :L], 0.0)
    s = jnp.einsum('gij,bnjgc->bnigc', w, v) + b_s[:, :L].T[None, None, :, :, None]
    return u * s


def layer(x, p, conv_prev, S0, attend, gdn_len, cm_len):
    B, T, _ = x.shape
    f32 = jnp.float32
    h = rmsnorm(x, p['g_pre_mix'])
    z = h @ p['w_in']
    aq, ak, av, af, bqkv, ba, bb, bz, cu, cv = jnp.split(z, SPLIT_IDX, axis=-1)
    qa = aq.reshape(B, T, H_A, HEAD_DIM)
    ka = ak.reshape(B, T, H_A, HEAD_DIM)
    va = av.reshape(B, T, H_A, HEAD_DIM)
    logf = jax.nn.log_sigmoid(af.astype(f32) + p['b_f'].astype(f32))
    oa = rmsnorm(attend(qa, ka, va, logf).reshape(B, T, W_A), p['g_a_out']).astype(x.dtype)
    yc, conv_new = short_conv(bqkv, conv_prev, p['conv_w'])
    yc = jax.nn.silu(yc.astype(f32))
    qb, kb, vb = jnp.split(yc, 3, axis=-1)
    qb = l2norm(qb.reshape(B, T, H_B, HEAD_DIM)) * ATTN_SCALE
    kb = l2norm(kb.reshape(B, T, H_B, HEAD_DIM))
    vb = vb.reshape(B, T, H_B, HEAD_DIM)
    beta = jax.nn.sigmoid(bb.astype(f32))
    gl = -jnp.exp(p['a_log'].astype(f32)) * jax.nn.softplus(ba.astype(f32) + p['dt_bias'].astype(f32))
    nc = T // gdn_len
    rs = lambda t: t.reshape((B, nc, gdn_len) + t.shape[2:])
    ob, S_new = gdn_chunked(rs(qb), rs(kb), rs(vb), rs(beta), rs(gl), S0.astype(f32))
    ob = rmsnorm(ob.reshape(B, T, H_B, HEAD_DIM), p['g_b_out']) * jax.nn.silu(bz.astype(f32).reshape(B, T, H_B, HEAD_DIM))
    ob = ob.reshape(B, T, W_B).astype(x.dtype)
    u = jax.nn.gelu(cu)
    vn = layernorm(jax.nn.gelu(cv), p['g_cv'], p['b_cv'])
    nm = T // cm_len
    rc = lambda t: t.reshape(B, nm, cm_len, G_C, W_C // G_C)
    oc = spatial_gate(rc(u), rc(vn), p['w_s'], p['b_s']).reshape(B, T, W_C)
    oc = rmsnorm(oc, p['g_c_out']).astype(x.dtype)
    m = jnp.concatenate([oa, ob, oc], axis=-1) @ p['w_out']
    x = x + rmsnorm(m, p['g_post_mix'])
    gate, up = jnp.split(rmsnorm(x, p['g_pre_ffn']) @ p['w_ffn_in'], 2, axis=-1)
    x = x + rmsnorm((jax.nn.silu(gate) * up) @ p['w_ffn_out'], p['g_post_ffn'])
    return x, (ka, va, logf, conv_new, S_new, vn)


def setup_inputs(seed: int = 0) -> dict:
    key = jax.random.key(seed)
    ks = jax.random.split(key, 26)
    f32 = jnp.float32
    nrm = lambda k, shape, s: jax.random.normal(k, shape, f32) * s
    gain = lambda k, shape: 1.0 + 0.05 * jax.random.normal(k, shape, f32)
    x_prompt = nrm(ks[0], (BATCH, SEQ, D_MODEL), 1.0)
    x_sample = nrm(ks[1], (DEC_BATCH, DEC_SEQ, D_MODEL), 1.0)
    cache_a_k = nrm(ks[2], (DEPTH, DEC_BATCH, PAST_LEN, H_A, HEAD_DIM), 1.0)
    cache_a_v = nrm(ks[3], (DEPTH, DEC_BATCH, PAST_LEN, H_A, HEAD_DIM), 1.0)
    cache_a_logf = jax.nn.log_sigmoid(FORGET_BIAS + nrm(ks[4], (DEPTH, DEC_BATCH, PAST_LEN, H_A), 1.0))
    state_b_conv = nrm(ks[5], (DEPTH, DEC_BATCH, CONV_K - 1, 3 * W_B), 1.0)
    state_b_S = nrm(ks[6], (DEPTH, DEC_BATCH, H_B, HEAD_DIM, HEAD_DIM), 0.1)
    g_pre_mix = gain(ks[7], (DEPTH, D_MODEL))
    w_in = nrm(ks[8], (DEPTH, D_MODEL, D_IN), D_MODEL ** -0.5)
    b_f = FORGET_BIAS + nrm(ks[9], (DEPTH, H_A), 0.1)
    conv_w = nrm(ks[10], (DEPTH, CONV_K, 3 * W_B), 0.5)
    a_log = jnp.log(jax.random.uniform(ks[11], (DEPTH, H_B), f32, 1.0, 16.0))
    dt = jnp.exp(jax.random.uniform(ks[12], (DEPTH, H_B), f32, math.log(1e-3), math.log(1e-1)))
    dt_bias = dt + jnp.log(-jnp.expm1(-dt))
    g_b_out = gain(ks[13], (DEPTH, HEAD_DIM))
    g_a_out = gain(ks[14], (DEPTH, W_A))
    g_cv = gain(ks[15], (DEPTH, W_C))
    b_cv = nrm(ks[16], (DEPTH, W_C), 0.02)
    w_s = nrm(ks[17], (DEPTH, G_C, CM_LEN, CM_LEN), CM_LEN ** -0.5)
    b_s = 1.0 + nrm(ks[18], (DEPTH, G_C, CM_LEN), 0.1)
    g_c_out = gain(ks[19], (DEPTH, W_C))
    w_out = nrm(ks[20], (DEPTH, MIX, D_MODEL), MIX ** -0.5)
    g_post_mix = gain(ks[21], (DEPTH, D_MODEL))
    g_pre_ffn = gain(ks[22], (DEPTH, D_MODEL))
    w_ffn_in = nrm(ks[23], (DEPTH, D_MODEL, 2 * D_FF), D_MODEL ** -0.5)
    w_ffn_out = nrm(ks[24], (DEPTH, D_FF, D_MODEL), D_FF ** -0.5)
    g_post_ffn = gain(ks[25], (DEPTH, D_MODEL))
    return {'x_prompt': x_prompt, 'x_sample': x_sample, 'cache_a_k': cache_a_k, 'cache_a_v': cache_a_v,
            'cache_a_logf': cache_a_logf, 'state_b_conv': state_b_conv, 'state_b_S': state_b_S,
            'g_pre_mix': g_pre_mix, 'w_in': w_in, 'b_f': b_f, 'conv_w': conv_w, 'a_log': a_log,
            'dt_bias': dt_bias, 'g_b_out': g_b_out, 'g_a_out': g_a_out, 'g_cv': g_cv, 'b_cv': b_cv,
            'w_s': w_s, 'b_s': b_s, 'g_c_out': g_c_out, 'w_out': w_out, 'g_post_mix': g_post_mix,
            'g_pre_ffn': g_pre_ffn, 'w_ffn_in': w_ffn_in, 'w_ffn_out': w_ffn_out, 'g_post_ffn': g_post_ffn}


def reference(x_prompt, x_sample, cache_a_k, cache_a_v, cache_a_logf, state_b_conv, state_b_S,
              g_pre_mix, w_in, b_f, conv_w, a_log, dt_bias, g_b_out, g_a_out, g_cv, b_cv,
              w_s, b_s, g_c_out, w_out, g_post_mix, g_pre_ffn, w_ffn_in, w_ffn_out, g_post_ffn):
    bp = x_prompt.shape[0]
    n_new = x_sample.shape[1]
    yp, ys = x_prompt, x_sample
    outs_p, outs_s = [], []
    for l in range(DEPTH):
        p = dict(g_pre_mix=g_pre_mix[l], w_in=w_in[l], b_f=b_f[l], conv_w=conv_w[l], a_log=a_log[l],
                 dt_bias=dt_bias[l], g_b_out=g_b_out[l], g_a_out=g_a_out[l], g_cv=g_cv[l], b_cv=b_cv[l],
                 w_s=w_s[l], b_s=b_s[l], g_c_out=g_c_out[l], w_out=w_out[l], g_post_mix=g_post_mix[l],
                 g_pre_ffn=g_pre_ffn[l], w_ffn_in=w_ffn_in[l], w_ffn_out=w_ffn_out[l], g_post_ffn=g_post_ffn[l])
        conv0 = jnp.zeros((bp, CONV_K - 1, 3 * W_B), x_prompt.dtype)
        S0 = jnp.zeros((bp, H_B, HEAD_DIM, HEAD_DIM), jnp.float32)
        yp, st_p = layer(yp, p, conv0, S0, fox_attend_prompt, GDN_CHUNK, CM_LEN)
        attend_s = functools.partial(fox_attend_sample, ck=cache_a_k[l], cv=cache_a_v[l], clogf=cache_a_logf[l])
        ys, st_s = layer(ys, p, state_b_conv[l], state_b_S[l], attend_s, n_new, n_new)
        outs_p.append(st_p)
        outs_s.append(st_s)
    stk = lambda outs, i: jnp.stack([o[i] for o in outs], axis=0)
    new_a_k_prompt = stk(outs_p, 0)
    new_a_v_prompt = stk(outs_p, 1)
    new_a_logf_prompt = stk(outs_p, 2)
    new_b_conv_prompt = stk(outs_p, 3)
    new_b_S_prompt = stk(outs_p, 4)
    new_a_k_sample = stk(outs_s, 0)
    new_a_v_sample = stk(outs_s, 1)
    new_a_logf_sample = stk(outs_s, 2)
    new_b_conv_sample = stk(outs_s, 3)
    new_b_S_sample = stk(outs_s, 4)
    new_c_v_sample = stk(outs_s, 5)
    return (yp, ys, new_a_k_prompt, new_a_v_prompt, new_a_logf_prompt, new_b_conv_prompt, new_b_S_prompt,
            new_a_k_sample, new_a_v_sample, new_a_logf_sample, new_b_conv_sample, new_b_S_sample, new_c_v_sample)
```

```python
import numpy as np
import concourse.bass as bass
import concourse.mybir as mybir
from concourse.bass_utils import run_bass_kernel_spmd

F32 = mybir.dt.float32
BF16 = mybir.dt.bfloat16
ALU = mybir.AluOpType
AF = mybir.ActivationFunctionType
AX = mybir.AxisListType
EPOCH = 30000

D = 1024
HD = 64
NH = 6
WA = 384
DFF = 2816
DIN = 3218
NS = 16
NSEQ = 4
SCALE = HD ** -0.5
BIG = 30000.0
TT = 512


class Buf:
    def __init__(self, name, v, track=True):
        self.name = name
        self.v = v
        self.track = track
        self.lw = None
        self.rd = {}
        self.dws = None
        self.dwc = 0
        self.drs = None
        self.drc = 0
        self.dr_pending = False
        self.pre = []
        self.via = []
        self.rtrack = True
        self.excl = False

    def __getitem__(self, k):
        return self.v[k]


class Em:
    def __init__(self, nc):
        self.nc = nc
        self.eng = {"pe": nc.tensor, "act": nc.scalar, "dve": nc.vector,
                    "pool": nc.gpsimd, "sp": nc.sync}
        self.cnt = {e: 0 for e in self.eng}
        self.esems = {e: [] for e in self.eng}
        self.seen = {e: {} for e in self.eng}
        self.bufs = []
        self.nsem = 0
        self.ninst = 0
        self.sem_pool = {}

    def sem(self, name):
        self.nsem += 1
        return self.nc.alloc_semaphore(name)

    def reg(self, b):
        self.bufs.append(b)
        return b

    def sb(self, name, shape, dtype=F32):
        t = self.nc.alloc_sbuf_tensor(name, list(shape), dtype)
        return self.reg(Buf(name, t.ap()))

    def dram(self, name, shape, dtype, kind="Internal", track=True):
        t = self.nc.dram_tensor(name, list(shape), dtype, kind=kind)
        b = self.reg(Buf(name, t.ap(), track=track))
        b.rtrack = False
        return b

    def _esem(self, e, seq):
        k = (seq - 1) // EPOCH
        while len(self.esems[e]) <= k:
            self.esems[e].append(self.sem(f"E_{e}_{len(self.esems[e])}"))
        return (e, k), self.esems[e][k], seq - k * EPOCH

    @staticmethod
    def _need(waits, key, sem, val):
        if key not in waits or waits[key][1] < val:
            waits[key] = (sem, val)

    def _collect(self, e, reads, writes, is_dma):
        waits = {}

        def last_write(b, is_write):
            lw = b.lw
            if lw is None:
                return
            if lw[0] == "c":
                _, f, n = lw
                if f == e and not is_dma and e == "pe":
                    return
                key, s, v = self._esem(f, n)
                self._need(waits, key, s, v)
            elif lw[0] == "v":
                if is_dma and is_write:
                    return
                for sb_ in b.via:
                    self._need(waits, ("dr", id(sb_)), sb_.drs, 16 * sb_.drc)
            else:
                if is_dma and is_write:
                    return
                self._need(waits, ("dw", id(b)), b.dws, 16 * b.dwc)

        for b in reads:
            if b.track:
                last_write(b, False)
                if b.excl:
                    for f, n in b.rd.items():
                        if f != e:
                            key, s, v = self._esem(f, n)
                            self._need(waits, key, s, v)
        for b in writes:
            if not b.track:
                continue
            last_write(b, True)
            for f, n in b.rd.items():
                if f == e and not is_dma and e == "pe":
                    continue
                key, s, v = self._esem(f, n)
                self._need(waits, key, s, v)
            if b.dr_pending:
                self._need(waits, ("dr", id(b)), b.drs, 16 * b.drc)
            for key, s, v in b.pre:
                self._need(waits, key, s, v)
            b.pre = []
        out = []
        seen = self.seen[e]
        for key, (s, v) in waits.items():
            if seen.get(key, 0) >= v:
                continue
            seen[key] = v
            out.append((s, v))
        return out

    def op(self, e, make, reads=(), writes=(), after=()):
        waits = self._collect(e, reads, writes, False)
        for (f, n) in after:
            key, s_, v_ = self._esem(f, n)
            if self.seen[e].get(key, 0) < v_:
                self.seen[e][key] = v_
                waits.append((s_, v_))
        eng = self.eng[e]
        for s, v in waits[:-1]:
            eng.wait_ge(s, v)
        ins = make()
        if waits:
            ins._wait_ge(waits[-1][0], waits[-1][1])
        self.cnt[e] += 1
        seq = self.cnt[e]
        _, s, v = self._esem(e, seq)
        ins.then_inc(s, 1)
        for b in reads:
            if b.track:
                b.rd[e] = seq
        for b in writes:
            if b.track:
                b.lw = ("c", e, seq)
                b.rd = {}
                b.dr_pending = False
        self.ninst += 1
        return ins

    def dma(self, q, out_ap, in_ap, src, dst, **kw):
        waits = self._collect(q, [src], [dst], True)
        eng = self.eng[q]
        for s, v in waits[:-1]:
            eng.wait_ge(s, v)
        ins = eng.dma_start(out=out_ap, in_=in_ap, **kw)
        if waits:
            ins._wait_ge(waits[-1][0], waits[-1][1])
        srct = src.track and src.rtrack
        if srct:
            if src.drs is None:
                src.drs = self.sem(f"dr_{src.name}")
            src.drc += 1
            src.dr_pending = True
            ins.then_inc(src.drs, 16)
        if dst.track and srct:
            if src not in dst.via:
                dst.via.append(src)
            dst.lw = ("v",)
            dst.rd = {}
            dst.dr_pending = False
        elif dst.track:
            if dst.dws is None:
                dst.dws = self.sem(f"dw_{dst.name}")
            dst.dwc += 1
            dst.lw = ("d",)
            dst.rd = {}
            dst.dr_pending = False
            ins.then_inc(dst.dws, 16)
        self.ninst += 1
        return ins

    def retire(self, old, new):
        pre = {}
        for b in old:
            if b.lw is not None:
                if b.lw[0] == "c":
                    key, s, v = self._esem(b.lw[1], b.lw[2])
                    self._need(pre, key, s, v)
                elif b.lw[0] == "v":
                    for sb_ in b.via:
                        self._need(pre, ("dr", id(sb_)), sb_.drs, 16 * sb_.drc)
                else:
                    self._need(pre, ("dw", id(b)), b.dws, 16 * b.dwc)
            for f, n in b.rd.items():
                key, s, v = self._esem(f, n)
                self._need(pre, key, s, v)
            if b.dr_pending:
                self._need(pre, ("dr", id(b)), b.drs, 16 * b.drc)
            for key, s, v in b.pre:
                self._need(pre, key, s, v)
        lst = [(k, s, v) for k, (s, v) in pre.items()]
        for b in new:
            b.lw = None
            b.rd = {}
            b.dr_pending = False
            b.pre = list(lst)

    def finish(self):
        sp = self.nc.sync
        for b in self.bufs:
            if b.drs is not None and b.drc:
                sp.wait_ge(b.drs, 16 * b.drc)
            if b.dws is not None and b.dwc:
                sp.wait_ge(b.dws, 16 * b.dwc)
        for e in self.eng:
            if self.cnt[e]:
                _, s, v = self._esem(e, self.cnt[e])
                sp.wait_ge(s, v)


CNAMES = ["IDENT", "ONES", "UINCL", "SG", "NEGN", "NEGM", "NEGQ", "LAST", "BLK64",
          "MASKC", "UINCL_S", "NEGN_S", "NEGM_S", "NEGQ_S", "SEQ_S", "ROWM_S",
          "HALF_LO", "HALF_HI", "SELLO_S0", "SELLO_S1", "SELLO_S2", "SELLO_S3",
          "SELHI_S0", "SELHI_S1", "SELHI_S2", "SELHI_S3", "SEQCOL_S01", "SEQCOL_S23"]


def make_consts():
    i = np.arange(128)
    r = i[:, None]
    c = i[None, :]
    t = {}
    t["IDENT"] = (r == c)
    t["ONES"] = np.ones((128, 128), bool)
    t["UINCL"] = (r <= c)
    t["SG"] = (r > c)
    t["NEGN"] = np.where(c < r, 0.0, -BIG)
    t["NEGM"] = np.where(r < c, 0.0, -BIG)
    t["NEGQ"] = np.where(r <= c, 0.0, -BIG)
    t["LAST"] = (r == 127) & (c >= 0)
    t["BLK64"] = (r // 64 == c // 64)
    t["MASKC"] = (c // 64 <= r // 64)
    same = (r // NS == c // NS) & (r < 64) & (c < 64)
    t["UINCL_S"] = same & (r <= c)
    t["NEGN_S"] = np.where(same & (c < r), 0.0, -BIG)
    t["NEGM_S"] = np.where(same & (r < c), 0.0, -BIG)
    t["NEGQ_S"] = np.where(same & (r <= c), 0.0, -BIG)
    t["SEQ_S"] = same
    rowm = np.zeros((128, 128), np.float32)
    for s in range(NSEQ):
        rowm[s * NS:(s + 1) * NS, s] = 1.0
    t["ROWM_S"] = rowm
    t["HALF_LO"] = (c < 64) & (r >= 0)
    t["HALF_HI"] = (c >= 64) & (r >= 0)
    for s in range(NSEQ):
        inseq = (r // NS == s) & (r < 64)
        t[f"SELLO_S{s}"] = inseq & (c < 64)
        t[f"SELHI_S{s}"] = inseq & (c >= 64)
    t["SEQCOL_S01"] = ((c % 64) // NS == c // 64) & (r >= 0)
    t["SEQCOL_S23"] = ((c % 64) // NS == 2 + c // 64) & (r >= 0)
    return np.concatenate([np.asarray(t[k], np.float32) for k in CNAMES], axis=1)


C_AQ, C_AK, C_AV, C_AF, C_BQKV, C_BA, C_BB, C_BZ, C_CU, C_CV = 0, 384, 768, 1152, 1158, 2310, 2316, 2322, 2706, 2962


def slot_plan():
    slots = []

    def fm(wname, cols):
        pcs = [(j * 1024, wname, 8, c0, 128) for j, c0 in enumerate(cols)]
        slots.append(dict(kind="fm", n=len(cols), used=len(cols) * 1024, pieces=pcs))

    def tm(wname, segs, kt=8):
        ncols = sum(n for _, n in segs)
        pcs, off = [], 0
        for c0, n in segs:
            pcs.append((off, wname, kt, c0, n, ncols))
            off += n
        slots.append(dict(kind="tm", ncols=ncols, kt=kt, used=kt * ncols, pieces=pcs))

    fm("w_in", [C_AQ, C_AQ + 128, C_AQ + 256, C_AK])
    fm("w_in", [C_AK + 128, C_AK + 256])
    tm("w_in", [(C_AK, 384)])
    tm("w_in", [(C_AV, 384)])
    tm("w_in", [(C_AF, 6), (C_BA, 12)])
    fm("w_in", [C_BQKV + 128 * j for j in range(0, 4)])
    fm("w_in", [C_BQKV + 128 * j for j in range(4, 8)])
    fm("w_in", [C_BQKV + 128 * 8])
    tm("w_in", [(C_BZ, 384)])
    tm("w_in", [(C_CU, 512)])
    tm("w_out", [(0, 512)])
    tm("w_out", [(512, 512)])
    for s in range(11):
        fm("w_ffn_in", [256 * s, 256 * s + 128, DFF + 256 * s, DFF + 256 * s + 128])
    for oc in range(8):
        slots.append(dict(kind="fo", used=22 * 128, pieces=[(0, "w_ffn_out", 22, oc * 128, 128, 128)]))
    return slots


def make_masks():
    i = np.arange(128)[:, None]
    j = np.arange(128)[None, :]
    ms = []
    for lv in range(1, 8):
        ms.append(((i >> lv) == (j >> lv)) & ((i >> (lv - 1)) != (j >> (lv - 1))) & (j < i))
    ms = ms + [m.T for m in ms]
    return np.concatenate([m.astype(np.float32) for m in ms], axis=1)


SLOTS = slot_plan()
NSLOT_L = len(SLOTS)
SLOT_E = 4096


IN_SPECS = lambda S, P: [
    ("xp", [S, D]), ("xs", [NSEQ * NS, D]),
    ("ck", [2, NSEQ, P, WA]), ("cv", [2, NSEQ, P, WA]), ("clf", [2, NSEQ, P, NH]),
    ("sconv", [2, NSEQ, 3, 1152]), ("sS", [2, NSEQ, NH, HD, HD]),
    ("g_pre_mix", [2, D]), ("w_in", [2, D, DIN]), ("b_f", [2, NH]), ("conv_w", [2, 4, 1152]),
    ("a_log", [2, NH]), ("dt_bias", [2, NH]), ("g_b_out", [2, HD]), ("g_a_out", [2, WA]),
    ("g_cv", [2, 256]), ("b_cv", [2, 256]), ("w_s", [2, 4, 128, 128]), ("b_s", [2, 4, 128]),
    ("g_c_out", [2, 256]), ("w_out", [2, D, D]), ("g_post_mix", [2, D]), ("g_pre_ffn", [2, D]),
    ("w_ffn_in", [2, D, 2 * DFF]), ("w_ffn_out", [2, DFF, D]), ("g_post_ffn", [2, D]),
    ("cst", [128, 128 * len(CNAMES)]), ("cstm", [128, 14 * 128]),
]
OUT_SPECS = lambda S: [
    ("yp", [S, D]), ("ys", [NSEQ * NS, D]),
    ("kp", [2, S, WA]), ("vp", [2, S, WA]), ("lfp", [2, S, NH]), ("convp", [2, 3, 1152]),
    ("Sp", [2, NH * HD, HD]),
    ("ks", [2, NSEQ * NS, WA]), ("vs", [2, NSEQ * NS, WA]), ("lfs", [2, NSEQ * NS, NH]),
    ("convs", [2, NSEQ, 3, 1152]), ("Ss", [2, NSEQ, NH * HD, HD]), ("cvs", [2, NSEQ * NS, 256]),
]


import os as _os


class StopBuild(Exception):
    pass


def _stop(tag):
    if _os.environ.get("KSTOP") == tag:
        raise StopBuild(tag)


class Seg:
    def __init__(self, sample):
        self.sample = sample
        if sample:
            self.R, self.NB, self.nseq, self.L, self.lev = 64, 1, NSEQ, NS, 4
        else:
            self.R, self.NB, self.nseq, self.L, self.lev = 128, TT // 128, 1, 128, 7
        self.T = self.R * self.NB


class Builder:
    def __init__(self, S, P, dbg=()):
        self.S, self.P = S, P
        self.dbg_names = list(dbg)
        nc = self.nc = bass.Bass("TRN2", target_bir_lowering=False)
        em = self.em = Em(nc)
        self.din = {n: em.dram(n, sh, F32, kind="ExternalInput", track=False) for n, sh in IN_SPECS(S, P)}
        self.dout = {n: em.dram(n, sh, F32, kind="ExternalOutput", track=False) for n, sh in OUT_SPECS(S)}
        self.dbg_out = {}
        self.wscr = em.dram("wscr", [2, NSLOT_L, 128, SLOT_E], BF16)
        self.ktscr = [em.dram(f"ktscr{l}", [128, 3, S], BF16) for l in range(2)]
        self.vscr = [em.dram(f"vscr{l}", [S, NH * 65], BF16) for l in range(2)]
        self._alloc()
        self._ops()

    def carve(self, name, off, shape, dtype, parts=128):
        esz = 2 if dtype == BF16 else 4
        n = int(np.prod(shape[1:]))
        nbytes = n * esz
        assert off % 4 == 0 and off + nbytes <= self.ARENA, (name, off, nbytes)
        v = self.arena_ap[:, off // 4:(off + nbytes + 3) // 4]
        if dtype == BF16:
            v = v.bitcast(BF16)[:, 0:n]
        if len(shape) > 2:
            names = " ".join(f"d{i}" for i in range(len(shape) - 1))
            kw = {f"d{i}": shape[i + 1] for i in range(len(shape) - 1)}
            v = v.rearrange(f"p ({names}) -> p {names}", **kw)
        if shape[0] < 128:
            v = v[0:shape[0]]
        return self.em.reg(Buf(name, v))

    def _alloc(self):
        em, nc = self.em, self.nc
        self.ARENA = 77 * 1024
        self.arena_ap = nc.alloc_sbuf_tensor("arena", [128, self.ARENA // 4], F32).ap()
        self.X = em.sb("X", [128, TT // 128, D])
        self.ring = [em.sb(f"ring{i}", [128, SLOT_E], BF16) for i in range(4)]
        self.cst = em.sb("cst_sb", [128, 128 * len(CNAMES)])
        self.cstb = em.sb("cstb", [128, 4 * 128], BF16)
        self.maskb = em.sb("maskb", [128, 14, 128], BF16)
        self.gA = em.sb("gA", [128, D])
        self.gB = em.sb("gB", [128, D])
        self.msb = [em.sb(f"msb{i}", [128, D]) for i in range(2)]
        self.junks = [em.sb(f"junk{i}", [128, D], BF16) for i in range(4)]
        self.junk_rr = 0
        self.psb = [em.sb(f"par{l}", [128, 1408]) for l in range(2)]
        self.wst = [em.sb(f"wst{l}", [128, 4, 128], BF16) for l in range(2)]
        self.wsts = [em.sb(f"wsts{l}", [128, 4, 64], BF16) for l in range(2)]
        self.bst = [em.sb(f"bst{l}", [128, 8]) for l in range(2)]
        self.call = [em.sb(f"call{l}", [128, max(self.S // 128, 1), NH]) for l in range(2)]
        self.carry = [em.sb(f"carry{l}", [1, NH]) for l in range(2)]
        self.Sst = [em.sb(f"Sst{l}", [128, 3, HD]) for l in range(2)]
        self.Sstb = [em.sb(f"Sstb{l}", [128, 3, HD], BF16) for l in range(2)]
        self.Sss = [[em.sb(f"Sss{l}_{s}", [128, 3, HD]) for s in range(NSEQ)] for l in range(2)]
        self.Sssb = [[em.sb(f"Sssb{l}_{s}", [128, 3, HD], BF16) for s in range(NSEQ)] for l in range(2)]
        self.cstate = [em.sb(f"cstate{l}", [128, 9, 3]) for l in range(2)]
        self.small = em.sb("small", [128, 64])
        self.eps = em.sb("eps", [128, 4])
        self.ptz = [em.sb(f"ptz{i}", [128, 64], BF16) for i in range(6 * NSEQ)]
        self.ch1 = {nm: em.sb("c1_" + nm, [128, 3, 128], BF16) for nm in ["NB0", "MB0", "NB1", "MB1", "CB3", "BB3", "QALL"]}
        self.psum = []
        for i in range(8):
            t = nc.alloc_psum_tensor(f"ps{i}", [128, 512], F32)
            self.psum.append(em.reg(Buf(f"ps{i}", t.ap())))
            self.psum[-1].excl = True
        self.ps_rr = 0
        self.ps_pool = list(range(8))
        cv = self.carve
        self.FT = cv("FT", 0, [128, 8, TT], BF16)
        self.HB = [cv("HB0", 8192, [128, D], BF16), cv("HB1", 10240, [128, D], BF16)]
        self.MIX = cv("MIX", 12288, [128, TT // 128, D], BF16)
        self.SM = cv("SM", 20480, [128, TT // 128, 18], F32)
        self.FX = [self.FT, self.HB[0], self.HB[1], self.MIX, self.SM]
        P0 = 21504
        NKB = max(self.S // 128, self.P // 128, 1)
        assert NKB * 24 <= 1536
        d = self.SD = {}
        d["QAT"] = cv("QAT", P0 + 0, [128, 3, TT], BF16)
        d["KAT"] = cv("KAT", P0 + 3072, [128, 3, TT], BF16)
        d["VAUG"] = cv("VAUG", P0 + 6144, [128, 4, NH * 65], BF16)
        d["STG0"] = cv("STG0", P0 + 9280, [128, 4, WA], F32)
        d["STG1"] = cv("STG1", P0 + 15424, [128, 4, WA], F32)
        self.OTH = [cv(f"OTH{h}", P0 + 9280 + 2048 * h, [128, TT], F32) for h in range(NH)]
        d["LOGF"] = cv("LOGF", P0 + 21568, [128, 4, NH], F32)
        d["BIAS"] = cv("BIAS", P0 + 21696, [128, NKB, NH], F32)
        d["CREF"] = cv("CREF", P0 + 23232, [128, 8], F32)
        d["RC"] = cv("RC", P0 + 23296, [128, 4, NH], F32)
        for i in range(3):
            d[f"KTR{i}"] = cv(f"KTR{i}", P0 + 23424 + 3072 * i, [128, 3, TT], BF16)
            d[f"VR{i}"] = cv(f"VR{i}", P0 + 32640 + 3136 * i, [128, 4, NH * 65], BF16)
        for i in range(4):
            d[f"PT{i}"] = cv(f"PT{i}", P0 + 42048 + 1024 * i, [128, TT], BF16)
        d["OA"] = cv("OA", P0 + 46144, [128, 4, WA], F32)
        ds = self.SDS = {}
        o = P0 + 23424
        ds["KCF"] = cv("KCF", o, [128, 4, WA], F32); o += 6144
        ds["VCF"] = cv("VCF", o, [128, 4, WA], F32); o += 6144
        ds["KCB"] = cv("KCB", o, [128, 4, WA], BF16); o += 3072
        ds["KCT"] = cv("KCT", o, [128, 3, 512], BF16); o += 3072
        ds["VAUGC"] = cv("VAUGC", o, [128, 4, NH * 65], BF16); o += 3136
        assert o <= P0 + 46144
        o = P0 + 52288
        ds["CCW"] = cv("CCW", o, [128, NKB, NH], F32); o += 768 * 2
        ds["CCT"] = cv("CCT", o, [128, NKB, NH], F32); o += 768 * 2
        ds["CCP"] = cv("CCP", o, [128, NKB, NH], F32); o += 768 * 2
        ds["CTOT"] = cv("CTOT", o, [128, NSEQ, NH], F32); o += 128
        assert o <= self.ARENA
        e = self.SE = {}
        o = P0
        e["ZC0"] = cv("ZC0", o, [128, 520], F32); o += 2080
        e["ZC1"] = cv("ZC1", o, [128, 520], F32); o += 2080
        e["Y0"] = cv("Y0", o, [128, TT], F32); o += 2048
        e["Y1"] = cv("Y1", o, [128, TT], F32); o += 2048
        e["SQ"] = cv("SQ", o, [128, TT], F32); o += 2048
        e["RS"] = cv("RS", o, [128, TT], F32); o += 2048
        assert o == P0 + 12352
        o = P0
        for nm in ["GS3", "LBN3", "LBD3", "E3"]:
            e[nm] = cv(nm, o, [128, 3, 128], F32); o += 1536
        for nm in ["NB0", "NB1", "MB0", "MB1"]:
            e[nm] = cv(nm, o, [128, 3, 128], BF16); o += 768
        e["RV"] = cv("RV", o, [128, NH, HD], BF16); o += 768
        e["U"] = cv("U", o, [128, NH, HD], BF16); o += 768
        e["UVSB"] = cv("UVSB", o, [128, NH * HD], F32); o += 1536
        assert o <= P0 + 12352
        e["BZS"] = cv("BZS", P0, [128, 4, WA], F32)
        e["SQO"] = cv("SQO", P0 + 6144, [128, 4, WA], F32)
        o = P0 + 12352
        for nm in ["QBT", "KBT", "VBT"]:
            e[nm] = cv(nm, o, [128, 3, TT], BF16); o += 3072
        for nm in ["KTOK", "VTOK"]:
            e[nm] = cv(nm, o, [128, 4, WA], BF16); o += 3072
        e["OTOK"] = cv("OTOK", o, [128, 4, WA], F32); o += 6144
        for nm in ["GL", "LB", "BETA", "T6A", "T6B", "RS6"]:
            e[nm] = cv(nm, o, [128, 4, NH], F32); o += 128
        e["GH"] = cv("GH", o, [128, 16], F32); o += 64
        for nm in ["EG", "BEG", "EGL"]:
            e[nm] = cv(nm, o, [128, 8], F32); o += 32
        e["GLS"] = cv("GLS", o, [128, NSEQ, 4], F32); o += 64
        e["CSN"] = cv("CSN", o, [128, 9, NSEQ, 3], F32); o += 448
        e["CSP"] = cv("CSP", o, [128, 9, NSEQ, 3], F32); o += 448
        e["GI6"] = cv("GI6", o, [128, NH, 128], F32); o += 3072
        e["TTALL"] = cv("TTALL", o, [128, NH, 128], BF16); o += 1536
        e["QKTALL"] = cv("QKTALL", o, [128, NH, 128], BF16); o += 1536
        for nm in ["RKPAD", "KDPAD", "KDM"]:
            e[nm] = cv(nm, o, [128, 3, 2, 128], BF16); o += 1536
        e["WT"] = cv("WT", o, [128, 3, 128], BF16); o += 768
        e["WTM"] = cv("WTM", o, [128, NSEQ, 3, 64], BF16); o += 1536
        e["QGT"] = cv("QGT", o, [128, 3, 128], BF16); o += 768
        e["QGTM"] = cv("QGTM", o, [128, NSEQ, 3, 64], BF16); o += 1536
        e["EGT"] = cv("EGT", o, [128, 3, 128], F32); o += 1536
        for nm in ["QALL", "CB3", "BB3"]:
            e[nm] = cv(nm, o, [128, 3, 128], BF16); o += 768
        assert o <= self.ARENA, o
        f = self.SF = {}
        o = P0
        f["CUV"] = cv("CUV", o, [128, 4, 512], F32); o += 8192
        f["T1"] = cv("T1", o, [128, 4, 512], F32); o += 8192
        f["T2"] = cv("T2", o, [128, 4, 256], F32); o += 4096
        f["VN"] = cv("VN", o, [128, 4, 256], F32); o += 4096
        f["VNB"] = cv("VNB", o, [128, 4, 256], BF16); o += 2048
        f["OC"] = cv("OC", o, [128, 4, 256], F32); o += 4096
        h = self.SH = {}
        o = P0
        h["ACTT"] = cv("ACTT", o, [128, 22, TT], BF16); o += 22528
        h["SGT0"] = cv("SGT0", o, [128, TT], F32); o += 2048
        h["SGT1"] = cv("SGT1", o, [128, TT], F32); o += 2048
        h["YT0"] = cv("YT0", o, [128, TT], F32); o += 2048
        h["YT1"] = cv("YT1", o, [128, TT], F32); o += 2048
        h["YTOK"] = cv("YTOK", o, [128, 4, D], F32); o += 16384
        assert o <= self.ARENA
        self.cur_set = []

    def switch(self, new):
        new = list(new)
        self.em.retire(self.cur_set, new)
        self.cur_set = new

    @property
    def junk(self):
        self.junk_rr += 1
        return self.junks[self.junk_rr % 4]

    def C(self, name, r=128, c=128):
        i = CNAMES.index(name)
        return self.cst[0:r, i * 128:i * 128 + c]

    def ps(self):
        i = self.ps_pool[self.ps_rr % len(self.ps_pool)]
        self.ps_rr += 1
        return self.psum[i]

    def mm(self, ps, out, lhsT, rhs, start, stop, reads, skip=False):
        nc = self.nc
        K = lhsT.shape[0]
        rg = (lhsT.base_partition(), K) if K <= 64 else None
        after = []
        if not hasattr(self, "_last_mm"):
            self._last_mm = {}
        last = self._last_mm.get(id(ps))
        if last is not None and last[0] != rg:
            after.append(("pe", last[1]))
        if skip:
            ins = self.em.op("pe", lambda: nc.tensor.matmul(out, lhsT, rhs, start=start, stop=stop, skip_group_check=True),
                             reads, [ps], after=after)
        else:
            ins = self.em.op("pe", lambda: nc.tensor.matmul(out, lhsT, rhs, start=start, stop=stop), reads, [ps], after=after)
        self._last_mm[id(ps)] = (rg, self.em.cnt["pe"])
        return ins

    def tr(self, ps, out, in_, ident, reads):
        nc = self.nc
        return self.em.op("pe", lambda: nc.tensor.transpose(out, in_, ident), reads, [ps])

    def act(self, out, in_, func, reads, writes, **kw):
        nc = self.nc
        return self.em.op("act", lambda: nc.scalar.activation(out, in_, func, **kw), reads, writes)

    def tt(self, out, a, b, op, reads, writes, eng="dve"):
        e = self.nc.vector if eng == "dve" else self.nc.gpsimd
        return self.em.op(eng, lambda: e.tensor_tensor(out, a, b, op), reads, writes)

    def ts(self, out, a, s1, s2, op0, op1, reads, writes, eng="dve"):
        e = self.nc.vector if eng == "dve" else self.nc.gpsimd
        if s2 is None:
            return self.em.op(eng, lambda: e.tensor_scalar(out, a, s1, None, op0), reads, writes)
        return self.em.op(eng, lambda: e.tensor_scalar(out, a, s1, s2, op0, op1), reads, writes)

    def stt(self, out, a, s, b, op0, op1, reads, writes, eng="dve"):
        e = self.nc.vector if eng == "dve" else self.nc.gpsimd
        return self.em.op(eng, lambda: e.scalar_tensor_tensor(out, a, s, b, op0, op1), reads, writes)

    def cp(self, out, a, reads, writes, eng="dve"):
        if eng == "act":
            return self.em.op("act", lambda: self.nc.scalar.copy(out, a), reads, writes)
        e = self.nc.vector if eng == "dve" else self.nc.gpsimd
        return self.em.op(eng, lambda: e.tensor_copy(out, a), reads, writes)

    def memset(self, buf, ap, val, eng="pool"):
        e = self.nc.vector if eng == "dve" else self.nc.gpsimd
        return self.em.op(eng, lambda: e.memset(ap, val), [], [buf])

    def rstd(self, out, in_, scale, eps_col, reads, writes):
        self.act(out, in_, AF.Ln, list(reads) + [self.eps], writes, bias=self.eps[0:out.shape[0], eps_col:eps_col + 1], scale=scale)
        self.act(out, out, AF.Exp, writes, writes, scale=-0.5)

    def dbg(self, name, buf, ap):
        if name not in self.dbg_names:
            return
        shape = list(ap.shape)
        d = self.em.dram("dbg_" + name, shape, F32 if ap.dtype == F32 else BF16, kind="ExternalOutput", track=False)
        self.dbg_out[name] = d
        self.em.dma("pool", d.v, ap, buf, d)

    PB = dict(g_a_out=(0, 384), g_b_out=(384, 64), g_cv=(448, 256), b_cv=(704, 256), g_c_out=(960, 256),
              b_f=(1216, 6), dt_bias=(1222, 6), a_log=(1228, 6), nega=(1234, 6), cw=(1300, 36))

    def par(self, l, name, r=128):
        o, n = self.PB[name]
        return self.psb[l][0:r, o:o + n]

    def setup(self):
        em, nc, din = self.em, self.nc, self.din
        em.dma("sp", self.cst[:, :], din["cst"][:, :], din["cst"], self.cst)
        names = ["IDENT", "UINCL", "UINCL_S", "ONES"]
        for i, nm in enumerate(names):
            self.cp(self.cstb[:, i * 128:(i + 1) * 128], self.C(nm), [self.cst], [self.cstb])
        self.memset(self.eps, self.eps[:, 0:1], 1e-6)
        self.memset(self.eps, self.eps[:, 1:2], 1e-5)
        self.memset(self.eps, self.eps[:, 2:3], 1.0)
        self.memset(self.eps, self.eps[:, 3:4], 0.0)
        for b in self.ptz:
            self.memset(b, b[:, :], 0.0)
        mtmp = self.carve("m_tmp", 8192, [128, 14, 128], F32)
        self.cur_set.append(mtmp)
        em.dma("sp", mtmp[:, :, :], din["cstm"][:, :].rearrange("p (a b) -> p a b", a=14), din["cstm"], mtmp)
        self.cp(self.maskb[:, :, :], mtmp[:, :, :], [mtmp], [self.maskb])
        tmp = self.carve("ws_tmp", 0, [128, 4, 128], F32)
        tmpb = self.carve("ws_tmpb", 2048, [128, 4, 128], BF16)
        tmp2 = self.carve("ws_tmp2", 4096, [64, 4, 64], F32)
        tmp2b = self.carve("ws_tmp2b", 5120, [64, 4, 64], BF16)
        self.cur_set += [tmp, tmpb, tmp2, tmp2b]
        for l in range(2):
            for nm in ["g_a_out", "g_b_out", "g_cv", "b_cv", "g_c_out", "b_f", "dt_bias", "a_log"]:
                o, n = self.PB[nm]
                em.dma("sp", self.psb[l][:, o:o + n], din[nm][l].partition_broadcast(128), din[nm], self.psb[l])
            o, n = self.PB["cw"]
            cwv = self.psb[l][:, o:o + n].rearrange("p (c i) -> p c i", c=9)
            for i in range(4):
                em.dma("sp", cwv[:, :, i], din["conv_w"][l, i].rearrange("(c p) -> p c", p=128), din["conv_w"], self.psb[l],
                       allow_slow_non_contiguous=True)
            self.act(self.par(l, "nega"), self.par(l, "a_log"), AF.Exp, [self.psb[l]], [self.psb[l]])
            self.ts(self.par(l, "nega"), self.par(l, "nega"), -1.0, None, ALU.mult, None, [self.psb[l]], [self.psb[l]])
            em.dma("sp", tmp[:, :, :], din["w_s"][l].rearrange("g i j -> i g j"), din["w_s"], tmp)
            self.tt(tmpb[:, :, :], tmp[:, :, :], self.C("MASKC").unsqueeze(1).to_broadcast([128, 4, 128]), ALU.mult,
                    [tmp, self.cst], [tmpb])
            ps = self.ps()
            psb = ps.v.bitcast(BF16)
            for g in range(4):
                self.tr(ps, psb[:, g * 128:(g + 1) * 128], tmpb[:, g, :], self.cstb[:, 0:128], [tmpb, self.cstb])
            self.cp(self.wst[l][:, :, :], psb[:, 0:512].rearrange("p (g i) -> p g i", g=4), [ps], [self.wst[l]])
            self.memset(tmp2, tmp2[:, :, :], 0.0)
            for s in range(NSEQ):
                em.dma("sp", tmp2[s * NS:(s + 1) * NS, :, s * NS:(s + 1) * NS],
                       din["w_s"][l, :, 0:NS, 0:NS].rearrange("g t u -> t g u"), din["w_s"], tmp2)
            self.cp(tmp2b[:, :, :], tmp2[:, :, :], [tmp2], [tmp2b])
            ps = self.ps()
            psb = ps.v.bitcast(BF16)
            for g in range(4):
                self.tr(ps, psb[0:64, g * 64:(g + 1) * 64], tmp2b[:, g, :], self.cstb[0:64, 0:64], [tmp2b, self.cstb])
            self.cp(self.wsts[l][0:64, :, :], psb[0:64, 0:256].rearrange("p (g i) -> p g i", g=4), [ps], [self.wsts[l]])
            em.dma("sp", self.bst[l][:, 0:4], din["b_s"][l].rearrange("g i -> i g"), din["b_s"], self.bst[l],
                   allow_slow_non_contiguous=True)
            for s in range(NSEQ):
                em.dma("sp", self.bst[l][s * NS:(s + 1) * NS, 4:8], din["b_s"][l, :, 0:NS].rearrange("g i -> i g"),
                       din["b_s"], self.bst[l], allow_slow_non_contiguous=True)
            self.memset(self.Sst[l], self.Sst[l][:, :, :], 0.0)
            self.memset(self.cstate[l], self.cstate[l][:, :, :], 0.0)
            self.memset(self.carry[l], self.carry[l][:, :], 0.0)
            for s in range(NSEQ):
                src = din["sS"][l, s].rearrange("(pr hh) d e -> hh d pr e", hh=2)
                for hh in range(2):
                    em.dma("sp", self.Sss[l][s][hh * 64:(hh + 1) * 64, :, :], src[hh], din["sS"], self.Sss[l][s])

    def wview(self, ring, slot):
        k = slot["kind"]
        if k == "fm":
            return ring.v.rearrange("p (j ko c) -> p j ko c", j=4, ko=8)
        if k == "tm":
            n = slot["ncols"]
            return ring.v[:, 0:slot["kt"] * n].rearrange("p (ko c) -> p ko c", c=n)
        return ring.v[:, 0:22 * 128].rearrange("p (kt c) -> p kt c", c=128)

    def prepass(self):
        em, din = self.em, self.din
        stg = [self.carve(f"pp_stg{i}", i * 16384, [128, SLOT_E], F32) for i in range(2)]
        img = [self.carve(f"pp_img{i}", 32768 + i * 8192, [128, SLOT_E], BF16) for i in range(2)]
        self.switch(stg + img)
        k = 0
        for l in range(2):
            for si, slot in enumerate(SLOTS):
                st, im = stg[k % 2], img[k % 2]
                for pc in slot["pieces"]:
                    if slot["kind"] == "fm":
                        off, wn, kt, c0, n = pc
                        dst = st.v[:, off:off + 1024].rearrange("p (ko c) -> p ko c", ko=8)
                        src = din[wn][l].rearrange("(ko p) c -> p ko c", p=128)[:, :, c0:c0 + n]
                    else:
                        off, wn, kt, c0, n, ncols = pc
                        dst = st.v[:, 0:kt * ncols].rearrange("p (ko c) -> p ko c", c=ncols)[:, :, off:off + n]
                        src = din[wn][l].rearrange("(ko p) c -> p ko c", p=128)[:, :, c0:c0 + n]
                    em.dma("sp", dst, src, din[wn], st)
                used = slot["used"]
                eng = ["dve", "act"][k % 2]
                self.cp(im[:, 0:used], st[:, 0:used], [st], [im], eng=eng)
                em.dma("pool", self.wscr[l, si, :, 0:used], im[:, 0:used], im, self.wscr)
                k += 1

    def wstream_init(self, ntl):
        self.w_seq = [(l, si) for (_, l) in ntl for si in range(NSLOT_L)]
        self.w_used = 0
        self.w_issued = 0

    def wnext(self, expect=None):
        em = self.em
        idx = self.w_used
        while self.w_issued < min(len(self.w_seq), idx + 3):
            l, si = self.w_seq[self.w_issued]
            used = SLOTS[si]["used"]
            rb = self.ring[self.w_issued % 4]
            em.dma("sp", rb[:, 0:used], self.wscr[l, si, :, 0:used], self.wscr, rb)
            self.w_issued += 1
        l, si = self.w_seq[idx]
        if expect is not None:
            assert si == expect, (si, expect)
        self.w_used += 1
        rb = self.ring[idx % 4]
        return rb, self.wview(rb, SLOTS[si])

    def norm_T(self, g, gain):
        R, NB = g.R, g.NB
        ss = self.small
        for n in range(NB):
            jk = self.junk
            self.act(jk[0:R, :], self.X[0:R, n, :], AF.Square, [self.X], [jk, ss],
                     accum_out=ss[0:R, n:n + 1])
        self.rstd(ss[0:R, 8:8 + NB], ss[0:R, 0:NB], 1.0 / D, 0, [ss], [ss])
        for n in range(NB):
            hb = self.HB[n % 2]
            self.stt(hb[0:R, :], self.X[0:R, n, :], ss[0:R, 8 + n:9 + n], gain[0:R, :], ALU.mult, ALU.mult,
                     [self.X, ss, gain], [hb])
            self.transp_block(g, hb, n, 8)

    def transp_block(self, g, hb, n, nk, dst=None):
        R = g.R
        dst = dst or self.FT
        ps = self.ps()
        psb = ps.v.bitcast(BF16)
        for k in range(nk):
            self.tr(ps, psb[:, k * R:(k + 1) * R], hb[0:R, k * 128:(k + 1) * 128], self.cstb[0:R, 0:R], [hb, self.cstb])
        self.cp(dst[:, 0:nk, n * R:(n + 1) * R], psb[:, 0:nk * R].rearrange("p (k r) -> p k r", k=nk), [ps], [dst],
                eng="act" if n % 2 else "dve")

    def fm_proj(self, g, ring, W, j, evac):
        T = g.T
        ps = self.ps()
        for ko in range(8):
            self.mm(ps, ps[:, 0:T], W[:, j, ko, :], self.FT[:, ko, 0:T], ko == 0, ko == 7, [ring, self.FT])
        evac(ps)

    def tm_proj(self, g, ring, W, ncols, evac, src=None, nk=8):
        R = g.R
        src = src or self.FT
        for n in range(g.NB):
            ps = self.ps()
            for ko in range(nk):
                self.mm(ps, ps[0:R, 0:ncols], src[:, ko, n * R:(n + 1) * R], W[:, ko, 0:ncols], ko == 0, ko == nk - 1,
                        [ring, src])
            evac(n, ps)

    def epilogue(self, g, n, halves, srcbufs, gain):
        R = g.R
        ss = self.small
        msb = self.msb[n % 2]
        for hf in range(2):
            jk = self.junk
            self.act(jk[0:R, 0:512], halves[hf], AF.Square, srcbufs, [jk, ss],
                     accum_out=ss[0:R, 16 + hf:17 + hf])
            self.cp(msb[0:R, hf * 512:(hf + 1) * 512], halves[hf], srcbufs, [msb])
        self.tt(ss[0:R, 18:19], ss[0:R, 16:17], ss[0:R, 17:18], ALU.add, [ss], [ss])
        self.rstd(ss[0:R, 19:20], ss[0:R, 18:19], 1.0 / D, 0, [ss], [ss])
        self.stt(msb[0:R, :], msb[0:R, :], ss[0:R, 19:20], gain[0:R, :], ALU.mult, ALU.mult, [msb, ss, gain], [msb])
        self.tt(self.X[0:R, n, :], self.X[0:R, n, :], msb[0:R, :], ALU.add, [self.X, msb], [self.X])

    def rms_rows(self, g, src, srcbuf, width, gain_ap, gainbuf, dst, dstbuf, scol=24):
        R, NB = g.R, g.NB
        ss = self.small
        for n in range(NB):
            jk = self.junk
            self.act(jk[0:R, 0:width], src[0:R, n, :], AF.Square, [srcbuf], [jk, ss],
                     accum_out=ss[0:R, scol + n:scol + n + 1])
        self.rstd(ss[0:R, scol + 4:scol + 4 + NB], ss[0:R, scol:scol + NB], 1.0 / width, 0, [ss], [ss])
        for n in range(NB):
            self.stt(dst[0:R, n, :], src[0:R, n, :], ss[0:R, scol + 4 + n:scol + 5 + n], gain_ap, ALU.mult, ALU.mult,
                     [srcbuf, ss, gainbuf], [dstbuf])

    def layer(self, g, l, ti):
        em, din, dout = self.em, self.din, self.dout
        R, NB, T = g.R, g.NB, g.T
        t0 = ti * TT
        psb = self.psb[l]
        em.dma("pool", self.gA[:, :], din["g_pre_mix"][l].partition_broadcast(128), din["g_pre_mix"], self.gA)
        em.dma("pool", self.gB[:, :], din["g_post_mix"][l].partition_broadcast(128), din["g_post_mix"], self.gB)
        self.norm_T(g, self.gA)
        _stop("normT")
        d = self.SD
        self.switch(list(d.values()) + (list(self.SDS.values()) if g.sample else []))
        QAT, KAT, VAUG, STG0, STG1 = d["QAT"], d["KAT"], d["VAUG"], d["STG0"], d["STG1"]
        ring, W = self.wnext(0)
        for j in range(3):
            self.fm_proj(g, ring, W, j, lambda ps, j=j: self.cp(QAT[:, j, 0:T], ps[:, 0:T], [ps], [QAT], eng="act"))
        self.fm_proj(g, ring, W, 3, lambda ps: self.cp(KAT[:, 0, 0:T], ps[:, 0:T], [ps], [KAT], eng="dve"))
        ring, W = self.wnext(1)
        for j in range(2):
            self.fm_proj(g, ring, W, j, lambda ps, j=j: self.cp(KAT[:, 1 + j, 0:T], ps[:, 0:T], [ps], [KAT], eng="act"))
        if not g.sample:
            em.dma("pool", self.ktscr[l][:, :, t0:t0 + T], KAT[:, :, 0:T], KAT, self.ktscr[l])
        kout = dout["ks"][l] if g.sample else dout["kp"][l, t0:t0 + T, :]
        vout = dout["vs"][l] if g.sample else dout["vp"][l, t0:t0 + T, :]
        ring, W = self.wnext(2)
        self.tm_proj(g, ring, W, WA, lambda n, ps: self.cp(STG0[0:R, n, :], ps[0:R, 0:WA], [ps], [STG0], eng="act"))
        em.dma("pool", kout.rearrange("(n p) c -> p n c", p=R), STG0[0:R, 0:NB, :], STG0, dout["kp"])
        ring, W = self.wnext(3)
        self.tm_proj(g, ring, W, WA, lambda n, ps: self.cp(STG1[0:R, n, :], ps[0:R, 0:WA], [ps], [STG1], eng="dve"))
        em.dma("pool", vout.rearrange("(n p) c -> p n c", p=R), STG1[0:R, 0:NB, :], STG1, dout["vp"])
        va4 = VAUG.v.rearrange("p n (h e) -> p n h e", h=NH)
        self.memset(VAUG, va4[0:R, 0:NB, :, 64:65], 1.0)
        self.cp(va4[0:R, 0:NB, :, 0:64], STG1[0:R, 0:NB, :].rearrange("p n (h e) -> p n h e", h=NH), [STG1], [VAUG])
        if not g.sample:
            em.dma("pool", self.vscr[l][t0:t0 + T, :].rearrange("(n p) c -> p n c", p=R), VAUG[0:R, 0:NB, :], VAUG,
                   self.vscr[l])
        ring, W = self.wnext(4)
        self.tm_proj(g, ring, W, 18, lambda n, ps: self.cp(self.SM[0:R, n, :], ps[0:R, 0:18], [ps], [self.SM]))
        LOGF = d["LOGF"]
        t6 = self.small[0:R, 32:32 + NB * NH].rearrange("p (n h) -> p n h", n=NB)
        self.tt(t6, self.SM[0:R, 0:NB, 0:6], self.par(l, "b_f", R).unsqueeze(1).to_broadcast([R, NB, NH]), ALU.add,
                [self.SM, psb], [self.small])
        self.act(t6, t6, AF.Exp, [self.small], [self.small], scale=-1.0)
        self.act(t6, t6, AF.Ln, [self.small, self.eps], [self.small], bias=self.eps[0:R, 2:3])
        self.ts(LOGF[0:R, 0:NB, :], t6, -1.0, None, ALU.mult, None, [self.small], [LOGF])
        lfout = dout["lfs"][l] if g.sample else dout["lfp"][l, t0:t0 + T, :]
        em.dma("pool", lfout.rearrange("(n p) h -> p n h", p=R), LOGF[0:R, 0:NB, :], LOGF, dout["lfp"])
        _stop("attnproj")
        if g.sample:
            self.attn_sample(g, l)
        else:
            self.attn_prompt(g, l, ti)
        _stop("attn")
        self.gdn(g, l, ti)
        _stop("gdn")
        self.spatial(g, l)
        _stop("spatial")
        if l == 0 and ti == 0:
            tg = "s" if g.sample else "p"
            self.dbg("mix" + tg, self.MIX, self.MIX[0:R, 0:NB, :])
        for n in range(NB):
            self.transp_mix(g, n)
        r0, W0 = self.wnext(10)
        r1, W1 = self.wnext(11)
        for n in range(NB):
            hal = []
            pss = []
            for hf, (rr, WW) in enumerate(((r0, W0), (r1, W1))):
                ps = self.ps()
                for ko in range(8):
                    self.mm(ps, ps[0:R, 0:512], self.FT[:, ko, n * R:(n + 1) * R], WW[:, ko, 0:512], ko == 0, ko == 7,
                            [rr, self.FT])
                hal.append(ps[0:R, 0:512])
                pss.append(ps)
            self.epilogue(g, n, hal, pss, self.gB)
        if l == 0 and ti == 0:
            self.dbg("x1" + tg, self.X, self.X[0:R, 0:NB, :])
        _stop("wout")
        em.dma("pool", self.gA[:, :], din["g_pre_ffn"][l].partition_broadcast(128), din["g_pre_ffn"], self.gA)
        em.dma("pool", self.gB[:, :], din["g_post_ffn"][l].partition_broadcast(128), din["g_post_ffn"], self.gB)
        self.norm_T(g, self.gA)
        h = self.SH
        self.switch(h.values())
        ACTT = h["ACTT"]
        for s in range(11):
            ring, W = self.wnext(12 + s)
            for jj in range(2):
                psg = self.ps()
                for ko in range(8):
                    self.mm(psg, psg[:, 0:T], W[:, jj, ko, :], self.FT[:, ko, 0:T], ko == 0, ko == 7, [ring, self.FT])
                psu = self.ps()
                for ko in range(8):
                    self.mm(psu, psu[:, 0:T], W[:, 2 + jj, ko, :], self.FT[:, ko, 0:T], ko == 0, ko == 7, [ring, self.FT])
                sg = h[f"SGT{jj}"]
                self.act(sg[:, 0:T], psg[:, 0:T], AF.Silu, [psg], [sg])
                self.tt(ACTT[:, 2 * s + jj, 0:T], sg[:, 0:T], psu[:, 0:T], ALU.mult, [sg, psu], [ACTT])
        YTOK = h["YTOK"]
        for oc in range(8):
            ring, W = self.wnext(23 + oc)
            ps = self.ps()
            for kt in range(22):
                self.mm(ps, ps[:, 0:T], W[:, kt, :], ACTT[:, kt, 0:T], kt == 0, kt == 21, [ring, ACTT])
            yt = h[f"YT{oc % 2}"]
            self.cp(yt[:, 0:T], ps[:, 0:T], [ps], [yt], eng="act")
            ps2 = self.ps()
            for n in range(NB):
                self.tr(ps2, ps2[0:R, n * 128:(n + 1) * 128], yt[:, n * R:(n + 1) * R], self.C("IDENT"), [yt, self.cst])
            self.cp(YTOK[0:R, 0:NB, oc * 128:(oc + 1) * 128], ps2[0:R, 0:NB * 128].rearrange("p (n c) -> p n c", n=NB),
                    [ps2], [YTOK])
        for n in range(NB):
            self.epilogue(g, n, [YTOK[0:R, n, 0:512], YTOK[0:R, n, 512:1024]], [YTOK], self.gB)
        if l == 0 and ti == 0:
            self.dbg("x2" + tg, self.X, self.X[0:R, 0:NB, :])

    def transp_mix(self, g, n):
        R = g.R
        ps = self.ps()
        psb = ps.v.bitcast(BF16)
        for k in range(8):
            self.tr(ps, psb[:, k * R:(k + 1) * R], self.MIX[0:R, n, k * 128:(k + 1) * 128], self.cstb[0:R, 0:R],
                    [self.MIX, self.cstb])
        self.cp(self.FT[:, 0:8, n * R:(n + 1) * R], psb[:, 0:8 * R].rearrange("p (k r) -> p k r", k=8), [ps], [self.FT],
                eng="act" if n % 2 else "dve")

    def cumsum_blocks(self, g, l, LOGF, blk0):
        R = g.R
        call, carry = self.call[l], self.carry[l]
        for n in range(g.NB):
            ps = self.ps()
            self.mm(ps, ps[0:R, 0:NH], self.C("UINCL", R, R), LOGF[0:R, n, :], True, False, [self.cst, LOGF])
            self.mm(ps, ps[0:R, 0:NH], self.C("ONES", 1, R), carry[0:1, :], False, True, [self.cst, carry])
            self.cp(call[0:R, blk0 + n, :], ps[0:R, 0:NH], [ps], [call])
            ps2 = self.ps()
            self.mm(ps2, ps2[0:1, 0:NH], self.C("ONES", R, 1), LOGF[0:R, n, :], True, False, [self.cst, LOGF])
            self.mm(ps2, ps2[0:1, 0:NH], self.C("ONES", 1, 1), carry[0:1, :], False, True, [self.cst, carry])
            self.cp(carry[0:1, :], ps2[0:1, 0:NH], [ps2], [carry])

    def attn_finish(self, g, l, OACC):
        d = self.SD
        R, NB = g.R, g.NB
        RC, OA = d["RC"], d["OA"]
        for qb in range(NB):
            o3 = OACC[qb].v[0:R, 0:NH * 65].rearrange("p (h e) -> p h e", h=NH)
            self.em.op("dve", lambda o3=o3, qb=qb: self.nc.vector.reciprocal(RC[0:R, qb, :].unsqueeze(2), o3[:, :, 64:65]),
                       [OACC[qb]], [RC])
            self.tt(OA[0:R, qb, :].rearrange("p (h e) -> p h e", h=NH), o3[:, :, 0:64],
                    RC[0:R, qb, :].unsqueeze(2).to_broadcast([R, NH, 64]), ALU.mult, [OACC[qb], RC], [OA])
        self.rms_rows(g, OA, OA, WA, self.par(l, "g_a_out", R), self.psb[l], self.MIX.v[:, :, 0:WA], self.MIX)

    def attn_prompt(self, g, l, qi):
        em, d = self.em, self.SD
        R, NB, T = g.R, g.NB, g.T
        QAT, KAT, VAUG, LOGF, BIAS, CREF = d["QAT"], d["KAT"], d["VAUG"], d["LOGF"], d["BIAS"], d["CREF"]
        call = self.call[l]
        blk0 = qi * NB
        self.cumsum_blocks(g, l, LOGF, blk0)
        ps = self.ps()
        self.mm(ps, ps[:, 0:NH], self.C("LAST"), call[:, blk0 + 1, :], True, True, [self.cst, call])
        self.cp(CREF[:, 0:NH], ps[:, 0:NH], [ps], [CREF])
        nkb = blk0 + NB
        self.tt(BIAS[:, 0:nkb, :], CREF[:, 0:NH].unsqueeze(1).to_broadcast([128, nkb, NH]), call[:, 0:nkb, :],
                ALU.subtract, [CREF, call], [BIAS])
        maskT = self.cstb[:, 128:256]
        OTH = self.OTH
        self.em.retire([d["STG0"], d["STG1"]], OTH)
        for h in range(NH):
            self.memset(OTH[h], OTH[h][0:65, :], 0.0, eng="pool")

        def load(kc):
            kt, vr = d[f"KTR{kc % 3}"], d[f"VR{kc % 3}"]
            em.dma("pool", kt[:, :, :], self.ktscr[l][:, :, kc * TT:(kc + 1) * TT], self.ktscr[l], kt)
            em.dma("pool", vr[:, :, :], self.vscr[l][kc * TT:(kc + 1) * TT, :].rearrange("(n p) c -> p n c", p=128),
                   self.vscr[l], vr)

        for kc in range(min(2, qi)):
            load(kc)
        pairs = [(kc, kb, pr) for kc in range(qi + 1) for kb in range(NB) for pr in range(3)]
        LAP = 1
        info = {}

        def emit_qk_pair(p):
            kc, kb, pr = pairs[p]
            diag = kc == qi
            if kb == 0 and pr == 0 and kc + 2 < qi:
                load(kc + 2)
            KT, VV = (KAT, VAUG) if diag else (d[f"KTR{kc % 3}"], d[f"VR{kc % 3}"])
            gk = kc * NB + kb
            q0 = kb * 128 if diag else 0
            N = T - q0
            pss = []
            for hh in range(2):
                rows = slice(hh * 64, hh * 64 + 64)
                ps = self.ps()
                self.mm(ps, ps[:, 0:N], KT[rows, pr, kb * 128:(kb + 1) * 128], QAT[rows, pr, q0:T], True, True, [KT, QAT])
                pss.append(ps)
            pts = []
            for hh in range(2):
                h = 2 * pr + hh
                pt = d[f"PT{(2 * p + hh) % 4}"]
                self.act(pt[:, 0:N], pss[hh][:, 0:N], AF.Exp, [pss[hh], BIAS], [pt], bias=BIAS[:, gk, h:h + 1], scale=SCALE)
                pts.append(pt)
            if diag:
                for hh in range(2):
                    self.tt(pts[hh][:, 0:128], pts[hh][:, 0:128], maskT, ALU.mult, [pts[hh], self.cstb], [pts[hh]])
            info[p] = (pts, VV, q0)

        def emit_pv_pair(p):
            kc, kb, pr = pairs[p]
            pts, VV, q0 = info.pop(p)
            N = T - q0
            pss = []
            for hh in range(2):
                h = 2 * pr + hh
                ps = self.ps()
                self.mm(ps, ps[0:65, 0:N], VV[:, kb, h * 65:(h + 1) * 65], pts[hh][:, 0:N], True, True, [pts[hh], VV])
                pss.append(ps)
            for hh in range(2):
                h = 2 * pr + hh
                self.tt(OTH[h][0:65, q0:T], OTH[h][0:65, q0:T], pss[hh][0:65, 0:N], ALU.add, [OTH[h], pss[hh]], [OTH[h]])

        n = len(pairs)
        for p in range(n + LAP):
            if p < n:
                emit_qk_pair(p)
            if p >= LAP:
                emit_pv_pair(p - LAP)
        OACC = []
        for qb in range(NB):
            ps = self.psum[qb]
            for h in range(NH):
                self.tr(ps, ps[:, h * 65:(h + 1) * 65], OTH[h][0:65, qb * 128:(qb + 1) * 128], self.C("IDENT", 65, 65),
                        [OTH[h], self.cst])
            OACC.append(ps)
        self.attn_finish(g, l, OACC)
        self.em.retire(OTH, [d["STG0"], d["STG1"]])

    def attn_sample(self, g, l):
        em, d, ds, din = self.em, self.SD, self.SDS, self.din
        R, P = g.R, self.P
        NKB = P // 128
        QAT, KAT, VAUG, LOGF, BIAS = d["QAT"], d["KAT"], d["VAUG"], d["LOGF"], d["BIAS"]
        CCW, CCT, CCP, CTOT = ds["CCW"], ds["CCT"], ds["CCP"], ds["CTOT"]
        KCF, VCF, KCB, KCT, VAUGC = ds["KCF"], ds["VCF"], ds["KCB"], ds["KCT"], ds["VAUGC"]
        self.ps_pool = [1, 2, 3, 4, 5, 6, 7]
        self._ptc = 0
        OACC = self.psum[0:1]
        o_started = [False] * NH
        vc4 = VAUGC.v.rearrange("p n (h e) -> p n h e", h=NH)
        self.memset(VAUGC, vc4[:, :, :, 64:65], 1.0)
        ptr = 0
        for s in range(NSEQ):
            em.dma("pool", CCP[:, 0:NKB, :], din["clf"][l, s].rearrange("(b p) h -> p b h", p=128), din["clf"], CCP)
            ps = self.ps()
            self.mm(ps, ps[:, 0:NKB * NH], self.C("UINCL"), CCP[:, 0:NKB, :], True, True, [self.cst, CCP])
            self.cp(CCW[:, 0:NKB, :], ps[:, 0:NKB * NH].rearrange("p (b h) -> p b h", h=NH), [ps], [CCW])
            ps = self.ps()
            self.mm(ps, ps[:, 0:NKB * NH], self.C("ONES"), CCP[:, 0:NKB, :], True, True, [self.cst, CCP])
            self.cp(CCT[:, 0:NKB, :], ps[:, 0:NKB * NH].rearrange("p (b h) -> p b h", h=NH), [ps], [CCT])
            self.memset(CCP, CCP[:, NKB - 1, :], 0.0, eng="dve")
            for b in range(NKB - 2, -1, -1):
                self.tt(CCP[:, b, :], CCP[:, b + 1, :], CCT[:, b + 1, :], ALU.add, [CCP, CCT], [CCP])
            self.tt(BIAS[:, 0:NKB, :], CCP[:, 0:NKB, :], CCT[:, 0:NKB, :], ALU.add, [CCP, CCT], [BIAS])
            self.tt(BIAS[:, 0:NKB, :], BIAS[:, 0:NKB, :], CCW[:, 0:NKB, :], ALU.subtract, [BIAS, CCW], [BIAS])
            for pc in range(NKB // 4):
                em.dma("pool", KCF[:, :, :], din["ck"][l, s, pc * 512:(pc + 1) * 512, :].rearrange("(b p) c -> p b c", p=128),
                       din["ck"], KCF)
                em.dma("pool", VCF[:, :, :], din["cv"][l, s, pc * 512:(pc + 1) * 512, :].rearrange("(b p) c -> p b c", p=128),
                       din["cv"], VCF)
                self.cp(KCB[:, :, :], KCF[:, :, :], [KCF], [KCB])
                self.cp(vc4[:, :, :, 0:64], VCF[:, :, :].rearrange("p n (h e) -> p n h e", h=NH), [VCF], [VAUGC], eng="act")
                for b in range(4):
                    ps = self.ps()
                    psb = ps.v.bitcast(BF16)
                    for pr in range(3):
                        self.tr(ps, psb[:, pr * 128:(pr + 1) * 128], KCB[:, b, pr * 128:(pr + 1) * 128], self.cstb[:, 0:128],
                                [KCB, self.cstb])
                    self.cp(KCT[:, :, b * 128:(b + 1) * 128], psb[:, 0:384].rearrange("p (k r) -> p k r", k=3), [ps], [KCT],
                            eng="act" if b % 2 else "dve")
                prs = [(b, pr) for b in range(4) for pr in range(3)]
                LAP = 2
                inf = {}

                def qk_pair(i, s=s, pc=pc):
                    b, pr = prs[i]
                    gk = pc * 4 + b
                    pss = []
                    for hh in range(2):
                        rows = slice(hh * 64, hh * 64 + 64)
                        ps = self.ps()
                        self.mm(ps, ps[:, 0:NS], KCT[rows, pr, b * 128:(b + 1) * 128], QAT[rows, pr, s * NS:(s + 1) * NS],
                                True, True, [KCT, QAT])
                        pss.append(ps)
                    pts = []
                    for hh in range(2):
                        h = 2 * pr + hh
                        pt = self.ptz[6 * s + (self._ptc % 6)]
                        self._ptc += 1
                        self.act(pt[:, s * NS:(s + 1) * NS], pss[hh][:, 0:NS], AF.Exp, [pss[hh], BIAS], [pt],
                                 bias=BIAS[:, gk, h:h + 1], scale=SCALE)
                        pts.append(pt)
                    inf[i] = pts

                def pv_pair(i):
                    b, pr = prs[i]
                    pts = inf.pop(i)
                    for hh in range(2):
                        h = 2 * pr + hh
                        self.mm(OACC[0], OACC[0][0:R, h * 65:(h + 1) * 65], pts[hh][:, 0:R], VAUGC[:, b, h * 65:(h + 1) * 65],
                                not any(o_started), False, [pts[hh], VAUGC], skip=True)
                        o_started[h] = True

                for i in range(len(prs) + LAP):
                    if i < len(prs):
                        qk_pair(i)
                    if i >= LAP:
                        pv_pair(i - LAP)
        ps = self.ps()
        self.mm(ps, ps[0:R, 0:NH], self.C("UINCL_S", R, R), LOGF[0:R, 0, :], True, True, [self.cst, LOGF])
        self.ts(BIAS[0:R, 0, :], ps[0:R, 0:NH], -1.0, None, ALU.mult, None, [ps], [BIAS])
        pt = d["PT0"]
        for h in range(NH):
            pr, hh = divmod(h, 2)
            rows = slice(hh * 64, hh * 64 + 64)
            ps = self.ps()
            self.mm(ps, ps[0:R, 0:R], KAT[rows, pr, 0:R], QAT[rows, pr, 0:R], True, True, [KAT, QAT])
            self.act(pt[0:R, h * 64:h * 64 + R], ps[0:R, 0:R], AF.Exp, [ps, BIAS], [pt], bias=BIAS[0:R, 0, h:h + 1], scale=SCALE)
            self.tt(pt[0:R, h * 64:h * 64 + R], pt[0:R, h * 64:h * 64 + R], self.cstb[0:R, 256:256 + R], ALU.mult,
                    [pt, self.cstb], [pt])
            self.mm(OACC[0], OACC[0][0:R, h * 65:(h + 1) * 65], pt[0:R, h * 64:h * 64 + R], VAUG[0:R, 0, h * 65:(h + 1) * 65],
                    False, False, [pt, VAUG], skip=True)
        self.attn_finish(g, l, OACC)
        self.ps_pool = list(range(8))

    def gdn(self, g, l, ti):
        em, din, dout, e = self.em, self.din, self.dout, self.SE
        R, NB, T, nseq = g.R, g.NB, g.T, g.nseq
        Ls = T // nseq
        psb = self.psb[l]
        front = [e[k] for k in ["ZC0", "ZC1", "Y0", "Y1", "SQ", "RS"]]
        blockt = [e[k] for k in ["GS3", "LBN3", "LBD3", "E3", "NB0", "NB1", "MB0", "MB1", "RV", "U", "UVSB"]]
        rest = [v for k, v in e.items() if v not in front and v not in blockt and k not in ("BZS", "SQO")]
        self.switch(front + rest)
        QBT, KBT, VBT, KTOK, VTOK, OTOK = e["QBT"], e["KBT"], e["VBT"], e["KTOK"], e["VTOK"], e["OTOK"]
        cw = self.par(l, "cw").rearrange("p (c i) -> p c i", c=9)
        CSN, CSP = e["CSN"], e["CSP"]
        if g.sample:
            for s in range(NSEQ):
                for r_ in range(3):
                    em.dma("pool", CSP[:, :, s, r_], din["sconv"][l, s, r_].rearrange("(c p) -> p c", p=128), din["sconv"], CSP,
                           allow_slow_non_contiguous=True)
        slot_of = [(5, 0), (5, 1), (5, 2), (5, 3), (6, 0), (6, 1), (6, 2), (6, 3), (7, 0)]
        ring = W = None
        for c in range(9):
            si, j = slot_of[c]
            if j == 0:
                ring, W = self.wnext(si)
            zc = e[f"ZC{c % 2}"].v[:, 0:nseq * (3 + Ls)].rearrange("p (s t) -> p s t", s=nseq)
            zcb = e[f"ZC{c % 2}"]
            y = e[f"Y{c % 2}"].v[:, 0:T].rearrange("p (s t) -> p s t", s=nseq)
            yb = e[f"Y{c % 2}"]
            if g.sample:
                self.cp(zc[:, :, 0:3], CSP[:, c, :, :], [CSP], [zcb])
            else:
                self.cp(zc[:, :, 0:3], self.cstate[l][:, c:c + 1, :], [self.cstate[l]], [zcb])
            self.fm_proj(g, ring, W, j, lambda ps: self.cp(zc[:, :, 3:3 + Ls], ps[:, 0:T].rearrange("p (s t) -> p s t", s=nseq),
                                                             [ps], [zcb], eng="act"))
            if g.sample:
                self.cp(CSN[:, c, :, :], zc[:, :, Ls:Ls + 3], [zcb], [CSN])
            else:
                self.cp(self.cstate[l][:, c:c + 1, :], zc[:, :, Ls:Ls + 3], [zcb], [self.cstate[l]])
            self.ts(y, zc[:, :, 0:Ls], cw[:, c, 0:1], None, ALU.mult, None, [zcb, psb], [yb])
            for i in range(1, 4):
                self.stt(y, zc[:, :, i:i + Ls], cw[:, c, i:i + 1], y, ALU.mult, ALU.add, [zcb, psb, yb], [yb])
            yf = yb[:, 0:T]
            self.act(yf, yf, AF.Silu, [yb], [yb])
            if c < 6:
                SQ, RS = e["SQ"], e["RS"]
                self.tt(SQ[:, 0:T], yf, yf, ALU.mult, [yb], [SQ])
                ps = self.ps()
                self.mm(ps, ps[:, 0:T], self.C("BLK64"), SQ[:, 0:T], True, True, [self.cst, SQ])
                self.rstd(RS[:, 0:T], ps[:, 0:T], 1.0, 0, [ps], [RS])
                if c < 3:
                    self.stt(QBT[:, c, 0:T], yf, SCALE, RS[:, 0:T], ALU.mult, ALU.mult, [yb, RS], [QBT])
                else:
                    self.tt(KBT[:, c - 3, 0:T], yf, RS[:, 0:T], ALU.mult, [yb, RS], [KBT])
            else:
                self.cp(VBT[:, c - 6, 0:T], yf, [yb], [VBT])
        if g.sample:
            for s in range(NSEQ):
                for r_ in range(3):
                    em.dma("pool", dout["convs"][l, s, r_].rearrange("(c p) -> p c", p=128), CSN[:, :, s, r_], CSN, dout["convs"],
                           allow_slow_non_contiguous=True)
        _stop("gdn_front")
        for n in range(NB):
            for src, dst in ((KBT, KTOK), (VBT, VTOK)):
                ps = self.ps()
                pb = ps.v.bitcast(BF16)
                for pr in range(3):
                    self.tr(ps, pb[0:R, pr * 128:(pr + 1) * 128], src[:, pr, n * R:(n + 1) * R], self.cstb[:, 0:128],
                            [src, self.cstb])
                self.cp(dst[0:R, n, :], pb[0:R, 0:WA], [ps], [dst], eng="act" if dst is VTOK else "dve")
        _stop("gdn_tr")
        GL, LB, BETA, T6A = e["GL"], e["LB"], e["BETA"], e["T6A"]
        bc = lambda nm: self.par(l, nm, R).unsqueeze(1).to_broadcast([R, NB, NH])
        self.tt(T6A[0:R, 0:NB, :], self.SM[0:R, 0:NB, 6:12], bc("dt_bias"), ALU.add, [self.SM, psb], [T6A])
        self.act(T6A[0:R, 0:NB, :], T6A[0:R, 0:NB, :], AF.Exp, [T6A], [T6A])
        self.act(T6A[0:R, 0:NB, :], T6A[0:R, 0:NB, :], AF.Ln, [T6A, self.eps], [T6A], bias=self.eps[0:R, 2:3])
        self.tt(GL[0:R, 0:NB, :], T6A[0:R, 0:NB, :], bc("nega"), ALU.mult, [T6A, psb], [GL])
        self.act(T6A[0:R, 0:NB, :], self.SM[0:R, 0:NB, 12:18], AF.Exp, [self.SM], [T6A], scale=-1.0)
        self.act(T6A[0:R, 0:NB, :], T6A[0:R, 0:NB, :], AF.Ln, [T6A, self.eps], [T6A], bias=self.eps[0:R, 2:3])
        self.ts(LB[0:R, 0:NB, :], T6A[0:R, 0:NB, :], -1.0, None, ALU.mult, None, [T6A], [LB])
        self.act(BETA[0:R, 0:NB, :], T6A[0:R, 0:NB, :], AF.Exp, [T6A], [BETA], scale=-1.0)
        _stop("gdn_scal")
        self.em.retire(front, blockt)
        self.cur_set = [b for b in self.cur_set if b not in front] + blockt
        self.memset(e["RKPAD"], e["RKPAD"][:, :, :, :], 0.0)
        self.memset(e["KDPAD"], e["KDPAD"][:, :, :, :], 0.0)
        for n in range(NB):
            self.gdn_block(g, l, n)
        if g.sample:
            for s in range(NSEQ):
                dst = dout["Ss"][l, s].rearrange("(pr hh d) e -> hh d pr e", hh=2, d=HD)
                for hh in range(2):
                    em.dma("pool", dst[hh], self.Sss[l][s][hh * 64:(hh + 1) * 64, :, :], self.Sss[l][s], dout["Ss"])
        BZS = e["BZS"]
        self.em.retire(blockt, [BZS, e["SQO"]])
        self.cur_set = [b for b in self.cur_set if b not in blockt] + [BZS, e["SQO"]]
        ring, W = self.wnext(8)
        self.tm_proj(g, ring, W, WA, lambda n, ps: self.act(BZS[0:R, n, :], ps[0:R, 0:WA], AF.Silu, [ps], [BZS]))
        RS6 = e["RS6"]
        o4 = OTOK.v[0:R, 0:NB, :].rearrange("p n (h e) -> p (n h) e", h=NH)
        SQO = e["SQO"]
        sqv = SQO.v[0:R, 0:NB, :].rearrange("p n (h e) -> p (n h) e", h=NH)
        self.tt(sqv, o4, o4, ALU.mult, [OTOK], [SQO])
        t6f = e["T6A"][0:R, 0:NB, :].rearrange("p n h -> p (n h)")
        self.em.op("dve", lambda: self.nc.vector.tensor_reduce(t6f, sqv, AX.X, ALU.add), [SQO], [e["T6A"]])
        self.rstd(RS6[0:R, 0:NB, :], e["T6A"][0:R, 0:NB, :], 1.0 / HD, 0, [e["T6A"]], [RS6])
        self.tt(o4, o4, RS6[0:R, 0:NB, :].rearrange("p n h -> p (n h)").unsqueeze(2).to_broadcast([R, NB * NH, HD]), ALU.mult,
                [OTOK, RS6], [OTOK])
        self.tt(o4, o4, self.par(l, "g_b_out", R).unsqueeze(1).to_broadcast([R, NB * NH, HD]), ALU.mult, [OTOK, psb], [OTOK])
        self.tt(self.MIX.v[0:R, 0:NB, WA:2 * WA], OTOK[0:R, 0:NB, :], BZS[0:R, 0:NB, :], ALU.mult, [OTOK, BZS], [self.MIX])

    def gdn_block(self, g, l, n):
        e = self.SE
        R, nseq, sample = g.R, g.nseq, g.sample
        sfx = "_S" if sample else ""
        UIN, NEGN, NEGM, NEGQ = (self.C(k + sfx, R, R) for k in ("UINCL", "NEGN", "NEGM", "NEGQ"))
        SEQ = self.C("SEQ_S", R, R) if sample else self.C("ONES", R, R)
        SG, IDENT, ONES = self.C("SG", R, R), self.C("IDENT", R, R), self.C("ONES", R, R)
        cst = self.cst
        GL, LB, BETA = e["GL"], e["LB"], e["BETA"]
        gl, lb, beta = GL[0:R, n, :], LB[0:R, n, :], BETA[0:R, n, :]
        GH, EG, BEG, EGL, GLS = e["GH"], e["EG"], e["BEG"], e["EGL"], e["GLS"]
        QBT, KBT, KTOK, VTOK, OTOK = e["QBT"], e["KBT"], e["KTOK"], e["VTOK"], e["OTOK"]
        cols = slice(n * R, (n + 1) * R)
        ps = self.ps()
        self.mm(ps, ps[0:R, 0:NH], UIN, gl, True, True, [cst, GL])
        self.mm(ps, ps[0:R, 8:8 + NH], SEQ, gl, True, True, [cst, GL])
        self.cp(GH[0:R, 0:16], ps[0:R, 0:16], [ps], [GH])
        self.act(EG[0:R, 0:NH], GH[0:R, 0:NH], AF.Exp, [GH], [EG])
        self.tt(BEG[0:R, 0:NH], EG[0:R, 0:NH], beta, ALU.mult, [EG, BETA], [BEG])
        self.tt(EGL[0:R, 0:NH], GH[0:R, 8:8 + NH], GH[0:R, 0:NH], ALU.subtract, [GH], [EGL])
        self.act(EGL[0:R, 0:NH], EGL[0:R, 0:NH], AF.Exp, [EGL], [EGL])
        gl2 = gl.rearrange("p (pr hh) -> p pr hh", hh=2)
        for s in range(nseq):
            ps = self.ps()
            lo = self.C(f"SELLO_S{s}", R, 128) if sample else self.C("HALF_LO", R, 128)
            hi = self.C(f"SELHI_S{s}", R, 128) if sample else self.C("HALF_HI", R, 128)
            self.mm(ps, ps[:, 0:3], lo, gl2[:, :, 0], True, False, [cst, GL])
            self.mm(ps, ps[:, 0:3], hi, gl2[:, :, 1], False, True, [cst, GL])
            self.act(GLS[:, s, 0:3], ps[:, 0:3], AF.Exp, [ps], [GLS])
        _stop("gb_gh")
        GI6 = e["GI6"]
        self.tt(GI6[0:R, :, 0:R], gl.unsqueeze(2).to_broadcast([R, NH, R]), UIN.unsqueeze(1).to_broadcast([R, NH, R]), ALU.mult,
                [GL, cst], [GI6])
        TTALL, QKTALL = e["TTALL"], e["QKTALL"]
        GS3, LBN3, LBD3, E3 = e["GS3"], e["LBN3"], e["LBD3"], e["E3"]
        NBs, MBs = [e["NB0"], e["NB1"]], [e["MB0"], e["MB1"]]
        identb = self.cstb[0:R, 0:R]
        p3 = lambda ps: ps[0:R, 0:3 * R].rearrange("p (h r) -> p h r", h=3)
        chains = [dict(NB0=e["NB0"], MB0=e["MB0"], NB1=e["NB1"], MB1=e["MB1"], CB3=e["CB3"], BB3=e["BB3"], QALL=e["QALL"]),
                  self.ch1]
        for hg in range(2):
            ch = chains[hg]
            hs = slice(3 * hg, 3 * hg + 3)
            b3 = lambda ap: ap.unsqueeze(2).to_broadcast([R, 3, R])
            m3 = lambda ap: ap.unsqueeze(1).to_broadcast([R, 3, R])
            self.tt(GS3[0:R, :, 0:R], b3(gl[:, hs]), m3(SG), ALU.mult, [GL, cst], [GS3])
            self.tt(LBN3[0:R, :, 0:R], b3(lb[:, hs]), m3(NEGN), ALU.add, [LB, cst], [LBN3])
            psK, psQ2 = self.ps(), self.ps()
            for hi_ in range(3):
                h = 3 * hg + hi_
                pr, hh = divmod(h, 2)
                rows = slice(hh * 64, hh * 64 + 64)
                self.mm(psK, psK[0:R, hi_ * R:(hi_ + 1) * R], KBT[rows, pr, cols], KBT[rows, pr, cols], True, True, [KBT])
                self.mm(psQ2, psQ2[0:R, hi_ * R:(hi_ + 1) * R], KBT[rows, pr, cols], QBT[rows, pr, cols], True, True, [KBT, QBT])
            psN = self.ps()
            self.mm(psN, p3(psN), UIN, GS3[0:R, :, 0:R], True, False, [cst, GS3])
            self.mm(psN, p3(psN), IDENT, LBN3[0:R, :, 0:R], False, True, [cst, LBN3])
            self.act(E3[0:R, :, 0:R], p3(psN), AF.Exp, [psN], [E3])
            self.stt(ch["NB0"][0:R, :, 0:R], E3[0:R, :, 0:R], -1.0, p3(psK), ALU.mult, ALU.mult, [E3, psK], [ch["NB0"]])
            psD = self.ps()
            self.mm(psD, p3(psD), SG, GI6[0:R, hs, 0:R], True, False, [cst, GI6])
            for hi_ in range(3):
                self.mm(psD, psD[0:R, hi_ * R:(hi_ + 1) * R], IDENT, NEGQ, False, hi_ == 2, [cst])
            self.act(E3[0:R, :, 0:R], p3(psD), AF.Exp, [psD], [E3])
            self.tt(QKTALL[0:R, hs, 0:R], E3[0:R, :, 0:R], p3(psQ2), ALU.mult, [E3, psQ2], [QKTALL])
        _stop("gb_nmq")
        m3b = lambda k: self.maskb[0:R, k, 0:R].unsqueeze(1).to_broadcast([R, 3, R])
        id3 = identb.unsqueeze(1).to_broadcast([R, 3, R])

        def masks(hg, lv):
            ch = chains[hg]
            self.tt(ch["NB1"][0:R, :, 0:R], ch["NB0"][0:R, :, 0:R], m3b(lv - 1), ALU.mult, [ch["NB0"], self.maskb], [ch["NB1"]],
                    eng="pool")

        def transp3(src_of, dst_ap, dstbuf, srcbufs, eng):
            ps = self.ps()
            pb = ps.v.bitcast(BF16)
            for hi_ in range(3):
                self.tr(ps, pb[0:R, hi_ * R:(hi_ + 1) * R], src_of(hi_), identb, srcbufs + [self.cstb])
            self.cp(dst_ap, pb[0:R, 0:3 * R].rearrange("p (h r) -> p h r", h=3), [ps], [dstbuf], eng=eng)

        for hg in range(2):
            masks(hg, 1)
        for hg in range(2):
            ch = chains[hg]
            self.tt(ch["QALL"][0:R, :, 0:R], ch["NB1"][0:R, :, 0:R], id3, ALU.add, [ch["NB1"], self.cstb], [ch["QALL"]])
        for hg in range(2):
            if g.lev >= 2:
                masks(hg, 2)
        for hg in range(2):
            ch = chains[hg]
            hs = slice(3 * hg, 3 * hg + 3)
            transp3(lambda hi_, ch=ch: ch["QALL"][0:R, hi_, 0:R], TTALL[0:R, hs, 0:R], TTALL, [ch["QALL"]], "act")
        for lv in range(2, g.lev + 1):
            pss = {}
            for hg in range(2):
                ch = chains[hg]
                psb_ = self.ps()
                for hi_ in range(3):
                    self.mm(psb_, psb_[0:R, hi_ * R:(hi_ + 1) * R], ch["NB1"][0:R, hi_, 0:R], TTALL[0:R, 3 * hg + hi_, 0:R], True, True,
                            [ch["NB1"], TTALL])
                pss[hg] = psb_
            for hg in range(2):
                ch = chains[hg]
                transp3(lambda hi_, hg=hg: TTALL[0:R, 3 * hg + hi_, 0:R], ch["QALL"][0:R, :, 0:R], ch["QALL"], [TTALL], "dve")
            for hg in range(2):
                if lv < g.lev:
                    masks(hg, lv + 1)
            for hg in range(2):
                ch = chains[hg]
                self.cp(ch["BB3"][0:R, :, 0:R], p3(pss[hg]), [pss[hg]], [ch["BB3"]], eng="act")
            for hg in range(2):
                ch = chains[hg]
                psp = self.ps()
                for hi_ in range(3):
                    self.mm(psp, psp[0:R, hi_ * R:(hi_ + 1) * R], ch["QALL"][0:R, hi_, 0:R], ch["BB3"][0:R, hi_, 0:R], True, True,
                            [ch["QALL"], ch["BB3"]])
                pss[hg] = psp
            for hg in range(2):
                hs = slice(3 * hg, 3 * hg + 3)
                Pv = TTALL[0:R, hs, 0:R]
                self.tt(Pv, Pv, p3(pss[hg]), ALU.add, [TTALL, pss[hg]], [TTALL])
        _stop("gb_inv")
        RV, U, UVSB, RKPAD, KDPAD, KDM = e["RV"], e["U"], e["UVSB"], e["RKPAD"], e["KDPAD"], e["KDM"]
        WT, WTM, QGT, QGTM, EGT = e["WT"], e["WTM"], e["QGT"], e["QGTM"], e["EGT"]
        self.tt(RV[0:R, :, :], VTOK[0:R, n, :].rearrange("p (h e) -> p h e", h=NH), beta.unsqueeze(2).to_broadcast([R, NH, HD]),
                ALU.mult, [VTOK, BETA], [RV])
        k4 = KTOK[0:R, n, :].rearrange("p (pr hh e) -> p pr hh e", pr=3, hh=2)
        for hh in range(2):
            sc = lambda t: t[0:R, 0:NH].rearrange("p (pr hh) -> p pr hh", hh=2)[:, :, hh].unsqueeze(2).to_broadcast([R, 3, HD])
            self.tt(RKPAD[0:R, :, hh, hh * 64:(hh + 1) * 64], k4[:, :, hh, :], sc(BEG), ALU.mult, [KTOK, BEG], [RKPAD])
            self.tt(KDPAD[0:R, :, hh, hh * 64:(hh + 1) * 64], k4[:, :, hh, :], sc(EGL), ALU.mult, [KTOK, EGL], [KDPAD])
        psU = self.ps()
        for h in range(NH):
            self.mm(psU, psU[0:R, h * HD:(h + 1) * HD], TTALL[0:R, h, 0:R], RV[0:R, h, :], True, True, [TTALL, RV])
        self.cp(UVSB[0:R, :], psU[0:R, 0:NH * HD], [psU], [UVSB], eng="act")
        psW = self.ps()
        for pr in range(3):
            for hh in range(2):
                self.mm(psW, psW[:, pr * R:(pr + 1) * R], RKPAD[0:R, pr, hh, :], TTALL[0:R, 2 * pr + hh, 0:R], hh == 0, hh == 1,
                        [RKPAD, TTALL])
        self.cp(WT[:, :, 0:R], psW[:, 0:3 * R].rearrange("p (k r) -> p k r", k=3), [psW], [WT])
        psE = self.ps()
        for pr in range(3):
            self.mm(psE, psE[:, pr * R:(pr + 1) * R], self.C("HALF_LO", R, 128), GI6[0:R, 2 * pr, 0:R], True, False, [cst, GI6])
            self.mm(psE, psE[:, pr * R:(pr + 1) * R], self.C("HALF_HI", R, 128), GI6[0:R, 2 * pr + 1, 0:R], False, True, [cst, GI6])
        self.act(EGT[:, :, 0:R], psE[:, 0:3 * R].rearrange("p (k r) -> p k r", k=3), AF.Exp, [psE], [EGT])
        self.tt(QGT[:, :, 0:R], QBT[:, :, cols], EGT[:, :, 0:R], ALU.mult, [QBT, EGT], [QGT])
        if sample:
            for s in range(NSEQ):
                cm = self.C("SEQCOL_S01" if s < 2 else "SEQCOL_S23")[:, (s % 2) * 64:(s % 2) * 64 + 64]
                cmb = cm.unsqueeze(1).to_broadcast([128, 3, 64])
                self.tt(WTM[:, s, :, :], WT[:, :, 0:R], cmb, ALU.mult, [WT, cst], [WTM])
                self.tt(QGTM[:, s, :, :], QGT[:, :, 0:R], cmb, ALU.mult, [QGT, cst], [QGTM])
        _stop("gb_rhs")
        Sf = self.Sss[l] if sample else [self.Sst[l]]
        Sb = self.Sssb[l] if sample else [self.Sstb[l]]
        if sample or n == 0:
            for s in range(nseq):
                self.cp(Sb[s][:, :, :], Sf[s][:, :, :], [Sf[s]], [Sb[s]])
        wt_of = (lambda s: WTM[:, s, :, :]) if sample else (lambda s: WT[:, :, 0:R])
        qg_of = (lambda s: QGTM[:, s, :, :]) if sample else (lambda s: QGT[:, :, 0:R])
        wtb, qgb = (WTM, QGTM) if sample else (WT, QGT)
        psWS = self.ps()
        for h in range(NH):
            pr, hh = divmod(h, 2)
            rows = slice(hh * 64, hh * 64 + 64)
            for s in range(nseq):
                self.mm(psWS, psWS[0:R, h * HD:(h + 1) * HD], wt_of(s)[rows, pr, :], Sb[s][rows, pr, :], s == 0, s == nseq - 1,
                        [wtb, Sb[s]])
        self.tt(U[0:R, :, :].rearrange("p h e -> p (h e)"), UVSB[0:R, :], psWS[0:R, 0:NH * HD], ALU.subtract, [UVSB, psWS], [U])
        psO = self.ps()
        for h in range(NH):
            pr, hh = divmod(h, 2)
            rows = slice(hh * 64, hh * 64 + 64)
            for s in range(nseq):
                self.mm(psO, psO[0:R, h * HD:(h + 1) * HD], qg_of(s)[rows, pr, :], Sb[s][rows, pr, :], s == 0, False, [qgb, Sb[s]])
            self.mm(psO, psO[0:R, h * HD:(h + 1) * HD], QKTALL[0:R, h, 0:R], U[0:R, h, :], False, True, [QKTALL, U])
        self.cp(OTOK[0:R, n, :], psO[0:R, 0:NH * HD], [psO], [OTOK], eng="act")
        for s in range(nseq):
            kd = KDPAD
            if sample:
                self.ts(KDM[0:R, :, :, :], KDPAD[0:R, :, :, :], self.C("ROWM_S", R, 128)[:, s:s + 1], None, ALU.mult, None,
                        [KDPAD, cst], [KDM])
                kd = KDM
            psS = self.ps()
            for pr in range(3):
                for hh in range(2):
                    self.mm(psS, psS[:, pr * HD:(pr + 1) * HD], kd[0:R, pr, hh, :], U[0:R, 2 * pr + hh, :], hh == 0, hh == 1, [kd, U])
            for pr in range(3):
                self.stt(Sf[s][:, pr, :], Sf[s][:, pr, :], GLS[:, s, pr:pr + 1], psS[:, pr * HD:(pr + 1) * HD], ALU.mult, ALU.add,
                         [Sf[s], GLS, psS], [Sf[s]])
            self.cp(Sb[s][:, :, :], Sf[s][:, :, :], [Sf[s]], [Sb[s]], eng="act")

    def spatial(self, g, l):
        em, f, dout = self.em, self.SF, self.dout
        R, NB = g.R, g.NB
        psb = self.psb[l]
        self.switch(f.values())
        CUV, T1, T2, VN, VNB, OC = f["CUV"], f["T1"], f["T2"], f["VN"], f["VNB"], f["OC"]
        ring, W = self.wnext(9)
        self.tm_proj(g, ring, W, 512, lambda n, ps: self.cp(CUV[0:R, n, :], ps[0:R, 0:512], [ps], [CUV],
                                                            eng="act" if n % 2 else "dve"))
        x, t = CUV[0:R, 0:NB, :], T1[0:R, 0:NB, :]
        self.tt(t, x, x, ALU.mult, [CUV], [T1])
        self.ts(t, t, 0.044715, 1.0, ALU.mult, ALU.add, [T1], [T1])
        self.tt(t, t, x, ALU.mult, [T1, CUV], [T1])
        self.act(t, t, AF.Sigmoid, [T1], [T1], scale=1.5957691216057308)
        self.tt(x, x, t, ALU.mult, [CUV, T1], [CUV])
        u, v = CUV[0:R, 0:NB, 0:256], CUV[0:R, 0:NB, 256:512]
        ss = self.small
        self.em.op("dve", lambda: self.nc.vector.tensor_reduce(ss[0:R, 40:40 + NB], v, AX.X, ALU.add), [CUV], [ss])
        self.ts(ss[0:R, 40:40 + NB], ss[0:R, 40:40 + NB], -1.0 / 256, None, ALU.mult, None, [ss], [ss])
        cen = T2[0:R, 0:NB, :]
        self.tt(cen, v, ss[0:R, 40:40 + NB].unsqueeze(2).to_broadcast([R, NB, 256]), ALU.add, [CUV, ss], [T2])
        sq = T1[0:R, 0:NB, 0:256]
        self.tt(sq, cen, cen, ALU.mult, [T2], [T1])
        self.em.op("dve", lambda: self.nc.vector.tensor_reduce(ss[0:R, 44:44 + NB], sq, AX.X, ALU.add), [T1], [ss])
        self.rstd(ss[0:R, 48:48 + NB], ss[0:R, 44:44 + NB], 1.0 / 256, 1, [ss], [ss])
        vn = VN[0:R, 0:NB, :]
        self.tt(vn, cen, ss[0:R, 48:48 + NB].unsqueeze(2).to_broadcast([R, NB, 256]), ALU.mult, [T2, ss], [VN])
        self.tt(vn, vn, self.par(l, "g_cv", R).unsqueeze(1).to_broadcast([R, NB, 256]), ALU.mult, [VN, psb], [VN])
        self.tt(vn, vn, self.par(l, "b_cv", R).unsqueeze(1).to_broadcast([R, NB, 256]), ALU.add, [VN, psb], [VN])
        if g.sample:
            em.dma("pool", dout["cvs"][l], VN[0:R, 0, :], VN, dout["cvs"])
        self.cp(VNB[0:R, 0:NB, :], vn, [VN], [VNB])
        wst = self.wsts[l] if g.sample else self.wst[l]
        bs = self.bst[l][0:R, 4:8] if g.sample else self.bst[l][0:R, 0:4]
        for n in range(NB):
            ps = self.ps()
            for gq in range(4):
                self.mm(ps, ps[0:R, gq * 64:(gq + 1) * 64], wst[0:R, gq, 0:R], VNB[0:R, n, gq * 64:(gq + 1) * 64], True, True,
                        [wst, VNB])
            o3 = OC[0:R, n, :].rearrange("p (g c) -> p g c", g=4)
            self.tt(o3, ps[0:R, 0:256].rearrange("p (g c) -> p g c", g=4), bs.unsqueeze(2).to_broadcast([R, 4, 64]), ALU.add,
                    [ps, self.bst[l]], [OC])
            self.tt(OC[0:R, n, :], OC[0:R, n, :], CUV[0:R, n, 0:256], ALU.mult, [OC, CUV], [OC])
        self.rms_rows(g, OC, OC, 256, self.par(l, "g_c_out", R), psb, self.MIX.v[:, :, 2 * WA:D], self.MIX, scol=52)

    def _ops(self):
        em, din, dout = self.em, self.din, self.dout
        S = self.S
        try:
            self._ops2()
        except StopBuild as ex:
            print("STOPPED AT", ex)
        em.finish()

    def _ops2(self):
        em, din, dout = self.em, self.din, self.dout
        S = self.S
        self.setup()
        _stop("setup")
        self.prepass()
        _stop("prepass")
        ntiles = S // TT
        tiles = [(False, ti) for ti in range(ntiles)] + [(True, 0)]
        ntl = [(t, l) for t in tiles for l in range(2)]
        self.wstream_init(ntl)
        self.em.retire(self.cur_set, self.FX)
        gp, gs = Seg(False), Seg(True)
        for (sample, ti) in tiles:
            g = gs if sample else gp
            R, NB = g.R, g.NB
            if sample:
                em.dma("pool", self.X[0:R, 0, :], din["xs"][:, :], din["xs"], self.X)
            else:
                em.dma("pool", self.X[:, :, :], din["xp"][ti * TT:(ti + 1) * TT, :].rearrange("(n p) d -> p n d", p=128),
                       din["xp"], self.X)
            for l in range(2):
                self.layer(g, l, ti)
            if sample:
                em.dma("pool", dout["ys"][:, :], self.X[0:R, 0, :], self.X, dout["ys"])
            else:
                em.dma("pool", dout["yp"][ti * TT:(ti + 1) * TT, :].rearrange("(n p) d -> p n d", p=128), self.X[:, :, :],
                       self.X, dout["yp"])
            if (not sample) and ti == ntiles - 1:
                for l in range(2):
                    for r_ in range(3):
                        em.dma("pool", dout["convp"][l, r_].rearrange("(c p) -> p c", p=128), self.cstate[l][:, :, r_],
                               self.cstate[l], dout["convp"], allow_slow_non_contiguous=True)
                    dst = dout["Sp"][l].rearrange("(pr hh d) e -> hh d pr e", hh=2, d=HD)
                    for hh in range(2):
                        em.dma("pool", dst[hh], self.Sst[l][hh * 64:(hh + 1) * 64, :, :], self.Sst[l], dout["Sp"])


_CACHE = {}


def kernel(x_prompt, x_sample, cache_a_k, cache_a_v, cache_a_logf, state_b_conv, state_b_S,
           g_pre_mix, w_in, b_f, conv_w, a_log, dt_bias, g_b_out, g_a_out, g_cv, b_cv,
           w_s, b_s, g_c_out, w_out, g_post_mix, g_pre_ffn, w_ffn_in, w_ffn_out, g_post_ffn, _dbg=()):
    f = lambda a: np.ascontiguousarray(np.asarray(a, dtype=np.float32))
    x_prompt, x_sample = f(x_prompt), f(x_sample)
    B, S, _ = x_prompt.shape
    DB, ns, _ = x_sample.shape
    P = cache_a_k.shape[2]
    assert ns == NS and DB == NSEQ * B and S % TT == 0 and P % 512 == 0
    key = (S, P, tuple(_dbg))
    if key not in _CACHE:
        _CACHE[key] = Builder(S, P, dbg=_dbg)
    bld = _CACHE[key]
    cst = make_consts()
    shared = dict(g_pre_mix=f(g_pre_mix), w_in=f(w_in), b_f=f(b_f), conv_w=f(conv_w), a_log=f(a_log), dt_bias=f(dt_bias),
                  g_b_out=f(g_b_out), g_a_out=f(g_a_out), g_cv=f(g_cv), b_cv=f(b_cv), w_s=f(w_s), b_s=f(b_s),
                  g_c_out=f(g_c_out), w_out=f(w_out), g_post_mix=f(g_post_mix), g_pre_ffn=f(g_pre_ffn),
                  w_ffn_in=f(w_ffn_in), w_ffn_out=f(w_ffn_out), g_post_ffn=f(g_post_ffn), cst=cst, cstm=make_masks())
    ck, cvv, clf = f(cache_a_k), f(cache_a_v), f(cache_a_logf)
    sc, sS = f(state_b_conv), f(state_b_S)
    in_maps = []
    for c in range(B):
        sl = slice(NSEQ * c, NSEQ * (c + 1))
        m = dict(shared)
        m["xp"] = x_prompt[c]
        m["xs"] = x_sample[sl].reshape(NSEQ * NS, D)
        m["ck"] = np.ascontiguousarray(ck[:, sl].reshape(2, NSEQ, P, WA))
        m["cv"] = np.ascontiguousarray(cvv[:, sl].reshape(2, NSEQ, P, WA))
        m["clf"] = np.ascontiguousarray(clf[:, sl])
        m["sconv"] = np.ascontiguousarray(sc[:, sl])
        m["sS"] = np.ascontiguousarray(sS[:, sl])
        in_maps.append(m)
    res = run_bass_kernel_spmd(bld.nc, in_maps, core_ids=list(range(B)))
    r = res.results
    cat = lambda k, ax: np.stack([r[c][k] for c in range(B)], axis=ax)
    yp = cat("yp", 0)
    ys = cat("ys", 0).reshape(DB, NS, D)
    kp = cat("kp", 1).reshape(2, B, S, NH, HD)
    vp = cat("vp", 1).reshape(2, B, S, NH, HD)
    lfp = cat("lfp", 1)
    convp = cat("convp", 1)
    Sp = cat("Sp", 1).reshape(2, B, NH, HD, HD)
    ks = cat("ks", 1).reshape(2, DB, NS, NH, HD)
    vs = cat("vs", 1).reshape(2, DB, NS, NH, HD)
    lfs = cat("lfs", 1).reshape(2, DB, NS, NH)
    convs = cat("convs", 1).reshape(2, DB, 3, 1152)
    Ss = cat("Ss", 1).reshape(2, DB, NH, HD, HD)
    cvs = cat("cvs", 1).reshape(2, DB, NS, 256)
    outs = (yp, ys, kp, vp, lfp, convp, Sp, ks, vs, lfs, convs, Ss, cvs)
    if _dbg:
        return outs, [{k: r[c]["dbg_" + k] for k in bld.dbg_out} for c in range(B)]
    return outs
```

```python
import numpy as np
import concourse.bass as bass
import concourse.mybir as mybir
from concourse.bass_utils import run_bass_kernel_spmd

F32 = mybir.dt.float32
BF16 = mybir.dt.bfloat16
ALU = mybir.AluOpType
AF = mybir.ActivationFunctionType
AX = mybir.AxisListType
EPOCH = 30000

D = 1024
HD = 64
NH = 6
WA = 384
DFF = 2816
DIN = 3218
NS = 16
NSEQ = 4
SCALE = HD ** -0.5
BIG = 30000.0
TT = 512


class Buf:
    def __init__(self, name, v, track=True):
        self.name = name
        self.v = v
        self.track = track
        self.lw = None
        self.rd = {}
        self.dws = None
        self.dwc = 0
        self.drs = None
        self.drc = 0
        self.dr_pending = False
        self.pre = []
        self.via = []
        self.rtrack = True
        self.excl = False

    def __getitem__(self, k):
        return self.v[k]


class Em:
    def __init__(self, nc):
        self.nc = nc
        self.eng = {"pe": nc.tensor, "act": nc.scalar, "dve": nc.vector,
                    "pool": nc.gpsimd, "sp": nc.sync}
        self.cnt = {e: 0 for e in self.eng}
        self.esems = {e: [] for e in self.eng}
        self.seen = {e: {} for e in self.eng}
        self.bufs = []
        self.nsem = 0
        self.ninst = 0
        self.sem_pool = {}

    def sem(self, name):
        self.nsem += 1
        return self.nc.alloc_semaphore(name)

    def reg(self, b):
        self.bufs.append(b)
        return b

    def sb(self, name, shape, dtype=F32):
        t = self.nc.alloc_sbuf_tensor(name, list(shape), dtype)
        return self.reg(Buf(name, t.ap()))

    def dram(self, name, shape, dtype, kind="Internal", track=True):
        t = self.nc.dram_tensor(name, list(shape), dtype, kind=kind)
        b = self.reg(Buf(name, t.ap(), track=track))
        b.rtrack = False
        return b

    def _esem(self, e, seq):
        k = (seq - 1) // EPOCH
        while len(self.esems[e]) <= k:
            self.esems[e].append(self.sem(f"E_{e}_{len(self.esems[e])}"))
        return (e, k), self.esems[e][k], seq - k * EPOCH

    @staticmethod
    def _need(waits, key, sem, val):
        if key not in waits or waits[key][1] < val:
            waits[key] = (sem, val)

    def _collect(self, e, reads, writes, is_dma):
        waits = {}

        def last_write(b, is_write):
            lw = b.lw
            if lw is None:
                return
            if lw[0] == "c":
                _, f, n = lw
                if f == e and not is_dma and e == "pe":
                    return
                key, s, v = self._esem(f, n)
                self._need(waits, key, s, v)
            elif lw[0] == "v":
                if is_dma and is_write:
                    return
                for sb_ in b.via:
                    self._need(waits, ("dr", id(sb_)), sb_.drs, 16 * sb_.drc)
            else:
                if is_dma and is_write:
                    return
                self._need(waits, ("dw", id(b)), b.dws, 16 * b.dwc)

        for b in reads:
            if b.track:
                last_write(b, False)
                if b.excl:
                    for f, n in b.rd.items():
                        if f != e:
                            key, s, v = self._esem(f, n)
                            self._need(waits, key, s, v)
        for b in writes:
            if not b.track:
                continue
            last_write(b, True)
            for f, n in b.rd.items():
                if f == e and not is_dma and e == "pe":
                    continue
                key, s, v = self._esem(f, n)
                self._need(waits, key, s, v)
            if b.dr_pending:
                self._need(waits, ("dr", id(b)), b.drs, 16 * b.drc)
            for key, s, v in b.pre:
                self._need(waits, key, s, v)
            b.pre = []
        out = []
        seen = self.seen[e]
        for key, (s, v) in waits.items():
            if seen.get(key, 0) >= v:
                continue
            seen[key] = v
            out.append((s, v))
        return out

    def op(self, e, make, reads=(), writes=(), after=()):
        waits = self._collect(e, reads, writes, False)
        for (f, n) in after:
            key, s_, v_ = self._esem(f, n)
            if self.seen[e].get(key, 0) < v_:
                self.seen[e][key] = v_
                waits.append((s_, v_))
        eng = self.eng[e]
        for s, v in waits[:-1]:
            eng.wait_ge(s, v)
        ins = make()
        if waits:
            ins._wait_ge(waits[-1][0], waits[-1][1])
        self.cnt[e] += 1
        seq = self.cnt[e]
        _, s, v = self._esem(e, seq)
        ins.then_inc(s, 1)
        for b in reads:
            if b.track:
                b.rd[e] = seq
        for b in writes:
            if b.track:
                b.lw = ("c", e, seq)
                b.rd = {}
                b.dr_pending = False
        self.ninst += 1
        return ins

    def dma(self, q, out_ap, in_ap, src, dst, **kw):
        waits = self._collect(q, [src], [dst], True)
        eng = self.eng[q]
        for s, v in waits[:-1]:
            eng.wait_ge(s, v)
        ins = eng.dma_start(out=out_ap, in_=in_ap, **kw)
        if waits:
            ins._wait_ge(waits[-1][0], waits[-1][1])
        srct = src.track and src.rtrack
        if srct:
            if src.drs is None:
                src.drs = self.sem(f"dr_{src.name}")
            src.drc += 1
            src.dr_pending = True
            ins.then_inc(src.drs, 16)
        if dst.track and srct:
            if src not in dst.via:
                dst.via.append(src)
            dst.lw = ("v",)
            dst.rd = {}
            dst.dr_pending = False
        elif dst.track:
            if dst.dws is None:
                dst.dws = self.sem(f"dw_{dst.name}")
            dst.dwc += 1
            dst.lw = ("d",)
            dst.rd = {}
            dst.dr_pending = False
            ins.then_inc(dst.dws, 16)
        self.ninst += 1
        return ins

    def retire(self, old, new):
        pre = {}
        for b in old:
            if b.lw is not None:
                if b.lw[0] == "c":
                    key, s, v = self._esem(b.lw[1], b.lw[2])
                    self._need(pre, key, s, v)
                elif b.lw[0] == "v":
                    for sb_ in b.via:
                        self._need(pre, ("dr", id(sb_)), sb_.drs, 16 * sb_.drc)
                else:
                    self._need(pre, ("dw", id(b)), b.dws, 16 * b.dwc)
            for f, n in b.rd.items():
                key, s, v = self._esem(f, n)
                self._need(pre, key, s, v)
            if b.dr_pending:
                self._need(pre, ("dr", id(b)), b.drs, 16 * b.drc)
            for key, s, v in b.pre:
                self._need(pre, key, s, v)
        lst = [(k, s, v) for k, (s, v) in pre.items()]
        for b in new:
            b.lw = None
            b.rd = {}
            b.dr_pending = False
            b.pre = list(lst)

    def finish(self):
        sp = self.nc.sync
        for b in self.bufs:
            if b.drs is not None and b.drc:
                sp.wait_ge(b.drs, 16 * b.drc)
            if b.dws is not None and b.dwc:
                sp.wait_ge(b.dws, 16 * b.dwc)
        for e in self.eng:
            if self.cnt[e]:
                _, s, v = self._esem(e, self.cnt[e])
                sp.wait_ge(s, v)


CNAMES = ["IDENT", "ONES", "UINCL", "SG", "NEGN", "NEGM", "NEGQ", "LAST", "BLK64",
          "MASKC", "UINCL_S", "NEGN_S", "NEGM_S", "NEGQ_S", "SEQ_S", "ROWM_S",
          "HALF_LO", "HALF_HI", "SELLO_S0", "SELLO_S1", "SELLO_S2", "SELLO_S3",
          "SELHI_S0", "SELHI_S1", "SELHI_S2", "SELHI_S3", "SEQCOL_S01", "SEQCOL_S23"]


def make_consts():
    i = np.arange(128)
    r = i[:, None]
    c = i[None, :]
    t = {}
    t["IDENT"] = (r == c)
    t["ONES"] = np.ones((128, 128), bool)
    t["UINCL"] = (r <= c)
    t["SG"] = (r > c)
    t["NEGN"] = np.where(c < r, 0.0, -BIG)
    t["NEGM"] = np.where(r < c, 0.0, -BIG)
    t["NEGQ"] = np.where(r <= c, 0.0, -BIG)
    t["LAST"] = (r == 127) & (c >= 0)
    t["BLK64"] = (r // 64 == c // 64)
    t["MASKC"] = (c // 64 <= r // 64)
    same = (r // NS == c // NS) & (r < 64) & (c < 64)
    t["UINCL_S"] = same & (r <= c)
    t["NEGN_S"] = np.where(same & (c < r), 0.0, -BIG)
    t["NEGM_S"] = np.where(same & (r < c), 0.0, -BIG)
    t["NEGQ_S"] = np.where(same & (r <= c), 0.0, -BIG)
    t["SEQ_S"] = same
    rowm = np.zeros((128, 128), np.float32)
    for s in range(NSEQ):
        rowm[s * NS:(s + 1) * NS, s] = 1.0
    t["ROWM_S"] = rowm
    t["HALF_LO"] = (c < 64) & (r >= 0)
    t["HALF_HI"] = (c >= 64) & (r >= 0)
    for s in range(NSEQ):
        inseq = (r // NS == s) & (r < 64)
        t[f"SELLO_S{s}"] = inseq & (c < 64)
        t[f"SELHI_S{s}"] = inseq & (c >= 64)
    t["SEQCOL_S01"] = ((c % 64) // NS == c // 64) & (r >= 0)
    t["SEQCOL_S23"] = ((c % 64) // NS == 2 + c // 64) & (r >= 0)
    return np.concatenate([np.asarray(t[k], np.float32) for k in CNAMES], axis=1)


C_AQ, C_AK, C_AV, C_AF, C_BQKV, C_BA, C_BB, C_BZ, C_CU, C_CV = 0, 384, 768, 1152, 1158, 2310, 2316, 2322, 2706, 2962


def slot_plan():
    slots = []

    def fm(wname, cols):
        pcs = [(j * 1024, wname, 8, c0, 128) for j, c0 in enumerate(cols)]
        slots.append(dict(kind="fm", n=len(cols), used=len(cols) * 1024, pieces=pcs))

    def tm(wname, segs, kt=8):
        ncols = sum(n for _, n in segs)
        pcs, off = [], 0
        for c0, n in segs:
            pcs.append((off, wname, kt, c0, n, ncols))
            off += n
        slots.append(dict(kind="tm", ncols=ncols, kt=kt, used=kt * ncols, pieces=pcs))

    fm("w_in", [C_AQ, C_AQ + 128, C_AQ + 256, C_AK])
    fm("w_in", [C_AK + 128, C_AK + 256])
    tm("w_in", [(C_AK, 384)])
    tm("w_in", [(C_AV, 384)])
    tm("w_in", [(C_AF, 6), (C_BA, 12)])
    fm("w_in", [C_BQKV + 128 * j for j in range(0, 4)])
    fm("w_in", [C_BQKV + 128 * j for j in range(4, 8)])
    fm("w_in", [C_BQKV + 128 * 8])
    tm("w_in", [(C_BZ, 384)])
    tm("w_in", [(C_CU, 512)])
    tm("w_out", [(0, 512)])
    tm("w_out", [(512, 512)])
    for s in range(11):
        fm("w_ffn_in", [256 * s, 256 * s + 128, DFF + 256 * s, DFF + 256 * s + 128])
    for oc in range(8):
        slots.append(dict(kind="fo", used=22 * 128, pieces=[(0, "w_ffn_out", 22, oc * 128, 128, 128)]))
    return slots


def make_masks():
    i = np.arange(128)[:, None]
    j = np.arange(128)[None, :]
    ms = []
    for lv in range(1, 8):
        ms.append(((i >> lv) == (j >> lv)) & ((i >> (lv - 1)) != (j >> (lv - 1))) & (j < i))
    ms = ms + [m.T for m in ms]
    return np.concatenate([m.astype(np.float32) for m in ms], axis=1)


SLOTS = slot_plan()
NSLOT_L = len(SLOTS)
SLOT_E = 4096


IN_SPECS = lambda S, P: [
    ("xp", [S, D]), ("xs", [NSEQ * NS, D]),
    ("ck", [2, NSEQ, P, WA]), ("cv", [2, NSEQ, P, WA]), ("clf", [2, NSEQ, P, NH]),
    ("sconv", [2, NSEQ, 3, 1152]), ("sS", [2, NSEQ, NH, HD, HD]),
    ("g_pre_mix", [2, D]), ("w_in", [2, D, DIN]), ("b_f", [2, NH]), ("conv_w", [2, 4, 1152]),
    ("a_log", [2, NH]), ("dt_bias", [2, NH]), ("g_b_out", [2, HD]), ("g_a_out", [2, WA]),
    ("g_cv", [2, 256]), ("b_cv", [2, 256]), ("w_s", [2, 4, 128, 128]), ("b_s", [2, 4, 128]),
    ("g_c_out", [2, 256]), ("w_out", [2, D, D]), ("g_post_mix", [2, D]), ("g_pre_ffn", [2, D]),
    ("w_ffn_in", [2, D, 2 * DFF]), ("w_ffn_out", [2, DFF, D]), ("g_post_ffn", [2, D]),
    ("cst", [128, 128 * len(CNAMES)]), ("cstm", [128, 14 * 128]),
]
OUT_SPECS = lambda S: [
    ("yp", [S, D]), ("ys", [NSEQ * NS, D]),
    ("kp", [2, S, WA]), ("vp", [2, S, WA]), ("lfp", [2, S, NH]), ("convp", [2, 3, 1152]),
    ("Sp", [2, NH * HD, HD]),
    ("ks", [2, NSEQ * NS, WA]), ("vs", [2, NSEQ * NS, WA]), ("lfs", [2, NSEQ * NS, NH]),
    ("convs", [2, NSEQ, 3, 1152]), ("Ss", [2, NSEQ, NH * HD, HD]), ("cvs", [2, NSEQ * NS, 256]),
]


import os as _os


class StopBuild(Exception):
    pass


def _stop(tag):
    if _os.environ.get("KSTOP") == tag:
        raise StopBuild(tag)


class Seg:
    def __init__(self, sample):
        self.sample = sample
        if sample:
            self.R, self.NB, self.nseq, self.L, self.lev = 64, 1, NSEQ, NS, 4
        else:
            self.R, self.NB, self.nseq, self.L, self.lev = 128, TT // 128, 1, 128, 7
        self.T = self.R * self.NB


class Builder:
    def __init__(self, S, P, dbg=()):
        self.S, self.P = S, P
        self.dbg_names = list(dbg)
        nc = self.nc = bass.Bass("TRN2", target_bir_lowering=False)
        em = self.em = Em(nc)
        self.din = {n: em.dram(n, sh, F32, kind="ExternalInput", track=False) for n, sh in IN_SPECS(S, P)}
        self.dout = {n: em.dram(n, sh, F32, kind="ExternalOutput", track=False) for n, sh in OUT_SPECS(S)}
        self.dbg_out = {}
        self.wscr = em.dram("wscr", [2, NSLOT_L, 128, SLOT_E], BF16)
        self.ktscr = [em.dram(f"ktscr{l}", [128, 3, S], BF16) for l in range(2)]
        self.vscr = [em.dram(f"vscr{l}", [S, NH * 65], BF16) for l in range(2)]
        self._alloc()
        self._ops()

    def carve(self, name, off, shape, dtype, parts=128):
        esz = 2 if dtype == BF16 else 4
        n = int(np.prod(shape[1:]))
        nbytes = n * esz
        assert off % 4 == 0 and off + nbytes <= self.ARENA, (name, off, nbytes)
        v = self.arena_ap[:, off // 4:(off + nbytes + 3) // 4]
        if dtype == BF16:
            v = v.bitcast(BF16)[:, 0:n]
        if len(shape) > 2:
            names = " ".join(f"d{i}" for i in range(len(shape) - 1))
            kw = {f"d{i}": shape[i + 1] for i in range(len(shape) - 1)}
            v = v.rearrange(f"p ({names}) -> p {names}", **kw)
        if shape[0] < 128:
            v = v[0:shape[0]]
        return self.em.reg(Buf(name, v))

    def _alloc(self):
        em, nc = self.em, self.nc
        self.ARENA = 77 * 1024
        self.arena_ap = nc.alloc_sbuf_tensor("arena", [128, self.ARENA // 4], F32).ap()
        self.X = em.sb("X", [128, TT // 128, D])
        self.ring = [em.sb(f"ring{i}", [128, SLOT_E], BF16) for i in range(4)]
        self.cst = em.sb("cst_sb", [128, 128 * len(CNAMES)])
        self.cstb = em.sb("cstb", [128, 4 * 128], BF16)
        self.maskb = em.sb("maskb", [128, 14, 128], BF16)
        self.gA = em.sb("gA", [128, D])
        self.gB = em.sb("gB", [128, D])
        self.msb = [em.sb(f"msb{i}", [128, D]) for i in range(2)]
        self.junks = [em.sb(f"junk{i}", [128, D], BF16) for i in range(4)]
        self.junk_rr = 0
        self.psb = [em.sb(f"par{l}", [128, 1408]) for l in range(2)]
        self.wst = [em.sb(f"wst{l}", [128, 4, 128], BF16) for l in range(2)]
        self.wsts = [em.sb(f"wsts{l}", [128, 4, 64], BF16) for l in range(2)]
        self.bst = [em.sb(f"bst{l}", [128, 8]) for l in range(2)]
        self.call = [em.sb(f"call{l}", [128, max(self.S // 128, 1), NH]) for l in range(2)]
        self.carry = [em.sb(f"carry{l}", [1, NH]) for l in range(2)]
        self.Sst = [em.sb(f"Sst{l}", [128, 3, HD]) for l in range(2)]
        self.Sstb = [em.sb(f"Sstb{l}", [128, 3, HD], BF16) for l in range(2)]
        self.Sss = [[em.sb(f"Sss{l}_{s}", [128, 3, HD]) for s in range(NSEQ)] for l in range(2)]
        self.Sssb = [[em.sb(f"Sssb{l}_{s}", [128, 3, HD], BF16) for s in range(NSEQ)] for l in range(2)]
        self.cstate = [em.sb(f"cstate{l}", [128, 9, 3]) for l in range(2)]
        self.small = em.sb("small", [128, 64])
        self.eps = em.sb("eps", [128, 4])
        self.ptz = [em.sb(f"ptz{i}", [128, 64], BF16) for i in range(6 * NSEQ)]
        self.ch1 = {nm: em.sb("c1_" + nm, [128, 3, 128], BF16) for nm in ["NB0", "MB0", "NB1", "MB1", "CB3", "BB3", "QALL"]}
        self.psum = []
        for i in range(8):
            t = nc.alloc_psum_tensor(f"ps{i}", [128, 512], F32)
            self.psum.append(em.reg(Buf(f"ps{i}", t.ap())))
            self.psum[-1].excl = True
        self.ps_rr = 0
        self.ps_pool = list(range(8))
        cv = self.carve
        self.FT = cv("FT", 0, [128, 8, TT], BF16)
        self.HB = [cv("HB0", 8192, [128, D], BF16), cv("HB1", 10240, [128, D], BF16)]
        self.MIX = cv("MIX", 12288, [128, TT // 128, D], BF16)
        self.SM = cv("SM", 20480, [128, TT // 128, 18], F32)
        self.FX = [self.FT, self.HB[0], self.HB[1], self.MIX, self.SM]
        P0 = 21504
        NKB = max(self.S // 128, self.P // 128, 1)
        assert NKB * 24 <= 1536
        d = self.SD = {}
        d["QAT"] = cv("QAT", P0 + 0, [128, 3, TT], BF16)
        d["KAT"] = cv("KAT", P0 + 3072, [128, 3, TT], BF16)
        d["VAUG"] = cv("VAUG", P0 + 6144, [128, 4, NH * 65], BF16)
        d["STG0"] = cv("STG0", P0 + 9280, [128, 4, WA], F32)
        d["STG1"] = cv("STG1", P0 + 15424, [128, 4, WA], F32)
        self.OTH = [cv(f"OTH{h}", P0 + 9280 + 2048 * h, [128, TT], F32) for h in range(NH)]
        d["LOGF"] = cv("LOGF", P0 + 21568, [128, 4, NH], F32)
        d["BIAS"] = cv("BIAS", P0 + 21696, [128, NKB, NH], F32)
        d["CREF"] = cv("CREF", P0 + 23232, [128, 8], F32)
        d["RC"] = cv("RC", P0 + 23296, [128, 4, NH], F32)
        for i in range(3):
            d[f"KTR{i}"] = cv(f"KTR{i}", P0 + 23424 + 3072 * i, [128, 3, TT], BF16)
            d[f"VR{i}"] = cv(f"VR{i}", P0 + 32640 + 3136 * i, [128, 4, NH * 65], BF16)
        for i in range(4):
            d[f"PT{i}"] = cv(f"PT{i}", P0 + 42048 + 1024 * i, [128, TT], BF16)
        d["OA"] = cv("OA", P0 + 46144, [128, 4, WA], F32)
        ds = self.SDS = {}
        o = P0 + 23424
        ds["KCF"] = cv("KCF", o, [128, 4, WA], F32); o += 6144
        ds["VCF"] = cv("VCF", o, [128, 4, WA], F32); o += 6144
        ds["KCB"] = cv("KCB", o, [128, 4, WA], BF16); o += 3072
        ds["KCT"] = cv("KCT", o, [128, 3, 512], BF16); o += 3072
        ds["VAUGC"] = cv("VAUGC", o, [128, 4, NH * 65], BF16); o += 3136
        assert o <= P0 + 46144
        o = P0 + 52288
        ds["CCW"] = cv("CCW", o, [128, NKB, NH], F32); o += 768 * 2
        ds["CCT"] = cv("CCT", o, [128, NKB, NH], F32); o += 768 * 2
        ds["CCP"] = cv("CCP", o, [128, NKB, NH], F32); o += 768 * 2
        ds["CTOT"] = cv("CTOT", o, [128, NSEQ, NH], F32); o += 128
        assert o <= self.ARENA
        e = self.SE = {}
        o = P0
        e["ZC0"] = cv("ZC0", o, [128, 520], F32); o += 2080
        e["ZC1"] = cv("ZC1", o, [128, 520], F32); o += 2080
        e["Y0"] = cv("Y0", o, [128, TT], F32); o += 2048
        e["Y1"] = cv("Y1", o, [128, TT], F32); o += 2048
        e["SQ"] = cv("SQ", o, [128, TT], F32); o += 2048
        e["RS"] = cv("RS", o, [128, TT], F32); o += 2048
        assert o == P0 + 12352
        o = P0
        for nm in ["GS3", "LBN3", "LBD3", "E3"]:
            e[nm] = cv(nm, o, [128, 3, 128], F32); o += 1536
        for nm in ["NB0", "NB1", "MB0", "MB1"]:
            e[nm] = cv(nm, o, [128, 3, 128], BF16); o += 768
        e["RV"] = cv("RV", o, [128, NH, HD], BF16); o += 768
        e["U"] = cv("U", o, [128, NH, HD], BF16); o += 768
        e["UVSB"] = cv("UVSB", o, [128, NH * HD], F32); o += 1536
        assert o <= P0 + 12352
        e["BZS"] = cv("BZS", P0, [128, 4, WA], F32)
        e["SQO"] = cv("SQO", P0 + 6144, [128, 4, WA], F32)
        o = P0 + 12352
        for nm in ["QBT", "KBT", "VBT"]:
            e[nm] = cv(nm, o, [128, 3, TT], BF16); o += 3072
        for nm in ["KTOK", "VTOK"]:
            e[nm] = cv(nm, o, [128, 4, WA], BF16); o += 3072
        e["OTOK"] = cv("OTOK", o, [128, 4, WA], F32); o += 6144
        for nm in ["GL", "LB", "BETA", "T6A", "T6B", "RS6"]:
            e[nm] = cv(nm, o, [128, 4, NH], F32); o += 128
        e["GH"] = cv("GH", o, [128, 16], F32); o += 64
        for nm in ["EG", "BEG", "EGL"]:
            e[nm] = cv(nm, o, [128, 8], F32); o += 32
        e["GLS"] = cv("GLS", o, [128, NSEQ, 4], F32); o += 64
        e["CSN"] = cv("CSN", o, [128, 9, NSEQ, 3], F32); o += 448
        e["CSP"] = cv("CSP", o, [128, 9, NSEQ, 3], F32); o += 448
        e["GI6"] = cv("GI6", o, [128, NH, 128], F32); o += 3072
        e["TTALL"] = cv("TTALL", o, [128, NH, 128], BF16); o += 1536
        e["QKTALL"] = cv("QKTALL", o, [128, NH, 128], BF16); o += 1536
        for nm in ["RKPAD", "KDPAD", "KDM"]:
            e[nm] = cv(nm, o, [128, 3, 2, 128], BF16); o += 1536
        e["WT"] = cv("WT", o, [128, 3, 128], BF16); o += 768
        e["WTM"] = cv("WTM", o, [128, NSEQ, 3, 64], BF16); o += 1536
        e["QGT"] = cv("QGT", o, [128, 3, 128], BF16); o += 768
        e["QGTM"] = cv("QGTM", o, [128, NSEQ, 3, 64], BF16); o += 1536
        e["EGT"] = cv("EGT", o, [128, 3, 128], F32); o += 1536
        for nm in ["QALL", "CB3", "BB3"]:
            e[nm] = cv(nm, o, [128, 3, 128], BF16); o += 768
        assert o <= self.ARENA, o
        f = self.SF = {}
        o = P0
        f["CUV"] = cv("CUV", o, [128, 4, 512], F32); o += 8192
        f["T1"] = cv("T1", o, [128, 4, 512], F32); o += 8192
        f["T2"] = cv("T2", o, [128, 4, 256], F32); o += 4096
        f["VN"] = cv("VN", o, [128, 4, 256], F32); o += 4096
        f["VNB"] = cv("VNB", o, [128, 4, 256], BF16); o += 2048
        f["OC"] = cv("OC", o, [128, 4, 256], F32); o += 4096
        h = self.SH = {}
        o = P0
        h["ACTT"] = cv("ACTT", o, [128, 22, TT], BF16); o += 22528
        h["SGT0"] = cv("SGT0", o, [128, TT], F32); o += 2048
        h["SGT1"] = cv("SGT1", o, [128, TT], F32); o += 2048
        h["YT0"] = cv("YT0", o, [128, TT], F32); o += 2048
        h["YT1"] = cv("YT1", o, [128, TT], F32); o += 2048
        h["YTOK"] = cv("YTOK", o, [128, 4, D], F32); o += 16384
        assert o <= self.ARENA
        self.cur_set = []

    def switch(self, new):
        new = list(new)
        self.em.retire(self.cur_set, new)
        self.cur_set = new

    @property
    def junk(self):
        self.junk_rr += 1
        return self.junks[self.junk_rr % 4]

    def C(self, name, r=128, c=128):
        i = CNAMES.index(name)
        return self.cst[0:r, i * 128:i * 128 + c]

    def ps(self):
        i = self.ps_pool[self.ps_rr % len(self.ps_pool)]
        self.ps_rr += 1
        return self.psum[i]

    def mm(self, ps, out, lhsT, rhs, start, stop, reads, skip=False):
        nc = self.nc
        K = lhsT.shape[0]
        rg = (lhsT.base_partition(), K) if K <= 64 else None
        after = []
        if not hasattr(self, "_last_mm"):
            self._last_mm = {}
        last = self._last_mm.get(id(ps))
        if last is not None and last[0] != rg:
            after.append(("pe", last[1]))
        if skip:
            ins = self.em.op("pe", lambda: nc.tensor.matmul(out, lhsT, rhs, start=start, stop=stop, skip_group_check=True),
                             reads, [ps], after=after)
        else:
            ins = self.em.op("pe", lambda: nc.tensor.matmul(out, lhsT, rhs, start=start, stop=stop), reads, [ps], after=after)
        self._last_mm[id(ps)] = (rg, self.em.cnt["pe"])
        return ins

    def tr(self, ps, out, in_, ident, reads):
        nc = self.nc
        return self.em.op("pe", lambda: nc.tensor.transpose(out, in_, ident), reads, [ps])

    def act(self, out, in_, func, reads, writes, **kw):
        nc = self.nc
        return self.em.op("act", lambda: nc.scalar.activation(out, in_, func, **kw), reads, writes)

    def tt(self, out, a, b, op, reads, writes, eng="dve"):
        e = self.nc.vector if eng == "dve" else self.nc.gpsimd
        return self.em.op(eng, lambda: e.tensor_tensor(out, a, b, op), reads, writes)

    def ts(self, out, a, s1, s2, op0, op1, reads, writes, eng="dve"):
        e = self.nc.vector if eng == "dve" else self.nc.gpsimd
        if s2 is None:
            return self.em.op(eng, lambda: e.tensor_scalar(out, a, s1, None, op0), reads, writes)
        return self.em.op(eng, lambda: e.tensor_scalar(out, a, s1, s2, op0, op1), reads, writes)

    def stt(self, out, a, s, b, op0, op1, reads, writes, eng="dve"):
        e = self.nc.vector if eng == "dve" else self.nc.gpsimd
        return self.em.op(eng, lambda: e.scalar_tensor_tensor(out, a, s, b, op0, op1), reads, writes)

    def cp(self, out, a, reads, writes, eng="dve"):
        if eng == "act":
            return self.em.op("act", lambda: self.nc.scalar.copy(out, a), reads, writes)
        e = self.nc.vector if eng == "dve" else self.nc.gpsimd
        return self.em.op(eng, lambda: e.tensor_copy(out, a), reads, writes)

    def memset(self, buf, ap, val, eng="pool"):
        e = self.nc.vector if eng == "dve" else self.nc.gpsimd
        return self.em.op(eng, lambda: e.memset(ap, val), [], [buf])

    def rstd(self, out, in_, scale, eps_col, reads, writes):
        self.act(out, in_, AF.Ln, list(reads) + [self.eps], writes, bias=self.eps[0:out.shape[0], eps_col:eps_col + 1], scale=scale)
        self.act(out, out, AF.Exp, writes, writes, scale=-0.5)

    def dbg(self, name, buf, ap):
        if name not in self.dbg_names:
            return
        shape = list(ap.shape)
        d = self.em.dram("dbg_" + name, shape, F32 if ap.dtype == F32 else BF16, kind="ExternalOutput", track=False)
        self.dbg_out[name] = d
        self.em.dma("pool", d.v, ap, buf, d)

    PB = dict(g_a_out=(0, 384), g_b_out=(384, 64), g_cv=(448, 256), b_cv=(704, 256), g_c_out=(960, 256),
              b_f=(1216, 6), dt_bias=(1222, 6), a_log=(1228, 6), nega=(1234, 6), cw=(1300, 36))

    def par(self, l, name, r=128):
        o, n = self.PB[name]
        return self.psb[l][0:r, o:o + n]

    def setup(self):
        em, nc, din = self.em, self.nc, self.din
        em.dma("sp", self.cst[:, :], din["cst"][:, :], din["cst"], self.cst)
        names = ["IDENT", "UINCL", "UINCL_S", "ONES"]
        for i, nm in enumerate(names):
            self.cp(self.cstb[:, i * 128:(i + 1) * 128], self.C(nm), [self.cst], [self.cstb])
        self.memset(self.eps, self.eps[:, 0:1], 1e-6)
        self.memset(self.eps, self.eps[:, 1:2], 1e-5)
        self.memset(self.eps, self.eps[:, 2:3], 1.0)
        self.memset(self.eps, self.eps[:, 3:4], 0.0)
        for b in self.ptz:
            self.memset(b, b[:, :], 0.0)
        mtmp = self.carve("m_tmp", 8192, [128, 14, 128], F32)
        self.cur_set.append(mtmp)
        em.dma("sp", mtmp[:, :, :], din["cstm"][:, :].rearrange("p (a b) -> p a b", a=14), din["cstm"], mtmp)
        self.cp(self.maskb[:, :, :], mtmp[:, :, :], [mtmp], [self.maskb])
        tmp = self.carve("ws_tmp", 0, [128, 4, 128], F32)
        tmpb = self.carve("ws_tmpb", 2048, [128, 4, 128], BF16)
        tmp2 = self.carve("ws_tmp2", 4096, [64, 4, 64], F32)
        tmp2b = self.carve("ws_tmp2b", 5120, [64, 4, 64], BF16)
        self.cur_set += [tmp, tmpb, tmp2, tmp2b]
        for l in range(2):
            for nm in ["g_a_out", "g_b_out", "g_cv", "b_cv", "g_c_out", "b_f", "dt_bias", "a_log"]:
                o, n = self.PB[nm]
                em.dma("sp", self.psb[l][:, o:o + n], din[nm][l].partition_broadcast(128), din[nm], self.psb[l])
            o, n = self.PB["cw"]
            cwv = self.psb[l][:, o:o + n].rearrange("p (c i) -> p c i", c=9)
            for i in range(4):
                em.dma("sp", cwv[:, :, i], din["conv_w"][l, i].rearrange("(c p) -> p c", p=128), din["conv_w"], self.psb[l],
                       allow_slow_non_contiguous=True)
            self.act(self.par(l, "nega"), self.par(l, "a_log"), AF.Exp, [self.psb[l]], [self.psb[l]])
            self.ts(self.par(l, "nega"), self.par(l, "nega"), -1.0, None, ALU.mult, None, [self.psb[l]], [self.psb[l]])
            em.dma("sp", tmp[:, :, :], din["w_s"][l].rearrange("g i j -> i g j"), din["w_s"], tmp)
            self.tt(tmpb[:, :, :], tmp[:, :, :], self.C("MASKC").unsqueeze(1).to_broadcast([128, 4, 128]), ALU.mult,
                    [tmp, self.cst], [tmpb])
            ps = self.ps()
            psb = ps.v.bitcast(BF16)
            for g in range(4):
                self.tr(ps, psb[:, g * 128:(g + 1) * 128], tmpb[:, g, :], self.cstb[:, 0:128], [tmpb, self.cstb])
            self.cp(self.wst[l][:, :, :], psb[:, 0:512].rearrange("p (g i) -> p g i", g=4), [ps], [self.wst[l]])
            self.memset(tmp2, tmp2[:, :, :], 0.0)
            for s in range(NSEQ):
                em.dma("sp", tmp2[s * NS:(s + 1) * NS, :, s * NS:(s + 1) * NS],
                       din["w_s"][l, :, 0:NS, 0:NS].rearrange("g t u -> t g u"), din["w_s"], tmp2)
            self.cp(tmp2b[:, :, :], tmp2[:, :, :], [tmp2], [tmp2b])
            ps = self.ps()
            psb = ps.v.bitcast(BF16)
            for g in range(4):
                self.tr(ps, psb[0:64, g * 64:(g + 1) * 64], tmp2b[:, g, :], self.cstb[0:64, 0:64], [tmp2b, self.cstb])
            self.cp(self.wsts[l][0:64, :, :], psb[0:64, 0:256].rearrange("p (g i) -> p g i", g=4), [ps], [self.wsts[l]])
            em.dma("sp", self.bst[l][:, 0:4], din["b_s"][l].rearrange("g i -> i g"), din["b_s"], self.bst[l],
                   allow_slow_non_contiguous=True)
            for s in range(NSEQ):
                em.dma("sp", self.bst[l][s * NS:(s + 1) * NS, 4:8], din["b_s"][l, :, 0:NS].rearrange("g i -> i g"),
                       din["b_s"], self.bst[l], allow_slow_non_contiguous=True)
            self.memset(self.Sst[l], self.Sst[l][:, :, :], 0.0)
            self.memset(self.cstate[l], self.cstate[l][:, :, :], 0.0)
            self.memset(self.carry[l], self.carry[l][:, :], 0.0)
            for s in range(NSEQ):
                src = din["sS"][l, s].rearrange("(pr hh) d e -> hh d pr e", hh=2)
                for hh in range(2):
                    em.dma("sp", self.Sss[l][s][hh * 64:(hh + 1) * 64, :, :], src[hh], din["sS"], self.Sss[l][s])

    def wview(self, ring, slot):
        k = slot["kind"]
        if k == "fm":
            return ring.v.rearrange("p (j ko c) -> p j ko c", j=4, ko=8)
        if k == "tm":
            n = slot["ncols"]
            return ring.v[:, 0:slot["kt"] * n].rearrange("p (ko c) -> p ko c", c=n)
        return ring.v[:, 0:22 * 128].rearrange("p (kt c) -> p kt c", c=128)

    def prepass(self):
        em, din = self.em, self.din
        stg = [self.carve(f"pp_stg{i}", i * 16384, [128, SLOT_E], F32) for i in range(3)]
        img = [self.carve(f"pp_img{i}", 49152 + i * 8192, [128, SLOT_E], BF16) for i in range(3)]
        self.switch(stg + img)
        k = 0
        for l in range(2):
            for si, slot in enumerate(SLOTS):
                st, im = stg[k % 3], img[k % 3]
                for pc in slot["pieces"]:
                    if slot["kind"] == "fm":
                        off, wn, kt, c0, n = pc
                        dst = st.v[:, off:off + 1024].rearrange("p (ko c) -> p ko c", ko=8)
                        src = din[wn][l].rearrange("(ko p) c -> p ko c", p=128)[:, :, c0:c0 + n]
                    else:
                        off, wn, kt, c0, n, ncols = pc
                        dst = st.v[:, 0:kt * ncols].rearrange("p (ko c) -> p ko c", c=ncols)[:, :, off:off + n]
                        src = din[wn][l].rearrange("(ko p) c -> p ko c", p=128)[:, :, c0:c0 + n]
                    em.dma("sp", dst, src, din[wn], st)
                used = slot["used"]
                eng = ["dve", "act"][k % 2]
                self.cp(im[:, 0:used], st[:, 0:used], [st], [im], eng=eng)
                em.dma("pool", self.wscr[l, si, :, 0:used], im[:, 0:used], im, self.wscr)
                k += 1

    def wstream_init(self, ntl):
        self.w_seq = [(l, si) for (_, l) in ntl for si in range(NSLOT_L)]
        self.w_used = 0
        self.w_issued = 0

    def wnext(self, expect=None):
        em = self.em
        idx = self.w_used
        while self.w_issued < min(len(self.w_seq), idx + 3):
            l, si = self.w_seq[self.w_issued]
            used = SLOTS[si]["used"]
            rb = self.ring[self.w_issued % 4]
            em.dma("sp", rb[:, 0:used], self.wscr[l, si, :, 0:used], self.wscr, rb)
            self.w_issued += 1
        l, si = self.w_seq[idx]
        if expect is not None:
            assert si == expect, (si, expect)
        self.w_used += 1
        rb = self.ring[idx % 4]
        return rb, self.wview(rb, SLOTS[si])

    def norm_T(self, g, gain):
        R, NB = g.R, g.NB
        ss = self.small
        for n in range(NB):
            jk = self.junk
            self.act(jk[0:R, :], self.X[0:R, n, :], AF.Square, [self.X], [jk, ss],
                     accum_out=ss[0:R, n:n + 1])
        self.rstd(ss[0:R, 8:8 + NB], ss[0:R, 0:NB], 1.0 / D, 0, [ss], [ss])
        for n in range(NB):
            hb = self.HB[n % 2]
            self.stt(hb[0:R, :], self.X[0:R, n, :], ss[0:R, 8 + n:9 + n], gain[0:R, :], ALU.mult, ALU.mult,
                     [self.X, ss, gain], [hb])
            self.transp_block(g, hb, n, 8)

    def transp_block(self, g, hb, n, nk, dst=None):
        R = g.R
        dst = dst or self.FT
        ps = self.ps()
        psb = ps.v.bitcast(BF16)
        for k in range(nk):
            self.tr(ps, psb[:, k * R:(k + 1) * R], hb[0:R, k * 128:(k + 1) * 128], self.cstb[0:R, 0:R], [hb, self.cstb])
        self.cp(dst[:, 0:nk, n * R:(n + 1) * R], psb[:, 0:nk * R].rearrange("p (k r) -> p k r", k=nk), [ps], [dst],
                eng="act" if n % 2 else "dve")

    def fm_proj(self, g, ring, W, j, evac):
        T = g.T
        ps = self.ps()
        for ko in range(8):
            self.mm(ps, ps[:, 0:T], W[:, j, ko, :], self.FT[:, ko, 0:T], ko == 0, ko == 7, [ring, self.FT])
        evac(ps)

    def tm_proj(self, g, ring, W, ncols, evac, src=None, nk=8):
        R = g.R
        src = src or self.FT
        for n in range(g.NB):
            ps = self.ps()
            for ko in range(nk):
                self.mm(ps, ps[0:R, 0:ncols], src[:, ko, n * R:(n + 1) * R], W[:, ko, 0:ncols], ko == 0, ko == nk - 1,
                        [ring, src])
            evac(n, ps)

    def epilogue(self, g, n, halves, srcbufs, gain):
        R = g.R
        ss = self.small
        msb = self.msb[n % 2]
        for hf in range(2):
            jk = self.junk
            self.act(jk[0:R, 0:512], halves[hf], AF.Square, srcbufs, [jk, ss],
                     accum_out=ss[0:R, 16 + hf:17 + hf])
            self.cp(msb[0:R, hf * 512:(hf + 1) * 512], halves[hf], srcbufs, [msb])
        self.tt(ss[0:R, 18:19], ss[0:R, 16:17], ss[0:R, 17:18], ALU.add, [ss], [ss])
        self.rstd(ss[0:R, 19:20], ss[0:R, 18:19], 1.0 / D, 0, [ss], [ss])
        self.stt(msb[0:R, :], msb[0:R, :], ss[0:R, 19:20], gain[0:R, :], ALU.mult, ALU.mult, [msb, ss, gain], [msb])
        self.tt(self.X[0:R, n, :], self.X[0:R, n, :], msb[0:R, :], ALU.add, [self.X, msb], [self.X])

    def rms_rows(self, g, src, srcbuf, width, gain_ap, gainbuf, dst, dstbuf, scol=24):
        R, NB = g.R, g.NB
        ss = self.small
        for n in range(NB):
            jk = self.junk
            self.act(jk[0:R, 0:width], src[0:R, n, :], AF.Square, [srcbuf], [jk, ss],
                     accum_out=ss[0:R, scol + n:scol + n + 1])
        self.rstd(ss[0:R, scol + 4:scol + 4 + NB], ss[0:R, scol:scol + NB], 1.0 / width, 0, [ss], [ss])
        for n in range(NB):
            self.stt(dst[0:R, n, :], src[0:R, n, :], ss[0:R, scol + 4 + n:scol + 5 + n], gain_ap, ALU.mult, ALU.mult,
                     [srcbuf, ss, gainbuf], [dstbuf])

    def layer(self, g, l, ti):
        em, din, dout = self.em, self.din, self.dout
        R, NB, T = g.R, g.NB, g.T
        t0 = ti * TT
        psb = self.psb[l]
        em.dma("pool", self.gA[:, :], din["g_pre_mix"][l].partition_broadcast(128), din["g_pre_mix"], self.gA)
        em.dma("pool", self.gB[:, :], din["g_post_mix"][l].partition_broadcast(128), din["g_post_mix"], self.gB)
        self.norm_T(g, self.gA)
        _stop("normT")
        d = self.SD
        self.switch(list(d.values()) + (list(self.SDS.values()) if g.sample else []))
        QAT, KAT, VAUG, STG0, STG1 = d["QAT"], d["KAT"], d["VAUG"], d["STG0"], d["STG1"]
        ring, W = self.wnext(0)
        for j in range(3):
            self.fm_proj(g, ring, W, j, lambda ps, j=j: self.cp(QAT[:, j, 0:T], ps[:, 0:T], [ps], [QAT], eng="act"))
        self.fm_proj(g, ring, W, 3, lambda ps: self.cp(KAT[:, 0, 0:T], ps[:, 0:T], [ps], [KAT], eng="dve"))
        ring, W = self.wnext(1)
        for j in range(2):
            self.fm_proj(g, ring, W, j, lambda ps, j=j: self.cp(KAT[:, 1 + j, 0:T], ps[:, 0:T], [ps], [KAT], eng="act"))
        if not g.sample:
            em.dma("pool", self.ktscr[l][:, :, t0:t0 + T], KAT[:, :, 0:T], KAT, self.ktscr[l])
        kout = dout["ks"][l] if g.sample else dout["kp"][l, t0:t0 + T, :]
        vout = dout["vs"][l] if g.sample else dout["vp"][l, t0:t0 + T, :]
        ring, W = self.wnext(2)
        self.tm_proj(g, ring, W, WA, lambda n, ps: self.cp(STG0[0:R, n, :], ps[0:R, 0:WA], [ps], [STG0], eng="act"))
        em.dma("pool", kout.rearrange("(n p) c -> p n c", p=R), STG0[0:R, 0:NB, :], STG0, dout["kp"])
        ring, W = self.wnext(3)
        self.tm_proj(g, ring, W, WA, lambda n, ps: self.cp(STG1[0:R, n, :], ps[0:R, 0:WA], [ps], [STG1], eng="dve"))
        em.dma("pool", vout.rearrange("(n p) c -> p n c", p=R), STG1[0:R, 0:NB, :], STG1, dout["vp"])
        va4 = VAUG.v.rearrange("p n (h e) -> p n h e", h=NH)
        self.memset(VAUG, va4[0:R, 0:NB, :, 64:65], 1.0)
        self.cp(va4[0:R, 0:NB, :, 0:64], STG1[0:R, 0:NB, :].rearrange("p n (h e) -> p n h e", h=NH), [STG1], [VAUG])
        if not g.sample:
            em.dma("pool", self.vscr[l][t0:t0 + T, :].rearrange("(n p) c -> p n c", p=R), VAUG[0:R, 0:NB, :], VAUG,
                   self.vscr[l])
        ring, W = self.wnext(4)
        self.tm_proj(g, ring, W, 18, lambda n, ps: self.cp(self.SM[0:R, n, :], ps[0:R, 0:18], [ps], [self.SM]))
        LOGF = d["LOGF"]
        t6 = self.small[0:R, 32:32 + NB * NH].rearrange("p (n h) -> p n h", n=NB)
        self.tt(t6, self.SM[0:R, 0:NB, 0:6], self.par(l, "b_f", R).unsqueeze(1).to_broadcast([R, NB, NH]), ALU.add,
                [self.SM, psb], [self.small])
        self.act(t6, t6, AF.Exp, [self.small], [self.small], scale=-1.0)
        self.act(t6, t6, AF.Ln, [self.small, self.eps], [self.small], bias=self.eps[0:R, 2:3])
        self.ts(LOGF[0:R, 0:NB, :], t6, -1.0, None, ALU.mult, None, [self.small], [LOGF])
        lfout = dout["lfs"][l] if g.sample else dout["lfp"][l, t0:t0 + T, :]
        em.dma("pool", lfout.rearrange("(n p) h -> p n h", p=R), LOGF[0:R, 0:NB, :], LOGF, dout["lfp"])
        _stop("attnproj")
        if g.sample:
            self.attn_sample(g, l)
        else:
            self.attn_prompt(g, l, ti)
        _stop("attn")
        self.gdn(g, l, ti)
        _stop("gdn")
        self.spatial(g, l)
        _stop("spatial")
        if l == 0 and ti == 0:
            tg = "s" if g.sample else "p"
            self.dbg("mix" + tg, self.MIX, self.MIX[0:R, 0:NB, :])
        for n in range(NB):
            self.transp_mix(g, n)
        r0, W0 = self.wnext(10)
        r1, W1 = self.wnext(11)
        for n in range(NB):
            hal = []
            pss = []
            for hf, (rr, WW) in enumerate(((r0, W0), (r1, W1))):
                ps = self.ps()
                for ko in range(8):
                    self.mm(ps, ps[0:R, 0:512], self.FT[:, ko, n * R:(n + 1) * R], WW[:, ko, 0:512], ko == 0, ko == 7,
                            [rr, self.FT])
                hal.append(ps[0:R, 0:512])
                pss.append(ps)
            self.epilogue(g, n, hal, pss, self.gB)
        if l == 0 and ti == 0:
            self.dbg("x1" + tg, self.X, self.X[0:R, 0:NB, :])
        _stop("wout")
        em.dma("pool", self.gA[:, :], din["g_pre_ffn"][l].partition_broadcast(128), din["g_pre_ffn"], self.gA)
        em.dma("pool", self.gB[:, :], din["g_post_ffn"][l].partition_broadcast(128), din["g_post_ffn"], self.gB)
        self.norm_T(g, self.gA)
        h = self.SH
        self.switch(h.values())
        ACTT = h["ACTT"]
        for s in range(11):
            ring, W = self.wnext(12 + s)
            for jj in range(2):
                psg = self.ps()
                for ko in range(8):
                    self.mm(psg, psg[:, 0:T], W[:, jj, ko, :], self.FT[:, ko, 0:T], ko == 0, ko == 7, [ring, self.FT])
                psu = self.ps()
                for ko in range(8):
                    self.mm(psu, psu[:, 0:T], W[:, 2 + jj, ko, :], self.FT[:, ko, 0:T], ko == 0, ko == 7, [ring, self.FT])
                sg = h[f"SGT{jj}"]
                self.act(sg[:, 0:T], psg[:, 0:T], AF.Silu, [psg], [sg])
                self.tt(ACTT[:, 2 * s + jj, 0:T], sg[:, 0:T], psu[:, 0:T], ALU.mult, [sg, psu], [ACTT])
        YTOK = h["YTOK"]
        for oc in range(8):
            ring, W = self.wnext(23 + oc)
            ps = self.ps()
            for kt in range(22):
                self.mm(ps, ps[:, 0:T], W[:, kt, :], ACTT[:, kt, 0:T], kt == 0, kt == 21, [ring, ACTT])
            yt = h[f"YT{oc % 2}"]
            self.cp(yt[:, 0:T], ps[:, 0:T], [ps], [yt], eng="act")
            ps2 = self.ps()
            for n in range(NB):
                self.tr(ps2, ps2[0:R, n * 128:(n + 1) * 128], yt[:, n * R:(n + 1) * R], self.C("IDENT"), [yt, self.cst])
            self.cp(YTOK[0:R, 0:NB, oc * 128:(oc + 1) * 128], ps2[0:R, 0:NB * 128].rearrange("p (n c) -> p n c", n=NB),
                    [ps2], [YTOK])
        for n in range(NB):
            self.epilogue(g, n, [YTOK[0:R, n, 0:512], YTOK[0:R, n, 512:1024]], [YTOK], self.gB)
        if l == 0 and ti == 0:
            self.dbg("x2" + tg, self.X, self.X[0:R, 0:NB, :])

    def transp_mix(self, g, n):
        R = g.R
        ps = self.ps()
        psb = ps.v.bitcast(BF16)
        for k in range(8):
            self.tr(ps, psb[:, k * R:(k + 1) * R], self.MIX[0:R, n, k * 128:(k + 1) * 128], self.cstb[0:R, 0:R],
                    [self.MIX, self.cstb])
        self.cp(self.FT[:, 0:8, n * R:(n + 1) * R], psb[:, 0:8 * R].rearrange("p (k r) -> p k r", k=8), [ps], [self.FT],
                eng="act" if n % 2 else "dve")

    def cumsum_blocks(self, g, l, LOGF, blk0):
        R = g.R
        call, carry = self.call[l], self.carry[l]
        for n in range(g.NB):
            ps = self.ps()
            self.mm(ps, ps[0:R, 0:NH], self.C("UINCL", R, R), LOGF[0:R, n, :], True, False, [self.cst, LOGF])
            self.mm(ps, ps[0:R, 0:NH], self.C("ONES", 1, R), carry[0:1, :], False, True, [self.cst, carry])
            self.cp(call[0:R, blk0 + n, :], ps[0:R, 0:NH], [ps], [call])
            ps2 = self.ps()
            self.mm(ps2, ps2[0:1, 0:NH], self.C("ONES", R, 1), LOGF[0:R, n, :], True, False, [self.cst, LOGF])
            self.mm(ps2, ps2[0:1, 0:NH], self.C("ONES", 1, 1), carry[0:1, :], False, True, [self.cst, carry])
            self.cp(carry[0:1, :], ps2[0:1, 0:NH], [ps2], [carry])

    def attn_finish(self, g, l, OACC):
        d = self.SD
        R, NB = g.R, g.NB
        RC, OA = d["RC"], d["OA"]
        for qb in range(NB):
            o3 = OACC[qb].v[0:R, 0:NH * 65].rearrange("p (h e) -> p h e", h=NH)
            self.em.op("dve", lambda o3=o3, qb=qb: self.nc.vector.reciprocal(RC[0:R, qb, :].unsqueeze(2), o3[:, :, 64:65]),
                       [OACC[qb]], [RC])
            self.tt(OA[0:R, qb, :].rearrange("p (h e) -> p h e", h=NH), o3[:, :, 0:64],
                    RC[0:R, qb, :].unsqueeze(2).to_broadcast([R, NH, 64]), ALU.mult, [OACC[qb], RC], [OA])
        self.rms_rows(g, OA, OA, WA, self.par(l, "g_a_out", R), self.psb[l], self.MIX.v[:, :, 0:WA], self.MIX)

    def attn_prompt(self, g, l, qi):
        em, d = self.em, self.SD
        R, NB, T = g.R, g.NB, g.T
        QAT, KAT, VAUG, LOGF, BIAS, CREF = d["QAT"], d["KAT"], d["VAUG"], d["LOGF"], d["BIAS"], d["CREF"]
        call = self.call[l]
        blk0 = qi * NB
        self.cumsum_blocks(g, l, LOGF, blk0)
        ps = self.ps()
        self.mm(ps, ps[:, 0:NH], self.C("LAST"), call[:, blk0 + 1, :], True, True, [self.cst, call])
        self.cp(CREF[:, 0:NH], ps[:, 0:NH], [ps], [CREF])
        nkb = blk0 + NB
        self.tt(BIAS[:, 0:nkb, :], CREF[:, 0:NH].unsqueeze(1).to_broadcast([128, nkb, NH]), call[:, 0:nkb, :],
                ALU.subtract, [CREF, call], [BIAS])
        maskT = self.cstb[:, 128:256]
        OTH = self.OTH
        self.em.retire([d["STG0"], d["STG1"]], OTH)
        for h in range(NH):
            self.memset(OTH[h], OTH[h][0:65, :], 0.0, eng="pool")

        def load(kc):
            kt, vr = d[f"KTR{kc % 3}"], d[f"VR{kc % 3}"]
            em.dma("pool", kt[:, :, :], self.ktscr[l][:, :, kc * TT:(kc + 1) * TT], self.ktscr[l], kt)
            em.dma("pool", vr[:, :, :], self.vscr[l][kc * TT:(kc + 1) * TT, :].rearrange("(n p) c -> p n c", p=128),
                   self.vscr[l], vr)

        for kc in range(min(2, qi)):
            load(kc)
        pairs = [(kc, kb, pr) for kc in range(qi + 1) for kb in range(NB) for pr in range(3)]
        LAP = 1
        info = {}

        def emit_qk_pair(p):
            kc, kb, pr = pairs[p]
            diag = kc == qi
            if kb == 0 and pr == 0 and kc + 2 < qi:
                load(kc + 2)
            KT, VV = (KAT, VAUG) if diag else (d[f"KTR{kc % 3}"], d[f"VR{kc % 3}"])
            gk = kc * NB + kb
            q0 = kb * 128 if diag else 0
            N = T - q0
            pss = []
            for hh in range(2):
                rows = slice(hh * 64, hh * 64 + 64)
                ps = self.ps()
                self.mm(ps, ps[:, 0:N], KT[rows, pr, kb * 128:(kb + 1) * 128], QAT[rows, pr, q0:T], True, True, [KT, QAT])
                pss.append(ps)
            pts = []
            for hh in range(2):
                h = 2 * pr + hh
                pt = d[f"PT{(2 * p + hh) % 4}"]
                self.act(pt[:, 0:N], pss[hh][:, 0:N], AF.Exp, [pss[hh], BIAS], [pt], bias=BIAS[:, gk, h:h + 1], scale=SCALE)
                pts.append(pt)
            if diag:
                for hh in range(2):
                    self.tt(pts[hh][:, 0:128], pts[hh][:, 0:128], maskT, ALU.mult, [pts[hh], self.cstb], [pts[hh]])
            info[p] = (pts, VV, q0)

        def emit_pv_pair(p):
            kc, kb, pr = pairs[p]
            pts, VV, q0 = info.pop(p)
            N = T - q0
            pss = []
            for hh in range(2):
                h = 2 * pr + hh
                ps = self.ps()
                self.mm(ps, ps[0:65, 0:N], VV[:, kb, h * 65:(h + 1) * 65], pts[hh][:, 0:N], True, True, [pts[hh], VV])
                pss.append(ps)
            for hh in range(2):
                h = 2 * pr + hh
                self.tt(OTH[h][0:65, q0:T], OTH[h][0:65, q0:T], pss[hh][0:65, 0:N], ALU.add, [OTH[h], pss[hh]], [OTH[h]])

        n = len(pairs)
        for p in range(n + LAP):
            if p < n:
                emit_qk_pair(p)
            if p >= LAP:
                emit_pv_pair(p - LAP)
        OACC = []
        for qb in range(NB):
            ps = self.psum[qb]
            for h in range(NH):
                self.tr(ps, ps[:, h * 65:(h + 1) * 65], OTH[h][0:65, qb * 128:(qb + 1) * 128], self.C("IDENT", 65, 65),
                        [OTH[h], self.cst])
            OACC.append(ps)
        self.attn_finish(g, l, OACC)
        self.em.retire(OTH, [d["STG0"], d["STG1"]])

    def attn_sample(self, g, l):
        em, d, ds, din = self.em, self.SD, self.SDS, self.din
        R, P = g.R, self.P
        NKB = P // 128
        QAT, KAT, VAUG, LOGF, BIAS = d["QAT"], d["KAT"], d["VAUG"], d["LOGF"], d["BIAS"]
        CCW, CCT, CCP, CTOT = ds["CCW"], ds["CCT"], ds["CCP"], ds["CTOT"]
        KCF, VCF, KCB, KCT, VAUGC = ds["KCF"], ds["VCF"], ds["KCB"], ds["KCT"], ds["VAUGC"]
        self.ps_pool = [1, 2, 3, 4, 5, 6, 7]
        self._ptc = 0
        OACC = self.psum[0:1]
        o_started = [False] * NH
        vc4 = VAUGC.v.rearrange("p n (h e) -> p n h e", h=NH)
        self.memset(VAUGC, vc4[:, :, :, 64:65], 1.0)
        ptr = 0
        for s in range(NSEQ):
            em.dma("pool", CCP[:, 0:NKB, :], din["clf"][l, s].rearrange("(b p) h -> p b h", p=128), din["clf"], CCP)
            ps = self.ps()
            self.mm(ps, ps[:, 0:NKB * NH], self.C("UINCL"), CCP[:, 0:NKB, :], True, True, [self.cst, CCP])
            self.cp(CCW[:, 0:NKB, :], ps[:, 0:NKB * NH].rearrange("p (b h) -> p b h", h=NH), [ps], [CCW])
            ps = self.ps()
            self.mm(ps, ps[:, 0:NKB * NH], self.C("ONES"), CCP[:, 0:NKB, :], True, True, [self.cst, CCP])
            self.cp(CCT[:, 0:NKB, :], ps[:, 0:NKB * NH].rearrange("p (b h) -> p b h", h=NH), [ps], [CCT])
            self.memset(CCP, CCP[:, NKB - 1, :], 0.0, eng="dve")
            for b in range(NKB - 2, -1, -1):
                self.tt(CCP[:, b, :], CCP[:, b + 1, :], CCT[:, b + 1, :], ALU.add, [CCP, CCT], [CCP])
            self.tt(BIAS[:, 0:NKB, :], CCP[:, 0:NKB, :], CCT[:, 0:NKB, :], ALU.add, [CCP, CCT], [BIAS])
            self.tt(BIAS[:, 0:NKB, :], BIAS[:, 0:NKB, :], CCW[:, 0:NKB, :], ALU.subtract, [BIAS, CCW], [BIAS])
            for pc in range(NKB // 4):
                em.dma("pool", KCF[:, :, :], din["ck"][l, s, pc * 512:(pc + 1) * 512, :].rearrange("(b p) c -> p b c", p=128),
                       din["ck"], KCF)
                em.dma("pool", VCF[:, :, :], din["cv"][l, s, pc * 512:(pc + 1) * 512, :].rearrange("(b p) c -> p b c", p=128),
                       din["cv"], VCF)
                self.cp(KCB[:, :, :], KCF[:, :, :], [KCF], [KCB])
                self.cp(vc4[:, :, :, 0:64], VCF[:, :, :].rearrange("p n (h e) -> p n h e", h=NH), [VCF], [VAUGC], eng="act")
                for b in range(4):
                    ps = self.ps()
                    psb = ps.v.bitcast(BF16)
                    for pr in range(3):
                        self.tr(ps, psb[:, pr * 128:(pr + 1) * 128], KCB[:, b, pr * 128:(pr + 1) * 128], self.cstb[:, 0:128],
                                [KCB, self.cstb])
                    self.cp(KCT[:, :, b * 128:(b + 1) * 128], psb[:, 0:384].rearrange("p (k r) -> p k r", k=3), [ps], [KCT],
                            eng="act" if b % 2 else "dve")
                prs = [(b, pr) for b in range(4) for pr in range(3)]
                LAP = 2
                inf = {}

                def qk_pair(i, s=s, pc=pc):
                    b, pr = prs[i]
                    gk = pc * 4 + b
                    pss = []
                    for hh in range(2):
                        rows = slice(hh * 64, hh * 64 + 64)
                        ps = self.ps()
                        self.mm(ps, ps[:, 0:NS], KCT[rows, pr, b * 128:(b + 1) * 128], QAT[rows, pr, s * NS:(s + 1) * NS],
                                True, True, [KCT, QAT])
                        pss.append(ps)
                    pts = []
                    for hh in range(2):
                        h = 2 * pr + hh
                        pt = self.ptz[6 * s + (self._ptc % 6)]
                        self._ptc += 1
                        self.act(pt[:, s * NS:(s + 1) * NS], pss[hh][:, 0:NS], AF.Exp, [pss[hh], BIAS], [pt],
                                 bias=BIAS[:, gk, h:h + 1], scale=SCALE)
                        pts.append(pt)
                    inf[i] = pts

                def pv_pair(i):
                    b, pr = prs[i]
                    pts = inf.pop(i)
                    for hh in range(2):
                        h = 2 * pr + hh
                        self.mm(OACC[0], OACC[0][0:R, h * 65:(h + 1) * 65], pts[hh][:, 0:R], VAUGC[:, b, h * 65:(h + 1) * 65],
                                not any(o_started), False, [pts[hh], VAUGC], skip=True)
                        o_started[h] = True

                for i in range(len(prs) + LAP):
                    if i < len(prs):
                        qk_pair(i)
                    if i >= LAP:
                        pv_pair(i - LAP)
        ps = self.ps()
        self.mm(ps, ps[0:R, 0:NH], self.C("UINCL_S", R, R), LOGF[0:R, 0, :], True, True, [self.cst, LOGF])
        self.ts(BIAS[0:R, 0, :], ps[0:R, 0:NH], -1.0, None, ALU.mult, None, [ps], [BIAS])
        pt = d["PT0"]
        for h in range(NH):
            pr, hh = divmod(h, 2)
            rows = slice(hh * 64, hh * 64 + 64)
            ps = self.ps()
            self.mm(ps, ps[0:R, 0:R], KAT[rows, pr, 0:R], QAT[rows, pr, 0:R], True, True, [KAT, QAT])
            self.act(pt[0:R, h * 64:h * 64 + R], ps[0:R, 0:R], AF.Exp, [ps, BIAS], [pt], bias=BIAS[0:R, 0, h:h + 1], scale=SCALE)
            self.tt(pt[0:R, h * 64:h * 64 + R], pt[0:R, h * 64:h * 64 + R], self.cstb[0:R, 256:256 + R], ALU.mult,
                    [pt, self.cstb], [pt])
            self.mm(OACC[0], OACC[0][0:R, h * 65:(h + 1) * 65], pt[0:R, h * 64:h * 64 + R], VAUG[0:R, 0, h * 65:(h + 1) * 65],
                    False, False, [pt, VAUG], skip=True)
        self.attn_finish(g, l, OACC)
        self.ps_pool = list(range(8))

    def gdn(self, g, l, ti):
        em, din, dout, e = self.em, self.din, self.dout, self.SE
        R, NB, T, nseq = g.R, g.NB, g.T, g.nseq
        Ls = T // nseq
        psb = self.psb[l]
        front = [e[k] for k in ["ZC0", "ZC1", "Y0", "Y1", "SQ", "RS"]]
        blockt = [e[k] for k in ["GS3", "LBN3", "LBD3", "E3", "NB0", "NB1", "MB0", "MB1", "RV", "U", "UVSB"]]
        rest = [v for k, v in e.items() if v not in front and v not in blockt and k not in ("BZS", "SQO")]
        self.switch(front + rest)
        QBT, KBT, VBT, KTOK, VTOK, OTOK = e["QBT"], e["KBT"], e["VBT"], e["KTOK"], e["VTOK"], e["OTOK"]
        cw = self.par(l, "cw").rearrange("p (c i) -> p c i", c=9)
        CSN, CSP = e["CSN"], e["CSP"]
        if g.sample:
            for s in range(NSEQ):
                for r_ in range(3):
                    em.dma("pool", CSP[:, :, s, r_], din["sconv"][l, s, r_].rearrange("(c p) -> p c", p=128), din["sconv"], CSP,
                           allow_slow_non_contiguous=True)
        slot_of = [(5, 0), (5, 1), (5, 2), (5, 3), (6, 0), (6, 1), (6, 2), (6, 3), (7, 0)]
        ring = W = None
        for c in range(9):
            si, j = slot_of[c]
            if j == 0:
                ring, W = self.wnext(si)
            zc = e[f"ZC{c % 2}"].v[:, 0:nseq * (3 + Ls)].rearrange("p (s t) -> p s t", s=nseq)
            zcb = e[f"ZC{c % 2}"]
            y = e[f"Y{c % 2}"].v[:, 0:T].rearrange("p (s t) -> p s t", s=nseq)
            yb = e[f"Y{c % 2}"]
            if g.sample:
                self.cp(zc[:, :, 0:3], CSP[:, c, :, :], [CSP], [zcb])
            else:
                self.cp(zc[:, :, 0:3], self.cstate[l][:, c:c + 1, :], [self.cstate[l]], [zcb])
            self.fm_proj(g, ring, W, j, lambda ps: self.cp(zc[:, :, 3:3 + Ls], ps[:, 0:T].rearrange("p (s t) -> p s t", s=nseq),
                                                             [ps], [zcb], eng="act"))
            if g.sample:
                self.cp(CSN[:, c, :, :], zc[:, :, Ls:Ls + 3], [zcb], [CSN])
            else:
                self.cp(self.cstate[l][:, c:c + 1, :], zc[:, :, Ls:Ls + 3], [zcb], [self.cstate[l]])
            self.ts(y, zc[:, :, 0:Ls], cw[:, c, 0:1], None, ALU.mult, None, [zcb, psb], [yb])
            for i in range(1, 4):
                self.stt(y, zc[:, :, i:i + Ls], cw[:, c, i:i + 1], y, ALU.mult, ALU.add, [zcb, psb, yb], [yb])
            yf = yb[:, 0:T]
            self.act(yf, yf, AF.Silu, [yb], [yb])
            if c < 6:
                SQ, RS = e["SQ"], e["RS"]
                self.tt(SQ[:, 0:T], yf, yf, ALU.mult, [yb], [SQ])
                ps = self.ps()
                self.mm(ps, ps[:, 0:T], self.C("BLK64"), SQ[:, 0:T], True, True, [self.cst, SQ])
                self.rstd(RS[:, 0:T], ps[:, 0:T], 1.0, 0, [ps], [RS])
                if c < 3:
                    self.stt(QBT[:, c, 0:T], yf, SCALE, RS[:, 0:T], ALU.mult, ALU.mult, [yb, RS], [QBT])
                else:
                    self.tt(KBT[:, c - 3, 0:T], yf, RS[:, 0:T], ALU.mult, [yb, RS], [KBT])
            else:
                self.cp(VBT[:, c - 6, 0:T], yf, [yb], [VBT])
        if g.sample:
            for s in range(NSEQ):
                for r_ in range(3):
                    em.dma("pool", dout["convs"][l, s, r_].rearrange("(c p) -> p c", p=128), CSN[:, :, s, r_], CSN, dout["convs"],
                           allow_slow_non_contiguous=True)
        _stop("gdn_front")
        for n in range(NB):
            for src, dst in ((KBT, KTOK), (VBT, VTOK)):
                ps = self.ps()
                pb = ps.v.bitcast(BF16)
                for pr in range(3):
                    self.tr(ps, pb[0:R, pr * 128:(pr + 1) * 128], src[:, pr, n * R:(n + 1) * R], self.cstb[:, 0:128],
                            [src, self.cstb])
                self.cp(dst[0:R, n, :], pb[0:R, 0:WA], [ps], [dst], eng="act" if dst is VTOK else "dve")
        _stop("gdn_tr")
        GL, LB, BETA, T6A = e["GL"], e["LB"], e["BETA"], e["T6A"]
        bc = lambda nm: self.par(l, nm, R).unsqueeze(1).to_broadcast([R, NB, NH])
        self.tt(T6A[0:R, 0:NB, :], self.SM[0:R, 0:NB, 6:12], bc("dt_bias"), ALU.add, [self.SM, psb], [T6A])
        self.act(T6A[0:R, 0:NB, :], T6A[0:R, 0:NB, :], AF.Exp, [T6A], [T6A])
        self.act(T6A[0:R, 0:NB, :], T6A[0:R, 0:NB, :], AF.Ln, [T6A, self.eps], [T6A], bias=self.eps[0:R, 2:3])
        self.tt(GL[0:R, 0:NB, :], T6A[0:R, 0:NB, :], bc("nega"), ALU.mult, [T6A, psb], [GL])
        self.act(T6A[0:R, 0:NB, :], self.SM[0:R, 0:NB, 12:18], AF.Exp, [self.SM], [T6A], scale=-1.0)
        self.act(T6A[0:R, 0:NB, :], T6A[0:R, 0:NB, :], AF.Ln, [T6A, self.eps], [T6A], bias=self.eps[0:R, 2:3])
        self.ts(LB[0:R, 0:NB, :], T6A[0:R, 0:NB, :], -1.0, None, ALU.mult, None, [T6A], [LB])
        self.act(BETA[0:R, 0:NB, :], T6A[0:R, 0:NB, :], AF.Exp, [T6A], [BETA], scale=-1.0)
        _stop("gdn_scal")
        self.em.retire(front, blockt)
        self.cur_set = [b for b in self.cur_set if b not in front] + blockt
        self.memset(e["RKPAD"], e["RKPAD"][:, :, :, :], 0.0)
        self.memset(e["KDPAD"], e["KDPAD"][:, :, :, :], 0.0)
        for n in range(NB):
            self.gdn_block(g, l, n)
        if g.sample:
            for s in range(NSEQ):
                dst = dout["Ss"][l, s].rearrange("(pr hh d) e -> hh d pr e", hh=2, d=HD)
                for hh in range(2):
                    em.dma("pool", dst[hh], self.Sss[l][s][hh * 64:(hh + 1) * 64, :, :], self.Sss[l][s], dout["Ss"])
        BZS = e["BZS"]
        self.em.retire(blockt, [BZS, e["SQO"]])
        self.cur_set = [b for b in self.cur_set if b not in blockt] + [BZS, e["SQO"]]
        ring, W = self.wnext(8)
        self.tm_proj(g, ring, W, WA, lambda n, ps: self.act(BZS[0:R, n, :], ps[0:R, 0:WA], AF.Silu, [ps], [BZS]))
        RS6 = e["RS6"]
        o4 = OTOK.v[0:R, 0:NB, :].rearrange("p n (h e) -> p (n h) e", h=NH)
        SQO = e["SQO"]
        sqv = SQO.v[0:R, 0:NB, :].rearrange("p n (h e) -> p (n h) e", h=NH)
        self.tt(sqv, o4, o4, ALU.mult, [OTOK], [SQO])
        t6f = e["T6A"][0:R, 0:NB, :].rearrange("p n h -> p (n h)")
        self.em.op("dve", lambda: self.nc.vector.tensor_reduce(t6f, sqv, AX.X, ALU.add), [SQO], [e["T6A"]])
        self.rstd(RS6[0:R, 0:NB, :], e["T6A"][0:R, 0:NB, :], 1.0 / HD, 0, [e["T6A"]], [RS6])
        self.tt(o4, o4, RS6[0:R, 0:NB, :].rearrange("p n h -> p (n h)").unsqueeze(2).to_broadcast([R, NB * NH, HD]), ALU.mult,
                [OTOK, RS6], [OTOK])
        self.tt(o4, o4, self.par(l, "g_b_out", R).unsqueeze(1).to_broadcast([R, NB * NH, HD]), ALU.mult, [OTOK, psb], [OTOK])
        self.tt(self.MIX.v[0:R, 0:NB, WA:2 * WA], OTOK[0:R, 0:NB, :], BZS[0:R, 0:NB, :], ALU.mult, [OTOK, BZS], [self.MIX])

    def gdn_block(self, g, l, n):
        e = self.SE
        R, nseq, sample = g.R, g.nseq, g.sample
        sfx = "_S" if sample else ""
        UIN, NEGN, NEGM, NEGQ = (self.C(k + sfx, R, R) for k in ("UINCL", "NEGN", "NEGM", "NEGQ"))
        SEQ = self.C("SEQ_S", R, R) if sample else self.C("ONES", R, R)
        SG, IDENT, ONES = self.C("SG", R, R), self.C("IDENT", R, R), self.C("ONES", R, R)
        cst = self.cst
        GL, LB, BETA = e["GL"], e["LB"], e["BETA"]
        gl, lb, beta = GL[0:R, n, :], LB[0:R, n, :], BETA[0:R, n, :]
        GH, EG, BEG, EGL, GLS = e["GH"], e["EG"], e["BEG"], e["EGL"], e["GLS"]
        QBT, KBT, KTOK, VTOK, OTOK = e["QBT"], e["KBT"], e["KTOK"], e["VTOK"], e["OTOK"]
        cols = slice(n * R, (n + 1) * R)
        ps = self.ps()
        self.mm(ps, ps[0:R, 0:NH], UIN, gl, True, True, [cst, GL])
        self.mm(ps, ps[0:R, 8:8 + NH], SEQ, gl, True, True, [cst, GL])
        self.cp(GH[0:R, 0:16], ps[0:R, 0:16], [ps], [GH])
        self.act(EG[0:R, 0:NH], GH[0:R, 0:NH], AF.Exp, [GH], [EG])
        self.tt(BEG[0:R, 0:NH], EG[0:R, 0:NH], beta, ALU.mult, [EG, BETA], [BEG])
        self.tt(EGL[0:R, 0:NH], GH[0:R, 8:8 + NH], GH[0:R, 0:NH], ALU.subtract, [GH], [EGL])
        self.act(EGL[0:R, 0:NH], EGL[0:R, 0:NH], AF.Exp, [EGL], [EGL])
        gl2 = gl.rearrange("p (pr hh) -> p pr hh", hh=2)
        for s in range(nseq):
            ps = self.ps()
            lo = self.C(f"SELLO_S{s}", R, 128) if sample else self.C("HALF_LO", R, 128)
            hi = self.C(f"SELHI_S{s}", R, 128) if sample else self.C("HALF_HI", R, 128)
            self.mm(ps, ps[:, 0:3], lo, gl2[:, :, 0], True, False, [cst, GL])
            self.mm(ps, ps[:, 0:3], hi, gl2[:, :, 1], False, True, [cst, GL])
            self.act(GLS[:, s, 0:3], ps[:, 0:3], AF.Exp, [ps], [GLS])
        _stop("gb_gh")
        GI6 = e["GI6"]
        self.tt(GI6[0:R, :, 0:R], gl.unsqueeze(2).to_broadcast([R, NH, R]), UIN.unsqueeze(1).to_broadcast([R, NH, R]), ALU.mult,
                [GL, cst], [GI6])
        TTALL, QKTALL = e["TTALL"], e["QKTALL"]
        GS3, LBN3, LBD3, E3 = e["GS3"], e["LBN3"], e["LBD3"], e["E3"]
        NBs, MBs = [e["NB0"], e["NB1"]], [e["MB0"], e["MB1"]]
        identb = self.cstb[0:R, 0:R]
        p3 = lambda ps: ps[0:R, 0:3 * R].rearrange("p (h r) -> p h r", h=3)
        chains = [dict(NB0=e["NB0"], MB0=e["MB0"], NB1=e["NB1"], MB1=e["MB1"], CB3=e["CB3"], BB3=e["BB3"], QALL=e["QALL"]),
                  self.ch1]
        for hg in range(2):
            ch = chains[hg]
            hs = slice(3 * hg, 3 * hg + 3)
            b3 = lambda ap: ap.unsqueeze(2).to_broadcast([R, 3, R])
            m3 = lambda ap: ap.unsqueeze(1).to_broadcast([R, 3, R])
            self.tt(GS3[0:R, :, 0:R], b3(gl[:, hs]), m3(SG), ALU.mult, [GL, cst], [GS3])
            self.tt(LBN3[0:R, :, 0:R], b3(lb[:, hs]), m3(NEGN), ALU.add, [LB, cst], [LBN3])
            psK, psQ2 = self.ps(), self.ps()
            for hi_ in range(3):
                h = 3 * hg + hi_
                pr, hh = divmod(h, 2)
                rows = slice(hh * 64, hh * 64 + 64)
                self.mm(psK, psK[0:R, hi_ * R:(hi_ + 1) * R], KBT[rows, pr, cols], KBT[rows, pr, cols], True, True, [KBT])
                self.mm(psQ2, psQ2[0:R, hi_ * R:(hi_ + 1) * R], KBT[rows, pr, cols], QBT[rows, pr, cols], True, True, [KBT, QBT])
            psN = self.ps()
            self.mm(psN, p3(psN), UIN, GS3[0:R, :, 0:R], True, False, [cst, GS3])
            self.mm(psN, p3(psN), IDENT, LBN3[0:R, :, 0:R], False, True, [cst, LBN3])
            self.act(E3[0:R, :, 0:R], p3(psN), AF.Exp, [psN], [E3])
            self.stt(ch["NB0"][0:R, :, 0:R], E3[0:R, :, 0:R], -1.0, p3(psK), ALU.mult, ALU.mult, [E3, psK], [ch["NB0"]])
            psD = self.ps()
            self.mm(psD, p3(psD), SG, GI6[0:R, hs, 0:R], True, False, [cst, GI6])
            for hi_ in range(3):
                self.mm(psD, psD[0:R, hi_ * R:(hi_ + 1) * R], IDENT, NEGQ, False, hi_ == 2, [cst])
            self.act(E3[0:R, :, 0:R], p3(psD), AF.Exp, [psD], [E3])
            self.tt(QKTALL[0:R, hs, 0:R], E3[0:R, :, 0:R], p3(psQ2), ALU.mult, [E3, psQ2], [QKTALL])
        _stop("gb_nmq")
        m3b = lambda k: self.maskb[0:R, k, 0:R].unsqueeze(1).to_broadcast([R, 3, R])
        id3 = identb.unsqueeze(1).to_broadcast([R, 3, R])

        def masks(hg, lv):
            ch = chains[hg]
            self.tt(ch["NB1"][0:R, :, 0:R], ch["NB0"][0:R, :, 0:R], m3b(lv - 1), ALU.mult, [ch["NB0"], self.maskb], [ch["NB1"]],
                    eng="pool")

        def transp3(src_of, dst_ap, dstbuf, srcbufs, eng):
            ps = self.ps()
            pb = ps.v.bitcast(BF16)
            for hi_ in range(3):
                self.tr(ps, pb[0:R, hi_ * R:(hi_ + 1) * R], src_of(hi_), identb, srcbufs + [self.cstb])
            self.cp(dst_ap, pb[0:R, 0:3 * R].rearrange("p (h r) -> p h r", h=3), [ps], [dstbuf], eng=eng)

        for hg in range(2):
            masks(hg, 1)
        for hg in range(2):
            ch = chains[hg]
            self.tt(ch["QALL"][0:R, :, 0:R], ch["NB1"][0:R, :, 0:R], id3, ALU.add, [ch["NB1"], self.cstb], [ch["QALL"]])
        for hg in range(2):
            if g.lev >= 2:
                masks(hg, 2)
        for hg in range(2):
            ch = chains[hg]
            hs = slice(3 * hg, 3 * hg + 3)
            transp3(lambda hi_, ch=ch: ch["QALL"][0:R, hi_, 0:R], TTALL[0:R, hs, 0:R], TTALL, [ch["QALL"]], "act")
        for lv in range(2, g.lev + 1):
            pss = {}
            for hg in range(2):
                ch = chains[hg]
                psb_ = self.ps()
                for hi_ in range(3):
                    self.mm(psb_, psb_[0:R, hi_ * R:(hi_ + 1) * R], ch["NB1"][0:R, hi_, 0:R], TTALL[0:R, 3 * hg + hi_, 0:R], True, True,
                            [ch["NB1"], TTALL])
                pss[hg] = psb_
            for hg in range(2):
                ch = chains[hg]
                transp3(lambda hi_, hg=hg: TTALL[0:R, 3 * hg + hi_, 0:R], ch["QALL"][0:R, :, 0:R], ch["QALL"], [TTALL], "dve")
            for hg in range(2):
                if lv < g.lev:
                    masks(hg, lv + 1)
            for hg in range(2):
                ch = chains[hg]
                self.cp(ch["BB3"][0:R, :, 0:R], p3(pss[hg]), [pss[hg]], [ch["BB3"]], eng="act")
            for hg in range(2):
                ch = chains[hg]
                psp = self.ps()
                for hi_ in range(3):
                    self.mm(psp, psp[0:R, hi_ * R:(hi_ + 1) * R], ch["QALL"][0:R, hi_, 0:R], ch["BB3"][0:R, hi_, 0:R], True, True,
                            [ch["QALL"], ch["BB3"]])
                pss[hg] = psp
            for hg in range(2):
                hs = slice(3 * hg, 3 * hg + 3)
                Pv = TTALL[0:R, hs, 0:R]
                self.tt(Pv, Pv, p3(pss[hg]), ALU.add, [TTALL, pss[hg]], [TTALL])
        _stop("gb_inv")
        RV, U, UVSB, RKPAD, KDPAD, KDM = e["RV"], e["U"], e["UVSB"], e["RKPAD"], e["KDPAD"], e["KDM"]
        WT, WTM, QGT, QGTM, EGT = e["WT"], e["WTM"], e["QGT"], e["QGTM"], e["EGT"]
        self.tt(RV[0:R, :, :], VTOK[0:R, n, :].rearrange("p (h e) -> p h e", h=NH), beta.unsqueeze(2).to_broadcast([R, NH, HD]),
                ALU.mult, [VTOK, BETA], [RV])
        k4 = KTOK[0:R, n, :].rearrange("p (pr hh e) -> p pr hh e", pr=3, hh=2)
        for hh in range(2):
            sc = lambda t: t[0:R, 0:NH].rearrange("p (pr hh) -> p pr hh", hh=2)[:, :, hh].unsqueeze(2).to_broadcast([R, 3, HD])
            self.tt(RKPAD[0:R, :, hh, hh * 64:(hh + 1) * 64], k4[:, :, hh, :], sc(BEG), ALU.mult, [KTOK, BEG], [RKPAD])
            self.tt(KDPAD[0:R, :, hh, hh * 64:(hh + 1) * 64], k4[:, :, hh, :], sc(EGL), ALU.mult, [KTOK, EGL], [KDPAD])
        psU = self.ps()
        for h in range(NH):
            self.mm(psU, psU[0:R, h * HD:(h + 1) * HD], TTALL[0:R, h, 0:R], RV[0:R, h, :], True, True, [TTALL, RV])
        self.cp(UVSB[0:R, :], psU[0:R, 0:NH * HD], [psU], [UVSB], eng="act")
        psW = self.ps()
        for pr in range(3):
            for hh in range(2):
                self.mm(psW, psW[:, pr * R:(pr + 1) * R], RKPAD[0:R, pr, hh, :], TTALL[0:R, 2 * pr + hh, 0:R], hh == 0, hh == 1,
                        [RKPAD, TTALL])
        self.cp(WT[:, :, 0:R], psW[:, 0:3 * R].rearrange("p (k r) -> p k r", k=3), [psW], [WT])
        psE = self.ps()
        for pr in range(3):
            self.mm(psE, psE[:, pr * R:(pr + 1) * R], self.C("HALF_LO", R, 128), GI6[0:R, 2 * pr, 0:R], True, False, [cst, GI6])
            self.mm(psE, psE[:, pr * R:(pr + 1) * R], self.C("HALF_HI", R, 128), GI6[0:R, 2 * pr + 1, 0:R], False, True, [cst, GI6])
        self.act(EGT[:, :, 0:R], psE[:, 0:3 * R].rearrange("p (k r) -> p k r", k=3), AF.Exp, [psE], [EGT])
        self.tt(QGT[:, :, 0:R], QBT[:, :, cols], EGT[:, :, 0:R], ALU.mult, [QBT, EGT], [QGT])
        if sample:
            for s in range(NSEQ):
                cm = self.C("SEQCOL_S01" if s < 2 else "SEQCOL_S23")[:, (s % 2) * 64:(s % 2) * 64 + 64]
                cmb = cm.unsqueeze(1).to_broadcast([128, 3, 64])
                self.tt(WTM[:, s, :, :], WT[:, :, 0:R], cmb, ALU.mult, [WT, cst], [WTM])
                self.tt(QGTM[:, s, :, :], QGT[:, :, 0:R], cmb, ALU.mult, [QGT, cst], [QGTM])
        _stop("gb_rhs")
        Sf = self.Sss[l] if sample else [self.Sst[l]]
        Sb = self.Sssb[l] if sample else [self.Sstb[l]]
        if sample or n == 0:
            for s in range(nseq):
                self.cp(Sb[s][:, :, :], Sf[s][:, :, :], [Sf[s]], [Sb[s]])
        wt_of = (lambda s: WTM[:, s, :, :]) if sample else (lambda s: WT[:, :, 0:R])
        qg_of = (lambda s: QGTM[:, s, :, :]) if sample else (lambda s: QGT[:, :, 0:R])
        wtb, qgb = (WTM, QGTM) if sample else (WT, QGT)
        psWS = self.ps()
        for h in range(NH):
            pr, hh = divmod(h, 2)
            rows = slice(hh * 64, hh * 64 + 64)
            for s in range(nseq):
                self.mm(psWS, psWS[0:R, h * HD:(h + 1) * HD], wt_of(s)[rows, pr, :], Sb[s][rows, pr, :], s == 0, s == nseq - 1,
                        [wtb, Sb[s]])
        self.tt(U[0:R, :, :].rearrange("p h e -> p (h e)"), UVSB[0:R, :], psWS[0:R, 0:NH * HD], ALU.subtract, [UVSB, psWS], [U])
        psO = self.ps()
        for h in range(NH):
            pr, hh = divmod(h, 2)
            rows = slice(hh * 64, hh * 64 + 64)
            for s in range(nseq):
                self.mm(psO, psO[0:R, h * HD:(h + 1) * HD], qg_of(s)[rows, pr, :], Sb[s][rows, pr, :], s == 0, False, [qgb, Sb[s]])
            self.mm(psO, psO[0:R, h * HD:(h + 1) * HD], QKTALL[0:R, h, 0:R], U[0:R, h, :], False, True, [QKTALL, U])
        self.cp(OTOK[0:R, n, :], psO[0:R, 0:NH * HD], [psO], [OTOK], eng="act")
        for s in range(nseq):
            kd = KDPAD
            if sample:
                self.ts(KDM[0:R, :, :, :], KDPAD[0:R, :, :, :], self.C("ROWM_S", R, 128)[:, s:s + 1], None, ALU.mult, None,
                        [KDPAD, cst], [KDM])
                kd = KDM
            psS = self.ps()
            for pr in range(3):
                for hh in range(2):
                    self.mm(psS, psS[:, pr * HD:(pr + 1) * HD], kd[0:R, pr, hh, :], U[0:R, 2 * pr + hh, :], hh == 0, hh == 1, [kd, U])
            for pr in range(3):
                self.stt(Sf[s][:, pr, :], Sf[s][:, pr, :], GLS[:, s, pr:pr + 1], psS[:, pr * HD:(pr + 1) * HD], ALU.mult, ALU.add,
                         [Sf[s], GLS, psS], [Sf[s]])
            self.cp(Sb[s][:, :, :], Sf[s][:, :, :], [Sf[s]], [Sb[s]], eng="act")

    def spatial(self, g, l):
        em, f, dout = self.em, self.SF, self.dout
        R, NB = g.R, g.NB
        psb = self.psb[l]
        self.switch(f.values())
        CUV, T1, T2, VN, VNB, OC = f["CUV"], f["T1"], f["T2"], f["VN"], f["VNB"], f["OC"]
        ring, W = self.wnext(9)
        self.tm_proj(g, ring, W, 512, lambda n, ps: self.cp(CUV[0:R, n, :], ps[0:R, 0:512], [ps], [CUV],
                                                            eng="act" if n % 2 else "dve"))
        x, t = CUV[0:R, 0:NB, :], T1[0:R, 0:NB, :]
        self.tt(t, x, x, ALU.mult, [CUV], [T1])
        self.ts(t, t, 0.044715, 1.0, ALU.mult, ALU.add, [T1], [T1])
        self.tt(t, t, x, ALU.mult, [T1, CUV], [T1])
        self.act(t, t, AF.Sigmoid, [T1], [T1], scale=1.5957691216057308)
        self.tt(x, x, t, ALU.mult, [CUV, T1], [CUV])
        u, v = CUV[0:R, 0:NB, 0:256], CUV[0:R, 0:NB, 256:512]
        ss = self.small
        self.em.op("dve", lambda: self.nc.vector.tensor_reduce(ss[0:R, 40:40 + NB], v, AX.X, ALU.add), [CUV], [ss])
        self.ts(ss[0:R, 40:40 + NB], ss[0:R, 40:40 + NB], -1.0 / 256, None, ALU.mult, None, [ss], [ss])
        cen = T2[0:R, 0:NB, :]
        self.tt(cen, v, ss[0:R, 40:40 + NB].unsqueeze(2).to_broadcast([R, NB, 256]), ALU.add, [CUV, ss], [T2])
        sq = T1[0:R, 0:NB, 0:256]
        self.tt(sq, cen, cen, ALU.mult, [T2], [T1])
        self.em.op("dve", lambda: self.nc.vector.tensor_reduce(ss[0:R, 44:44 + NB], sq, AX.X, ALU.add), [T1], [ss])
        self.rstd(ss[0:R, 48:48 + NB], ss[0:R, 44:44 + NB], 1.0 / 256, 1, [ss], [ss])
        vn = VN[0:R, 0:NB, :]
        self.tt(vn, cen, ss[0:R, 48:48 + NB].unsqueeze(2).to_broadcast([R, NB, 256]), ALU.mult, [T2, ss], [VN])
        self.tt(vn, vn, self.par(l, "g_cv", R).unsqueeze(1).to_broadcast([R, NB, 256]), ALU.mult, [VN, psb], [VN])
        self.tt(vn, vn, self.par(l, "b_cv", R).unsqueeze(1).to_broadcast([R, NB, 256]), ALU.add, [VN, psb], [VN])
        if g.sample:
            em.dma("pool", dout["cvs"][l], VN[0:R, 0, :], VN, dout["cvs"])
        self.cp(VNB[0:R, 0:NB, :], vn, [VN], [VNB])
        wst = self.wsts[l] if g.sample else self.wst[l]
        bs = self.bst[l][0:R, 4:8] if g.sample else self.bst[l][0:R, 0:4]
        for n in range(NB):
            ps = self.ps()
            for gq in range(4):
                self.mm(ps, ps[0:R, gq * 64:(gq + 1) * 64], wst[0:R, gq, 0:R], VNB[0:R, n, gq * 64:(gq + 1) * 64], True, True,
                        [wst, VNB])
            o3 = OC[0:R, n, :].rearrange("p (g c) -> p g c", g=4)
            self.tt(o3, ps[0:R, 0:256].rearrange("p (g c) -> p g c", g=4), bs.unsqueeze(2).to_broadcast([R, 4, 64]), ALU.add,
                    [ps, self.bst[l]], [OC])
            self.tt(OC[0:R, n, :], OC[0:R, n, :], CUV[0:R, n, 0:256], ALU.mult, [OC, CUV], [OC])
        self.rms_rows(g, OC, OC, 256, self.par(l, "g_c_out", R), psb, self.MIX.v[:, :, 2 * WA:D], self.MIX, scol=52)

    def _ops(self):
        em, din, dout = self.em, self.din, self.dout
        S = self.S
        try:
            self._ops2()
        except StopBuild as ex:
            print("STOPPED AT", ex)
        em.finish()

    def _ops2(self):
        em, din, dout = self.em, self.din, self.dout
        S = self.S
        self.setup()
        _stop("setup")
        self.prepass()
        _stop("prepass")
        ntiles = S // TT
        tiles = [(False, ti) for ti in range(ntiles)] + [(True, 0)]
        ntl = [(t, l) for t in tiles for l in range(2)]
        self.wstream_init(ntl)
        self.em.retire(self.cur_set, self.FX)
        gp, gs = Seg(False), Seg(True)
        for (sample, ti) in tiles:
            g = gs if sample else gp
            R, NB = g.R, g.NB
            if sample:
                em.dma("pool", self.X[0:R, 0, :], din["xs"][:, :], din["xs"], self.X)
            else:
                em.dma("pool", self.X[:, :, :], din["xp"][ti * TT:(ti + 1) * TT, :].rearrange("(n p) d -> p n d", p=128),
                       din["xp"], self.X)
            for l in range(2):
                self.layer(g, l, ti)
            if sample:
                em.dma("pool", dout["ys"][:, :], self.X[0:R, 0, :], self.X, dout["ys"])
            else:
                em.dma("pool", dout["yp"][ti * TT:(ti + 1) * TT, :].rearrange("(n p) d -> p n d", p=128), self.X[:, :, :],
                       self.X, dout["yp"])
            if (not sample) and ti == ntiles - 1:
                for l in range(2):
                    for r_ in range(3):
                        em.dma("pool", dout["convp"][l, r_].rearrange("(c p) -> p c", p=128), self.cstate[l][:, :, r_],
                               self.cstate[l], dout["convp"], allow_slow_non_contiguous=True)
                    dst = dout["Sp"][l].rearrange("(pr hh d) e -> hh d pr e", hh=2, d=HD)
                    for hh in range(2):
                        em.dma("pool", dst[hh], self.Sst[l][hh * 64:(hh + 1) * 64, :, :], self.Sst[l], dout["Sp"])


_CACHE = {}


def kernel(x_prompt, x_sample, cache_a_k, cache_a_v, cache_a_logf, state_b_conv, state_b_S,
           g_pre_mix, w_in, b_f, conv_w, a_log, dt_bias, g_b_out, g_a_out, g_cv, b_cv,
           w_s, b_s, g_c_out, w_out, g_post_mix, g_pre_ffn, w_ffn_in, w_ffn_out, g_post_ffn, _dbg=()):
    f = lambda a: np.ascontiguousarray(np.asarray(a, dtype=np.float32))
    x_prompt, x_sample = f(x_prompt), f(x_sample)
    B, S, _ = x_prompt.shape
    DB, ns, _ = x_sample.shape
    P = cache_a_k.shape[2]
    assert ns == NS and DB == NSEQ * B and S % TT == 0 and P % 512 == 0
    key = (S, P, tuple(_dbg))
    if key not in _CACHE:
        _CACHE[key] = Builder(S, P, dbg=_dbg)
    bld = _CACHE[key]
    cst = make_consts()
    shared = dict(g_pre_mix=f(g_pre_mix), w_in=f(w_in), b_f=f(b_f), conv_w=f(conv_w), a_log=f(a_log), dt_bias=f(dt_bias),
                  g_b_out=f(g_b_out), g_a_out=f(g_a_out), g_cv=f(g_cv), b_cv=f(b_cv), w_s=f(w_s), b_s=f(b_s),
                  g_c_out=f(g_c_out), w_out=f(w_out), g_post_mix=f(g_post_mix), g_pre_ffn=f(g_pre_ffn),
                  w_ffn_in=f(w_ffn_in), w_ffn_out=f(w_ffn_out), g_post_ffn=f(g_post_ffn), cst=cst, cstm=make_masks())
    ck, cvv, clf = f(cache_a_k), f(cache_a_v), f(cache_a_logf)
    sc, sS = f(state_b_conv), f(state_b_S)
    in_maps = []
    for c in range(B):
        sl = slice(NSEQ * c, NSEQ * (c + 1))
        m = dict(shared)
        m["xp"] = x_prompt[c]
        m["xs"] = x_sample[sl].reshape(NSEQ * NS, D)
        m["ck"] = np.ascontiguousarray(ck[:, sl].reshape(2, NSEQ, P, WA))
        m["cv"] = np.ascontiguousarray(cvv[:, sl].reshape(2, NSEQ, P, WA))
        m["clf"] = np.ascontiguousarray(clf[:, sl])
        m["sconv"] = np.ascontiguousarray(sc[:, sl])
        m["sS"] = np.ascontiguousarray(sS[:, sl])
        in_maps.append(m)
    res = run_bass_kernel_spmd(bld.nc, in_maps, core_ids=list(range(B)))
    r = res.results
    cat = lambda k, ax: np.stack([r[c][k] for c in range(B)], axis=ax)
    yp = cat("yp", 0)
    ys = cat("ys", 0).reshape(DB, NS, D)
    kp = cat("kp", 1).reshape(2, B, S, NH, HD)
    vp = cat("vp", 1).reshape(2, B, S, NH, HD)
    lfp = cat("lfp", 1)
    convp = cat("convp", 1)
    Sp = cat("Sp", 1).reshape(2, B, NH, HD, HD)
    ks = cat("ks", 1).reshape(2, DB, NS, NH, HD)
    vs = cat("vs", 1).reshape(2, DB, NS, NH, HD)
    lfs = cat("lfs", 1).reshape(2, DB, NS, NH)
    convs = cat("convs", 1).reshape(2, DB, 3, 1152)
    Ss = cat("Ss", 1).reshape(2, DB, NH, HD, HD)
    cvs = cat("cvs", 1).reshape(2, DB, NS, 256)
    outs = (yp, ys, kp, vp, lfp, convp, Sp, ks, vs, lfs, convs, Ss, cvs)
    if _dbg:
        return outs, [{k: r[c]["dbg_" + k] for k in bld.dbg_out} for c in range(B)]
    return outs
```
